# Optimizing a Trainium2 kernel written in Bass

```python
import math
import jax
import jax.numpy as jnp
from jax import lax
import numpy as np

D_MODEL = 1024
BATCH = 4
SEQ = 8192
DEPTH = 2

CTX_LEN = 256
GRID_W = 64
EPS = 1e-6

D_RNN = 384
LRU_HEADS = 6
LRU_HEAD_DIM = D_RNN // LRU_HEADS
CONV_WIDTH = 4
LRU_C = 8.0

D_SSM = 384
SSM_GROUP = 16
SSM_GROUPS = D_SSM // SSM_GROUP
SSM_STATE = 64
DT_MIN = 1e-3
DT_MAX = 1e-1

D_POOL = 256
POOL_WINDOWS = (2, 4, 8, 16)
POOL_GROUP = D_POOL // len(POOL_WINDOWS)

COL_AX = 0
COL_BU = D_RNN
COL_AG = D_RNN + D_SSM
COL_CV = 2 * D_RNN + D_SSM
SCAN_COLS = D_RNN + D_SSM
D_IN = 2 * D_RNN + D_SSM + D_POOL
D_MIX = D_RNN + D_SSM + D_POOL
D_FF = -(-8 * D_MODEL // (3 * 256)) * 256

kernel_name = 'hymba_style_rglru_s5_pool_diffusion_block'


def _rmsnorm(x, g):
    x32 = x.astype(jnp.float32)
    y = x32 * lax.rsqrt(jnp.mean(x32 * x32, axis=-1, keepdims=True) + EPS)
    return (y * g.astype(jnp.float32)).astype(x.dtype)


def _modulate(h, shift, scale):
    return h * (1.0 + scale) + shift


def _swiglu(h, w_gate, w_up, w_down):
    return (jax.nn.silu(h @ w_gate) * (h @ w_up)) @ w_down


def _dwconv(v, w, b):
    t = v.shape[1]
    left = CONV_WIDTH // 2
    vp = jnp.pad(v, ((0, 0), (left, CONV_WIDTH - 1 - left), (0, 0)))
    out = b + vp[:, 0:t] * w[0]
    for k in range(1, CONV_WIDTH):
        out = out + vp[:, k:k + t] * w[k]
    return out


def _combine_real(e1, e2):
    a1, b1 = e1
    a2, b2 = e2
    return a1 * a2, a2 * b1 + b2


def _linear_scan(a, b, h0, reverse):
    if reverse:
        a, b = jnp.flip(a, 1), jnp.flip(b, 1)
    a_cum, h = lax.associative_scan(_combine_real, (a, b), axis=1)
    if h0 is not None:
        h = h + a_cum * h0[:, None]
    if reverse:
        h = jnp.flip(h, 1)
    return h


def _combine_complex(e1, e2):
    ar1, ai1, br1, bi1 = e1
    ar2, ai2, br2, bi2 = e2
    return (ar1 * ar2 - ai1 * ai2, ar1 * ai2 + ai1 * ar2,
            ar2 * br1 - ai2 * bi1 + br2, ar2 * bi1 + ai2 * br1 + bi2)


def _complex_scan(ar, ai, br, bi, h0, reverse):
    if reverse:
        ar, ai, br, bi = [jnp.flip(e, 1) for e in (ar, ai, br, bi)]
    acr, aci, hr, hi = lax.associative_scan(_combine_complex, (ar, ai, br, bi), axis=1)
    if h0 is not None:
        h0r, h0i = h0[0][:, None], h0[1][:, None]
        hr, hi = hr + acr * h0r - aci * h0i, hi + acr * h0i + aci * h0r
    if reverse:
        hr, hi = jnp.flip(hr, 1), jnp.flip(hi, 1)
    return hr, hi


def _rglru(xc, wa, ba, wi, bi, lam, h0, reverse):
    f32 = jnp.float32
    bn, t, _ = xc.shape
    xh = xc.reshape(bn, t, LRU_HEADS, LRU_HEAD_DIM)
    r = jax.nn.sigmoid(jnp.einsum('bthi,hij->bthj', xh, wa.astype(f32)).reshape(bn, t, D_RNN) + ba.astype(f32))
    i = jax.nn.sigmoid(jnp.einsum('bthi,hij->bthj', xh, wi.astype(f32)).reshape(bn, t, D_RNN) + bi.astype(f32))
    log_a = -LRU_C * r * jax.nn.softplus(-lam.astype(f32))
    a = jnp.exp(log_a)
    b = jnp.sqrt(-jnp.expm1(2.0 * log_a)) * (i * xc)
    return _linear_scan(a, b, h0, reverse)


def _s5_states(u, lam_re, lam_im, log_dt, b_re, b_im, h0, reverse):
    f32 = jnp.float32
    lr, li = lam_re.astype(f32), lam_im.astype(f32)
    dt = jnp.exp(log_dt.astype(f32))[:, None]
    mag = jnp.exp(lr * dt)
    ang = li * dt
    bar_r, bar_i = mag * jnp.cos(ang), mag * jnp.sin(ang)
    den = lr * lr + li * li
    fr = ((bar_r - 1.0) * lr + bar_i * li) / den
    fi = (bar_i * lr - (bar_r - 1.0) * li) / den
    br_, bi_ = b_re.astype(f32), b_im.astype(f32)
    bbr = fr[..., None] * br_ - fi[..., None] * bi_
    bbi = fr[..., None] * bi_ + fi[..., None] * br_
    xr = jnp.einsum('btgk,gpk->btgp', u, bbr)
    xi = jnp.einsum('btgk,gpk->btgp', u, bbi)
    shape = (1, u.shape[1]) + bar_r.shape
    return _complex_scan(jnp.broadcast_to(bar_r, shape), jnp.broadcast_to(bar_i, shape), xr, xi, h0, reverse)


def _pool_latent(v):
    bn, t, ch = v.shape
    rows = t // GRID_W
    grid = v.reshape(bn, rows, GRID_W, ch)
    sat = jnp.pad(jnp.cumsum(jnp.cumsum(grid, axis=1), axis=2), ((0, 0), (1, 0), (1, 0), (0, 0)))
    r = jnp.arange(rows)
    col = jnp.arange(GRID_W)
    outs = []
    for g, w in enumerate(POOL_WINDOWS):
        half = w // 2
        r0, r1 = jnp.clip(r - half, 0, rows), jnp.clip(r + half, 0, rows)
        c0, c1 = jnp.clip(col - half, 0, GRID_W), jnp.clip(col + half, 0, GRID_W)
        sg = sat[..., g * POOL_GROUP:(g + 1) * POOL_GROUP]
        band = sg[:, r1] - sg[:, r0]
        box = band[:, :, c1] - band[:, :, c0]
        cnt = ((r1 - r0)[:, None] * (c1 - c0)[None, :]).astype(jnp.float32)
        outs.append(box / cnt[None, :, :, None])
    return jnp.concatenate(outs, axis=-1).reshape(bn, t, ch)


def _pool_seq(v):
    bn, t, ch = v.shape
    cs = jnp.pad(jnp.cumsum(v, axis=1), ((0, 0), (1, 0), (0, 0)))
    pos = jnp.arange(t)
    outs = []
    for g, w in enumerate(POOL_WINDOWS):
        half = w // 2
        t0, t1 = jnp.clip(pos - half, 0, t), jnp.clip(pos + half, 0, t)
        sg = cs[..., g * POOL_GROUP:(g + 1) * POOL_GROUP]
        cnt = (t1 - t0).astype(jnp.float32)[None, :, None]
        outs.append((sg[:, t1] - sg[:, t0]) / cnt)
    return jnp.concatenate(outs, axis=-1)


def _token_mixers(proj, p, h0, pool_fn, need_output, need_state):
    f32 = jnp.float32
    bn, t = proj.shape[0], proj.shape[1]
    xc = _dwconv(proj[..., COL_AX:COL_BU], p['conv_w'], p['conv_b']).astype(f32)
    u = proj[..., COL_BU:COL_AG].astype(f32).reshape(bn, t, SSM_GROUPS, SSM_GROUP)
    rnn_h, ssm_y, lru_fin, s5_fin = [], [], [], []
    for d, rev in enumerate((False, True)):
        end = 0 if rev else t - 1
        h = _rglru(xc, p['lru_wa'][d], p['lru_ba'][d], p['lru_wi'][d], p['lru_bi'][d], p['lru_lambda'][d],
                   None if h0 is None else h0[0][d], rev)
        hr, hi = _s5_states(u, p['s5_lre'][d], p['s5_lim'][d], p['s5_log_dt'][d], p['s5_b_re'], p['s5_b_im'],
                            None if h0 is None else h0[1][d], rev)
        if need_state:
            lru_fin.append(h[:, end])
            s5_fin.append((hr[:, end], hi[:, end]))
        if need_output:
            rnn_h.append(h)
            ssm_y.append(jnp.einsum('btgp,gkp->btgk', hr, p['s5_c_re'][d].astype(f32))
                         - jnp.einsum('btgp,gkp->btgk', hi, p['s5_c_im'][d].astype(f32)))
    finals = (tuple(lru_fin), tuple(s5_fin)) if need_state else None
    if not need_output:
        return None, finals
    y_a = jax.nn.gelu(proj[..., COL_AG:COL_CV].astype(f32)) * (rnn_h[0] + rnn_h[1])
    y_s = (p['s5_d'].astype(f32).reshape(SSM_GROUPS, SSM_GROUP) * u + ssm_y[0] + ssm_y[1]).reshape(bn, t, D_SSM)
    z = jax.nn.gelu(y_s)
    y_b = z * jax.nn.sigmoid(z @ p['s5_glu_w'].astype(f32) + p['s5_glu_b'].astype(f32))
    v = proj[..., COL_CV:].astype(f32)
    m = (pool_fn(v) - v).reshape(bn, t, len(POOL_WINDOWS), POOL_GROUP)
    y_c = (jnp.einsum('btgi,gij->btgj', m, p['pool_w'].astype(f32)).reshape(bn, t, D_POOL)
           + p['pool_b'].astype(f32)) * p['pool_scale'].astype(f32)
    y = jnp.concatenate([y_a, y_b, y_c], axis=-1).astype(proj.dtype)
    return y, finals


def setup_inputs(seed: int = 0) -> dict:
    key = jax.random.key(seed)
    ks = iter(jax.random.split(key, 48))
    f32 = jnp.float32
    L, D = DEPTH, D_MODEL

    def nrm(shape, scale):
        return scale * jax.random.normal(next(ks), shape, f32)

    def unif(shape, lo, hi):
        return jax.random.uniform(next(ks), shape, f32, lo, hi)

    x = nrm((BATCH, SEQ, D), 1.0)
    c = nrm((BATCH, D), 1.0)
    ctx = nrm((BATCH, CTX_LEN, D), 1.0)
    c_ctx = nrm((D,), 1.0)
    w_mod = nrm((L, D, 6 * D), D ** -0.5)
    b_mod = nrm((L, 6 * D), 0.02)
    norm1_g = 1.0 + nrm((L, D), 0.02)
    norm2_g = 1.0 + nrm((L, D), 0.02)
    w_in = nrm((L, D, D_IN), D ** -0.5)
    w_out = nrm((L, D_MIX, D), D_MIX ** -0.5)
    lru_conv_w = nrm((L, CONV_WIDTH, D_RNN), CONV_WIDTH ** -0.5)
    lru_conv_b = nrm((L, D_RNN), 0.02)
    lru_wa = nrm((L, 2, LRU_HEADS, LRU_HEAD_DIM, LRU_HEAD_DIM), LRU_HEAD_DIM ** -0.5)
    lru_ba = nrm((L, 2, D_RNN), 0.02)
    lru_wi = nrm((L, 2, LRU_HEADS, LRU_HEAD_DIM, LRU_HEAD_DIM), LRU_HEAD_DIM ** -0.5)
    lru_bi = nrm((L, 2, D_RNN), 0.02)
    a_c = unif((L, 2, D_RNN), 0.9, 0.999)
    a_base = a_c ** (1.0 / LRU_C)
    lru_lambda = jnp.log(a_base) - jnp.log1p(-a_base)
    s5_lambda_re = -0.5 + nrm((L, 2, SSM_GROUPS, SSM_STATE), 0.01)
    s5_lambda_im = jnp.pi * jnp.arange(SSM_STATE, dtype=f32) + nrm((L, 2, SSM_GROUPS, SSM_STATE), 0.01)
    s5_log_dt = unif((L, 2, SSM_GROUPS), math.log(DT_MIN), math.log(DT_MAX))
    s5_b_re = nrm((L, SSM_GROUPS, SSM_STATE, SSM_GROUP), (2 * SSM_GROUP) ** -0.5)
    s5_b_im = nrm((L, SSM_GROUPS, SSM_STATE, SSM_GROUP), (2 * SSM_GROUP) ** -0.5)
    s5_c_re = nrm((L, 2, SSM_GROUPS, SSM_GROUP, SSM_STATE), (2 * SSM_STATE) ** -0.5)
    s5_c_im = nrm((L, 2, SSM_GROUPS, SSM_GROUP, SSM_STATE), (2 * SSM_STATE) ** -0.5)
    s5_d = nrm((L, D_SSM), 1.0)
    s5_glu_w = nrm((L, D_SSM, D_SSM), D_SSM ** -0.5)
    s5_glu_b = nrm((L, D_SSM), 0.02)
    pool_w = nrm((L, len(POOL_WINDOWS), POOL_GROUP, POOL_GROUP), POOL_GROUP ** -0.5)
    pool_b = nrm((L, D_POOL), 0.02)
    pool_scale = 1.0 + nrm((L, D_POOL), 0.02)
    ffn_w_gate = nrm((L, D, D_FF), D ** -0.5)
    ffn_w_up = nrm((L, D, D_FF), D ** -0.5)
    ffn_w_down = nrm((L, D_FF, D), D_FF ** -0.5)
    final_g = 1.0 + nrm((D,), 0.02)
    return {'x': x, 'c': c, 'ctx': ctx, 'c_ctx': c_ctx, 'w_mod': w_mod, 'b_mod': b_mod,
            'norm1_g': norm1_g, 'norm2_g': norm2_g, 'w_in': w_in, 'w_out': w_out,
            'lru_conv_w': lru_conv_w, 'lru_conv_b': lru_conv_b, 'lru_wa': lru_wa, 'lru_ba': lru_ba,
            'lru_wi': lru_wi, 'lru_bi': lru_bi, 'lru_lambda': lru_lambda,
            's5_lambda_re': s5_lambda_re, 's5_lambda_im': s5_lambda_im, 's5_log_dt': s5_log_dt,
            's5_b_re': s5_b_re, 's5_b_im': s5_b_im, 's5_c_re': s5_c_re, 's5_c_im': s5_c_im,
            's5_d': s5_d, 's5_glu_w': s5_glu_w, 's5_glu_b': s5_glu_b,
            'pool_w': pool_w, 'pool_b': pool_b, 'pool_scale': pool_scale,
            'ffn_w_gate': ffn_w_gate, 'ffn_w_up': ffn_w_up, 'ffn_w_down': ffn_w_down, 'final_g': final_g}


def reference(x, c, ctx, c_ctx, w_mod, b_mod, norm1_g, norm2_g, w_in, w_out,
              lru_conv_w, lru_conv_b, lru_wa, lru_ba, lru_wi, lru_bi, lru_lambda,
              s5_lambda_re, s5_lambda_im, s5_log_dt, s5_b_re, s5_b_im, s5_c_re, s5_c_im,
              s5_d, s5_glu_w, s5_glu_b, pool_w, pool_b, pool_scale,
              ffn_w_gate, ffn_w_up, ffn_w_down, final_g):
    D = D_MODEL
    silu_c = jax.nn.silu(c)
    silu_cc = jax.nn.silu(c_ctx)
    for l in range(DEPTH):
        last = l == DEPTH - 1
        p = {'conv_w': lru_conv_w[l], 'conv_b': lru_conv_b[l], 'lru_wa': lru_wa[l], 'lru_ba': lru_ba[l],
             'lru_wi': lru_wi[l], 'lru_bi': lru_bi[l], 'lru_lambda': lru_lambda[l],
             's5_lre': s5_lambda_re[l], 's5_lim': s5_lambda_im[l], 's5_log_dt': s5_log_dt[l],
             's5_b_re': s5_b_re[l], 's5_b_im': s5_b_im[l], 's5_c_re': s5_c_re[l], 's5_c_im': s5_c_im[l],
             's5_d': s5_d[l], 's5_glu_w': s5_glu_w[l], 's5_glu_b': s5_glu_b[l],
             'pool_w': pool_w[l], 'pool_b': pool_b[l], 'pool_scale': pool_scale[l]}
        n_mod = 2 if last else 6
        mod_c = silu_cc @ w_mod[l][:, :n_mod * D] + b_mod[l][:n_mod * D]
        mc = jnp.split(mod_c, n_mod)
        hc = _modulate(_rmsnorm(ctx, norm1_g[l]), mc[0], mc[1])
        pc = hc @ (w_in[l][:, :SCAN_COLS] if last else w_in[l])
        yc, ctx_states = _token_mixers(pc, p, None, _pool_seq, not last, True)
        mod = silu_c @ w_mod[l] + b_mod[l]
        sh1, sc1, g1, sh2, sc2, g2 = [m[:, None, :] for m in jnp.split(mod, 6, axis=-1)]
        hx = _modulate(_rmsnorm(x, norm1_g[l]), sh1, sc1)
        yx, _ = _token_mixers(hx @ w_in[l], p, ctx_states, _pool_latent, True, False)
        x = x + g1 * (yx @ w_out[l])
        x = x + g2 * _swiglu(_modulate(_rmsnorm(x, norm2_g[l]), sh2, sc2),
                             ffn_w_gate[l], ffn_w_up[l], ffn_w_down[l])
        if not last:
            ctx = ctx + mc[2] * (yc @ w_out[l])
            ctx = ctx + mc[5] * _swiglu(_modulate(_rmsnorm(ctx, norm2_g[l]), mc[3], mc[4]),
                                        ffn_w_gate[l], ffn_w_up[l], ffn_w_down[l])
    return _rmsnorm(x, final_g)
```

```python
import math
from contextlib import ExitStack

import numpy as np
import concourse.bass as bass
import concourse.mybir as mybir
from concourse.bass_utils import run_bass_kernel_spmd

F32 = mybir.dt.float32
BF16 = mybir.dt.bfloat16
AF = mybir.ActivationFunctionType
ALU = mybir.AluOpType

D = 1024
T = 8192
TX = 256
DIN = 1408
DFF = 2816
NL = 2
NCOL = 116
ENG = ("pe", "act", "dve", "pool", "sp")
SBW = 53184

C_N1G, C_N2G, C_BMOD, C_CONVW, C_CONVB, C_BA, C_BI, C_LAM, C_S5D, C_GLUB, C_POOLB, C_POOLS, C_FING = (
    0, 8, 16, 64, 76, 79, 85, 91, 97, 100, 103, 105, 107)

POOL_HALF = (1, 2, 4, 8)


class Res:
    __slots__ = ("w", "r", "name")

    def __init__(self, name=""):
        self.w = None
        self.r = {}
        self.name = name


class Buf:
    def __init__(self, ap, res):
        self.ap = ap
        self.res = list(res) if isinstance(res, (list, tuple)) else [res]

    def __getitem__(self, idx):
        return Buf(self.ap[idx], self.res)

    def re(self, pat, **kw):
        return Buf(self.ap.rearrange(pat, **kw), self.res)

    def bc(self, shape):
        return Buf(self.ap.broadcast_to(list(shape)), self.res)

    def wr(self, res):
        return Buf(self.ap, res)


def _aps(x):
    return x.ap if isinstance(x, Buf) else x


class KB:
    def __init__(self, nc, big, psums):
        self.nc = nc
        self.big = big
        self.streams = {e: [] for e in ENG}
        self.cnt = {e: 0 for e in ENG}
        self.seen = {e: {} for e in ENG}
        self.ndma = 40
        self.dma_val = [0] * self.ndma
        self.dma_rr = 0
        self.sb_off = 0
        self.sb_peak = 0
        self.psums = psums
        self.bank_res = [Res(f"bank{i}") for i in range(8)]
        self.bank_rr = 0

    def alloc(self, name, shape, dtype=F32, nres=1):
        p = shape[0]
        n = int(np.prod(shape[1:]))
        isz = 2 if dtype == BF16 else 4
        words = (n * isz + 3) // 4
        words = (words + 7) // 8 * 8
        off = self.sb_off
        self.sb_off += words
        self.sb_peak = max(self.sb_peak, self.sb_off)
        assert self.sb_off <= SBW, f"SBUF arena overflow at {name}: {self.sb_off}"
        ap = self.big[0:p, off:off + words]
        if dtype == BF16:
            ap = ap.bitcast(BF16)
        ap = ap[:, 0:n]
        if len(shape) == 3:
            ap = ap.rearrange("p (a b) -> p a b", a=shape[1])
        elif len(shape) == 4:
            ap = ap.rearrange("p (a b c) -> p a b c", a=shape[1], b=shape[2])
        if nres == 1:
            return Buf(ap, Res(name))
        return Buf(ap, [Res(f"{name}{i}") for i in range(nres)])

    def mark(self):
        return self.sb_off

    def release(self, mark):
        self.barrier()
        self.sb_off = mark

    def bank(self, n=1):
        if n == 2 and self.bank_rr % 2:
            self.bank_rr += 1
        k = self.bank_rr % 8
        self.bank_rr += n
        ap = self.psums[k // 2]
        if n == 1:
            return Buf(ap[:, (k % 2) * 512:(k % 2) * 512 + 512], self.bank_res[k])
        return Buf(ap[:, :], [self.bank_res[k], self.bank_res[k + 1]])

    def _need(self, eng, ev):
        if ev is None:
            return
        key, val = ev
        if key == eng and eng == "pe":
            return
        if self.seen[eng].get(key, 0) >= val:
            return
        self.seen[eng][key] = val
        self.streams[eng].append(("w", key, val))

    def _deps(self, eng, reads, writes):
        for r in reads:
            self._need(eng, r.w)
        for w in writes:
            self._need(eng, w.w)
            for ev in w.r.values():
                self._need(eng, ev)

    def _done(self, key, ev, reads, writes):
        for r in reads:
            r.r[key] = ev
        for w in writes:
            w.w = ev
            w.r = {}

    def op(self, eng, fn, ins=(), outs=()):
        reads = [r for b in ins if isinstance(b, Buf) for r in b.res]
        writes = [r for b in outs if isinstance(b, Buf) for r in b.res]
        self._deps(eng, reads, writes)
        self.cnt[eng] += 1
        ev = (eng, self.cnt[eng])
        self.streams[eng].append(("i", fn, True))
        self._done(eng, ev, reads, writes)

    def mm(self, out, pairs, first_start=True):
        reads = [r for pr in pairs for b in pr[:2] for r in b.res]
        writes = list(out.res)
        self._deps("pe", reads, writes)
        n = len(pairs)
        for i, pr in enumerate(pairs):
            l, rr = pr[0], pr[1]
            o = pr[2].ap if len(pr) > 2 else out.ap
            self.streams["pe"].append(
                ("i", (lambda e, o=o, l=l.ap, r=rr.ap, st=(i == 0 and first_start), sp=(i == n - 1):
                       e.matmul(o, l, r, start=st, stop=sp)), i == n - 1))
        self.cnt["pe"] += 1
        self._done("pe", ("pe", self.cnt["pe"]), reads, writes)

    def transpose(self, out, in_, ident):
        self.op("pe", lambda e, o=out.ap, i=in_.ap, d=ident.ap: e.transpose(o, i, d), [in_, ident], [out])

    def dma(self, q, out, in_, **kw):
        i = self.dma_rr
        self.dma_rr = (i + 1) % self.ndma
        key = ("d", i)
        if self.dma_val[i] > 0:
            self._need(q, (key, self.dma_val[i]))
        reads = list(in_.res)
        writes = list(out.res)
        self._deps(q, reads, writes)
        self.dma_val[i] += 16
        ev = (key, self.dma_val[i])
        self.streams[q].append(("d", out.ap, in_.ap, i, kw))
        self._done(key, ev, reads, writes)

    def barrier(self):
        evs = [(e, self.cnt[e]) for e in ENG if self.cnt[e] > 0]
        evs += [(("d", i), v) for i, v in enumerate(self.dma_val) if v > 0]
        for e in ENG:
            for ev in evs:
                self._need(e, ev)

    def act(self, out, in_, func, bias=None, scale=None, eng="act"):
        kw = {}
        ins = [in_]
        if bias is not None:
            kw["bias"] = _aps(bias)
            ins.append(bias)
        if scale is not None:
            kw["scale"] = _aps(scale)
            ins.append(scale)
        self.op(eng, lambda e, o=out.ap, i=in_.ap, f=func, kw=kw: e.activation(out=o, in_=i, func=f, **kw), ins, [out])

    def tt(self, eng, out, a, b, op):
        self.op(eng, lambda e, o=out.ap, a_=a.ap, b_=b.ap, op=op: e.tensor_tensor(out=o, in0=a_, in1=b_, op=op), [a, b], [out])

    def ts(self, eng, out, a, s1, op0, s2=None, op1=None):
        def fn(e, o=out.ap, a_=a.ap, s1=_aps(s1), s2=_aps(s2), op0=op0, op1=op1):
            if op1 is None:
                return e.tensor_scalar(out=o, in0=a_, scalar1=s1, scalar2=None, op0=op0)
            return e.tensor_scalar(out=o, in0=a_, scalar1=s1, scalar2=s2, op0=op0, op1=op1)
        self.op(eng, fn, [a, s1, s2], [out])

    def stt(self, out, a, s, b, op0, op1):
        self.op("dve", lambda e, o=out.ap, a_=a.ap, s_=_aps(s), b_=b.ap, op0=op0, op1=op1:
                e.scalar_tensor_tensor(out=o, in0=a_, scalar=s_, in1=b_, op0=op0, op1=op1), [a, s, b], [out])

    def copy(self, eng, out, in_):
        if eng == "act":
            self.act(out, in_, AF.Copy)
        else:
            self.op(eng, lambda e, o=out.ap, i=in_.ap: e.tensor_copy(out=o, in_=i), [in_], [out])

    def memset(self, eng, out, val):
        self.op(eng, lambda e, o=out.ap, v=val: e.memset(o, v), [], [out])

    def scan(self, out, a, b, init):
        self.op("dve", lambda e, o=out.ap, a_=a.ap, b_=b.ap, i_=_aps(init):
                e.tensor_tensor_scan(out=o, data0=a_, data1=b_, initial=i_, op0=ALU.mult, op1=ALU.add),
                [a, b, init], [out])

    def recip(self, out, in_):
        self.op("dve", lambda e, o=out.ap, i=in_.ap: e.reciprocal(out=o, in_=i), [in_], [out])

    def replay(self, sems, dsems):
        nc = self.nc
        engs = {"pe": "tensor", "act": "scalar", "dve": "vector", "pool": "gpsimd", "sp": "sync"}

        def semh(key):
            return dsems[key[1]] if isinstance(key, tuple) else sems[key]

        with nc.Block() as block:
            for name in ENG:
                stream = self.streams[name]

                def body(e, stream=stream, name=name):
                    for it in stream:
                        if it[0] == "w":
                            e.wait_ge(semh(it[1]), it[2])
                        elif it[0] == "i":
                            ins = it[1](e)
                            if it[2]:
                                ins.then_inc(sems[name], 1)
                        else:
                            _, o, i, k, kw = it
                            e.dma_start(out=o, in_=i, **kw).then_inc(dsems[k], 16)
                getattr(block, engs[name])(body)


def _colT(v):
    return np.ascontiguousarray(np.asarray(v, np.float32).reshape(-1, 128).T)


def _pool_consts():
    lat, latidx = [], {}
    for g, half in enumerate(POOL_HALF):
        for dlt in range(-4, 5):
            m = np.zeros((128, 128), np.float32)
            for ri in range(2):
                for ro in range(2):
                    rel = 2 * dlt + ri
                    if not (ro - half <= rel < ro + half):
                        continue
                    for co in range(64):
                        lo, hi = max(co - half, 0), min(co + half, 64)
                        m[ri * 64 + lo:ri * 64 + hi, ro * 64 + co] = 1.0 / (hi - lo)
            if m.any():
                latidx[(g, dlt)] = len(lat)
                lat.append(m)
    ctx, ctxidx = [], {}
    for g, half in enumerate(POOL_HALF):
        for B in range(2):
            for dlt in (-1, 0, 1):
                if not 0 <= B + dlt < 2:
                    continue
                m = np.zeros((128, 128), np.float32)
                for o in range(128):
                    to = 128 * B + o
                    lo, hi = max(to - half, 0), min(to + half, TX)
                    for ti in range(lo, hi):
                        if 128 * (B + dlt) <= ti < 128 * (B + dlt + 1):
                            m[ti - 128 * (B + dlt), o] = 1.0 / (hi - lo)
                if m.any():
                    ctxidx[(g, B, dlt)] = len(ctx)
                    ctx.append(m)
    rinv = np.zeros((2, 128, 128), np.float32)
    for pt in range(2):
        for p in range(128):
            half = POOL_HALF[2 * pt + p // 64]
            for r in range(128):
                rinv[pt, p, r] = 1.0 / (min(r + half, 128) - max(r - half, 0))
    return np.stack(lat), latidx, np.stack(ctx), ctxidx, rinv


PLAT, LATIDX, PCTX, CTXIDX, RINV = _pool_consts()


def _consts():
    p = np.arange(128)
    ident = np.eye(128, dtype=np.float32)
    bdmask = (p[:, None] // 16 == p[None, :] // 16).astype(np.float32)
    qmask = np.stack([((p // 16) % 4 == v) for v in range(4)], axis=1).astype(np.float32)
    colmask = np.broadcast_to(qmask.T[None, :, :], (128, 4, 128)).astype(np.float32).copy()
    ii = np.concatenate([np.eye(64, dtype=np.float32)] * 2, axis=0)
    gmask = np.stack([((p // 16) == v) for v in range(8)], axis=1).astype(np.float32)
    misc = np.concatenate([ident, bdmask, colmask.reshape(128, 512), ii, qmask, gmask], axis=1)
    return np.ascontiguousarray(misc)


MISC = _consts()
M_ID, M_BD, M_CM, M_II, M_EV = 0, 128, 256, 768, 832
NMISC = 844
M_GM = 836


def pack_shared(inp):
    f = lambda k: np.asarray(inp[k], np.float32)
    sh = {}
    pcol = np.zeros((NL, 128, NCOL), np.float32)
    for l in range(NL):
        pc = pcol[l]
        pc[:, C_N1G:C_N1G + 8] = _colT(f("norm1_g")[l])
        pc[:, C_N2G:C_N2G + 8] = _colT(f("norm2_g")[l])
        pc[:, C_BMOD:C_BMOD + 48] = _colT(f("b_mod")[l])
        for j in range(3):
            sl = slice(j * 128, (j + 1) * 128)
            for k in range(4):
                pc[:, C_CONVW + j * 4 + k] = f("lru_conv_w")[l, k, sl]
            pc[:, C_CONVB + j] = f("lru_conv_b")[l, sl]
            for d in range(2):
                pc[:, C_BA + d * 3 + j] = f("lru_ba")[l, d, sl]
                pc[:, C_BI + d * 3 + j] = f("lru_bi")[l, d, sl]
                pc[:, C_LAM + d * 3 + j] = f("lru_lambda")[l, d, sl]
            pc[:, C_S5D + j] = f("s5_d")[l, sl]
            pc[:, C_GLUB + j] = f("s5_glu_b")[l, sl]
        for j in range(2):
            sl = slice(j * 128, (j + 1) * 128)
            pc[:, C_POOLB + j] = f("pool_b")[l, sl]
            pc[:, C_POOLS + j] = f("pool_scale")[l, sl]
        pc[:, C_FING:C_FING + 8] = _colT(f("final_g"))
    sh["pcol"] = pcol
    for k in ("w_mod", "w_in", "w_out", "ffn_w_gate", "ffn_w_up", "ffn_w_down", "pool_w"):
        sh[k] = np.ascontiguousarray(f(k))
    sh["glu_w"] = np.ascontiguousarray(f("s5_glu_w"))
    sh["lruw"] = np.ascontiguousarray(np.stack([f("lru_wa"), f("lru_wi")], axis=2))
    s5p = np.zeros((NL, 2, 128, 72), np.float32)
    s5b = np.zeros((NL, 128, 768), np.float32)
    s5c = np.zeros((NL, 2, 128, 768), np.float32)
    for l in range(NL):
        s5b[l, :, 0:384] = np.tile(f("s5_b_re")[l].transpose(1, 0, 2).reshape(64, 384), (2, 1))
        s5b[l, :, 384:768] = np.tile(f("s5_b_im")[l].transpose(1, 0, 2).reshape(64, 384), (2, 1))
        for d in range(2):
            s5p[l, d, :, 0:24] = np.tile(f("s5_lambda_re")[l, d].T, (2, 1))
            s5p[l, d, :, 24:48] = np.tile(f("s5_lambda_im")[l, d].T, (2, 1))
            s5p[l, d, :, 48:72] = np.broadcast_to(f("s5_log_dt")[l, d][None, :], (128, 24))
            s5c[l, d, :, 0:384] = np.tile(f("s5_c_re")[l, d].transpose(2, 0, 1).reshape(64, 384), (2, 1))
            s5c[l, d, :, 384:768] = np.tile(f("s5_c_im")[l, d].transpose(2, 0, 1).reshape(64, 384), (2, 1))
    sh["s5p"], sh["s5b"], sh["s5c"] = s5p, s5b, s5c
    sh["misc"] = MISC
    sh["plat"] = PLAT
    sh["pctx"] = PCTX
    sh["rinv"] = RINV
    return sh


def pack_core(inp, b):
    x = np.asarray(inp["x"], np.float32)[b]
    ctx = np.asarray(inp["ctx"], np.float32)[b]
    cc = np.stack([_colT(np.asarray(inp["c"], np.float32)[b]), _colT(np.asarray(inp["c_ctx"], np.float32))], axis=2)
    return {"xT": np.ascontiguousarray(x.T), "ctxT": np.ascontiguousarray(ctx.T),
            "cc": np.ascontiguousarray(cc.reshape(128, 16))}


class G:
    pass


def dram_in(nc, name, shape, dtype=F32):
    return Buf(nc.dram_tensor(name, list(shape), dtype, kind="ExternalInput").ap(), Res(name))


def dram_tmp(nc, name, shape, dtype, nres, dbg):
    kind = "ExternalOutput" if name in dbg else "Internal"
    return Buf(nc.dram_tensor(name, list(shape), dtype, kind=kind).ap(), [Res(f"{name}{i}") for i in range(nres)])


def chunk_res(buf, c0, c1, csz):
    return buf.res[c0 // csz:(c1 - 1) // csz + 1]


RC = 512


def colsl(buf, rows, c0, c1):
    return Buf(buf.ap[rows, c0:c1], chunk_res(buf, c0, c1, RC))


def stage0(kb, g):
    nc = kb.nc
    g.ones = kb.alloc("ones", [128, 128], BF16)
    kb.memset("pool", g.ones, 1.0)
    g.epsc = kb.alloc("epsc", [128, 1])
    kb.memset("pool", g.epsc, 1e-6)
    g.pcol = []
    for l in range(NL):
        pc = kb.alloc(f"pcol{l}", [128, NCOL])
        kb.dma("sp", pc, g.d_pcol[l])
        g.pcol.append(pc)
    cc = kb.alloc("cc", [128, 16])
    kb.dma("sp", cc, g.d_cc)
    scc = kb.alloc("scc", [128, 16])
    kb.act(scc, cc, AF.Silu)
    scc3 = scc.re("p (k j) -> p k j", j=2)
    g.mod = [kb.alloc(f"mod{l}", [128, 48, 2]) for l in range(NL)]
    mk = kb.mark()
    idn2 = kb.alloc("idn2", [128, 2])
    kb.dma("sp", idn2[0:2, :], g.d_misc[0:2, 0:2])
    modrow = kb.alloc("modrow", [128, 6 * D])
    wm = [kb.alloc(f"wm{i}", [128, 8, 512]) for i in range(2)]
    it = 0
    for l in range(NL):
        for cg in range(12):
            w = wm[it % 2]
            it += 1
            kb.dma("sp", w, g.d_wmod[l][:, cg * 512:(cg + 1) * 512].re("(k p) c -> p k c", p=128))
            bank = kb.bank()
            kb.mm(bank[0:2, :], [(scc3[:, k, :], w[:, k, :]) for k in range(8)])
            kb.copy("act" if cg % 2 else "dve", modrow[0:2, cg * 512:(cg + 1) * 512], bank[0:2, :])
        bank = kb.bank()
        for m in range(48):
            kb.transpose(bank[:, 2 * m:2 * m + 2], modrow[0:2, m * 128:(m + 1) * 128], idn2[0:2, 0:2])
        kb.tt("dve", g.mod[l], bank[:, 0:96].re("p (m j) -> p m j", j=2),
              g.pcol[l][:, C_BMOD:C_BMOD + 48].re("p (m o) -> p m o", o=1).bc([128, 48, 2]), ALU.add)
    kb.release(mk)
    g.gs1, g.gs2 = [], []
    for l in range(NL):
        for which, cg0, coln, lst in ((1, 8, C_N1G, g.gs1), (2, 32, C_N2G, g.gs2)):
            t = kb.alloc(f"gs{which}_{l}", [128, 8, 2])
            kb.ts("dve", t, g.mod[l][:, cg0:cg0 + 8, :], 1.0, ALU.add)
            kb.tt("dve", t, t, g.pcol[l][:, coln:coln + 8].re("p (m o) -> p m o", o=1).bc([128, 8, 2]), ALU.mult)
            lst.append(t)


def modcol(g, l, which, m, j):
    return g.mod[l][:, which * 8 + m, j:j + 1]


def rmsnorm_mod(kb, g, xs, tc, gs, l, which_sh, j, hx, tmp):
    sq, rs, xn = tmp["sq"], tmp["rs"], tmp["xn"]
    if not tmp.get("presquared"):
        kb.act(sq[:, :, 0:tc], xs[:, :, 0:tc], AF.Square)
    bank = kb.bank()
    kb.mm(bank[:, 0:tc], [(g.ones, sq[:, k, 0:tc]) for k in range(8)])
    kb.act(rs[:, 0:tc], bank[:, 0:tc], AF.Sqrt, bias=g.epsc, scale=1.0 / D)
    kb.recip(rs[:, 0:tc], rs[:, 0:tc])
    kb.tt("dve", xn[:, :, 0:tc], xs[:, :, 0:tc], rs[:, 0:tc].re("p (o t) -> p o t", o=1).bc([128, 8, tc]), ALU.mult)
    for k in range(8):
        kb.act(hx[:, k, 0:tc], xn[:, k, 0:tc], AF.Identity, bias=modcol(g, l, which_sh, k, j), scale=gs[:, k, j:j + 1])


def stage1(kb, g, l, seq):
    j = 1 if seq == "ctx" else 0
    tlen = TX if j else T
    tc = min(512, tlen)
    src = g.xcur[seq]
    dst = g.P[seq]
    VT = g.VT[seq]
    mk = kb.mark()
    vts = [kb.alloc(f"s1vt{i}", [128, 4, 256], BF16) for i in range(2)]
    xs = [kb.alloc(f"s1xs{i}", [128, 8, tc]) for i in range(2)]
    hx = [kb.alloc(f"s1hx{i}", [128, 8, tc], BF16) for i in range(2)]
    tmp = {"sq": kb.alloc("s1sq", [128, 8, tc], BF16), "rs": kb.alloc("s1rs", [128, tc]),
           "xn": kb.alloc("s1xn", [128, 8, tc])}
    pst = [kb.alloc(f"s1pst{i}", [128, 11, tc]) for i in range(2)]
    nch = tlen // tc

    def load(c):
        kb.dma("sp", xs[c % 2], Buf(src.ap[:, c * tc:(c + 1) * tc].rearrange("(k p) t -> p k t", p=128),
                                    chunk_res(src, c * tc, (c + 1) * tc, RC)))
    load(0)
    if nch > 1:
        load(1)
    rmsnorm_mod(kb, g, xs[0], tc, g.gs1[l], l, 0, j, hx[0], tmp)
    tmp["presquared"] = True
    if nch > 1:
        kb.act(tmp["sq"], xs[1], AF.Square)
    for c in range(nch):
        if c + 1 < nch:
            rmsnorm_mod(kb, g, xs[(c + 1) % 2], tc, g.gs1[l], l, 0, j, hx[(c + 1) % 2], tmp)
        if c + 2 < nch:
            load(c + 2)
            kb.act(tmp["sq"], xs[c % 2], AF.Square)
        h = hx[c % 2]
        ps = pst[c % 2]
        for m in range(11):
            bank = kb.bank()
            kb.mm(bank[:, 0:tc], [(g.win[:, k, m * 128:(m + 1) * 128], h[:, k, :]) for k in range(8)])
            kb.copy("act" if m % 2 else "dve", ps[:, m, :], bank[:, 0:tc])
        kb.dma("sp", Buf(dst.ap[:, c * tc:(c + 1) * tc].rearrange("(m p) t -> p m t", p=128),
                         chunk_res(dst, c * tc, (c + 1) * tc, RC)), ps)
        for b4 in range(tc // 128):
            blk = c * (tc // 128) + b4
            bank = kb.bank()
            kb.mm(bank[:, 0:256], [(h[:, k, b4 * 128:(b4 + 1) * 128], g.win[:, k, 1152:1408]) for k in range(8)])
            kb.copy("act", vts[c % 2][:, b4, :], bank[:, 0:256])
        nb4 = tc // 128
        kb.dma("sp", Buf(VT.ap[:, c * nb4:(c + 1) * nb4, :], VT.res), vts[c % 2][:, 0:nb4, :])
    kb.release(mk)


def build_program(stop="all", dbg=()):
    nc = bass.Bass("TRN2", target_bir_lowering=False)
    g = G()
    g.d_xT = dram_in(nc, "xT", [D, T]); g.d_xT.res = [Res(f"xT{i}") for i in range(T // RC)]
    g.d_ctxT = dram_in(nc, "ctxT", [D, TX])
    g.d_cc = dram_in(nc, "cc", [128, 16])
    g.d_misc = dram_in(nc, "misc", [128, NMISC])
    d_pcol = dram_in(nc, "pcol", [NL, 128, NCOL]); g.d_pcol = [d_pcol[l] for l in range(NL)]
    d_wmod = dram_in(nc, "w_mod", [NL, D, 6 * D]); g.d_wmod = [d_wmod[l] for l in range(NL)]
    g.d_win = dram_in(nc, "w_in", [NL, D, DIN])
    g.d_wout = dram_in(nc, "w_out", [NL, D, D])
    g.d_lruw = dram_in(nc, "lruw", [NL, 2, 2, 6, 64, 64])
    g.d_gluw = dram_in(nc, "glu_w", [NL, 384, 384])
    g.d_poolw = dram_in(nc, "pool_w", [NL, 4, 64, 64])
    g.d_wg = dram_in(nc, "ffn_w_gate", [NL, D, DFF])
    g.d_wu = dram_in(nc, "ffn_w_up", [NL, D, DFF])
    g.d_wd = dram_in(nc, "ffn_w_down", [NL, DFF, D])
    g.d_s5p = dram_in(nc, "s5p", [NL, 2, 128, 72])
    g.d_s5b = dram_in(nc, "s5b", [NL, 128, 768])
    g.d_s5c = dram_in(nc, "s5c", [NL, 2, 128, 768])
    g.d_plat = dram_in(nc, "plat", list(PLAT.shape))
    g.d_pctx = dram_in(nc, "pctx", list(PCTX.shape))
    g.d_rinv = dram_in(nc, "rinv", [2, 128, 128])
    g.d_out = Buf(nc.dram_tensor("outT", [D, T], F32, kind="ExternalOutput").ap(), [Res(f"out{i}") for i in range(T // RC)])
    g.P = {"lat": dram_tmp(nc, "P_lat", [DIN, T], F32, T // RC, dbg), "ctx": dram_tmp(nc, "P_ctx", [DIN, TX], F32, 1, dbg)}
    g.YM = {"lat": dram_tmp(nc, "YM_lat", [D, T], BF16, T // RC, dbg), "ctx": dram_tmp(nc, "YM_ctx", [D, TX], BF16, 1, dbg)}
    g.ZS = {"lat": dram_tmp(nc, "ZS_lat", [384, T], BF16, T // RC, dbg), "ctx": dram_tmp(nc, "ZS_ctx", [384, TX], BF16, 1, dbg)}
    g.X1 = {"lat": dram_tmp(nc, "X1_lat", [D, T], F32, T // RC, dbg), "ctx": dram_tmp(nc, "X1_ctx", [D, TX], F32, 1, dbg)}
    g.VT = {"lat": dram_tmp(nc, "VT_lat", [128, T // 128, 256], BF16, 1, dbg), "ctx": dram_tmp(nc, "VT_ctx", [128, TX // 128, 256], BF16, 1, dbg)}
    with ExitStack() as es:
        big = es.enter_context(nc.sbuf_tensor("big", [128, SBW], F32))
        psums = [es.enter_context(nc.psum_tensor(f"ps{i}", [128, 1024], F32)) for i in range(4)]
        sems = {e: es.enter_context(nc.semaphore(f"s_{e}")) for e in ENG}
        kb = KB(nc, big, psums)
        dsems = [es.enter_context(nc.semaphore(f"d{i}")) for i in range(kb.ndma)]
        emit_all(kb, g, stop, dbg)
        kb.barrier()
        kb.replay(sems, dsems)
    g.kb = kb
    return nc, g


def emit_all(kb, g, stop, dbg):
    stage0(kb, g)
    g.xcur = {"lat": g.d_xT, "ctx": g.d_ctxT}
    for l in range(NL):
        last = l == NL - 1
        lmark = kb.mark()
        g.h0 = kb.alloc("h0", [128, 6])
        s1mark = kb.mark()
        g.win = kb.alloc("win", [128, 8, DIN], BF16)
        for k in range(8):
            kb.dma("pool", g.win[:, k, :], g.d_win[l][k * 128:(k + 1) * 128, :])
        stage1(kb, g, l, "ctx")
        stage1(kb, g, l, "lat")
        kb.release(s1mark)
        if stop == "s1":
            return
        stage2(kb, g, l, ctx_out=not last)
        if stop == "s2":
            return
        stage3(kb, g, l, ctx_out=not last)
        if stop == "s3":
            return
        stage4(kb, g, l, ctx_out=not last)
        kb.release(g.midmark)
        if stop == "s4":
            return
        stage56(kb, g, l, ("lat",) if last else ("ctx", "lat"), last)
        if stop == "l0":
            return
        g.xcur = dict(g.X1)
        kb.release(lmark)


def run(inputs, stop="all", dbg=(), cores=8, trace=False):
    nc, g = build_program(stop, dbg)
    sh = pack_shared(inputs)
    in_maps = []
    for cid in range(cores):
        m = dict(sh)
        m.update(pack_core(inputs, cid % 4))
        in_maps.append(m)
    res = run_bass_kernel_spmd(nc, in_maps, core_ids=list(range(cores)), trace=trace)
    return res, g


def kernel(**inputs):
    res, g = run(inputs)
    out = np.stack([np.ascontiguousarray(res.results[b]["outT"].T) for b in range(4)], axis=0)
    return out.astype(np.float32)


def stage2(kb, g, l, ctx_out):
    pc = g.pcol[l]
    mk = kb.mark()
    lw = kb.alloc("lruw", [128, 12, 128], BF16)
    kb.memset("pool", lw, 0.0)
    for d in range(2):
        for gate in range(2):
            for j3 in range(3):
                for hh in range(2):
                    kb.dma("pool", lw[hh * 64:(hh + 1) * 64, (d * 2 + gate) * 3 + j3, hh * 64:(hh + 1) * 64],
                           g.d_lruw[l, d, gate, 2 * j3 + hh])
    cst = kb.alloc("lrucst", [128, 18])
    hcl, hba, hbi = cst[:, 0:6], cst[:, 6:12], cst[:, 12:18]
    kb.act(hcl, pc[:, C_LAM:C_LAM + 6], AF.Exp, scale=-1.0)
    kb.ts("dve", hcl, hcl, 1.0, ALU.add)
    kb.act(hcl, hcl, AF.Ln)
    kb.ts("dve", hcl, hcl, -4.0, ALU.mult)
    kb.ts("dve", hba, pc[:, C_BA:C_BA + 6], 0.5, ALU.mult)
    kb.ts("dve", hbi, pc[:, C_BI:C_BI + 6], 0.5, ALU.mult)
    q25 = kb.alloc("q25", [128, 1])
    kb.memset("pool", q25, 0.25)
    xpad_f = kb.alloc("xpad", [128, T + 8], F32, nres=T // 1024)
    xc_f = kb.alloc("xc", [128, T], F32, nres=T // 1024)
    xcb_f = kb.alloc("xcb", [128, T], BF16, nres=T // 1024)
    ring_f = [[kb.alloc(f"lr{n}{i}", [128, 1024]) for i in range(4)] for n in ("A", "M", "B", "H", "G")]
    yab_f = [kb.alloc(f"yab{i}", [128, 1024], BF16) for i in range(2)]
    carry = kb.alloc("carry", [128, 2])
    for j3 in range(3):
        rows = slice(j3 * 128, (j3 + 1) * 128)
        for seq in ("ctx", "lat"):
            isc = seq == "ctx"
            tl = TX if isc else T
            tcl = min(1024, tl)
            nch = tl // tcl
            want_out = (not isc) or ctx_out
            xpad = Buf(xpad_f.ap[:, 0:tl + 8], xpad_f.res[0:nch])
            xc = Buf(xc_f.ap[:, 0:tl], xc_f.res[0:nch])
            xcb = Buf(xcb_f.ap[:, 0:tl], xcb_f.res[0:nch])
            hf = Buf(xpad.ap[:, 0:tl], xpad.res)
            ring = [[b[:, 0:tcl] for b in r] for r in ring_f]
            yab = [b[:, 0:tcl] for b in yab_f]
            P = g.P[seq]

            def cres(buf, c0, c1):
                return buf.res[max(c0, 0):min(c1, nch - 1) + 1]
            kb.memset("pool", Buf(xpad.ap[:, 0:2], xpad.res[0]), 0.0)
            kb.memset("pool", Buf(xpad.ap[:, tl + 2:tl + 4], xpad.res[nch - 1]), 0.0)
            for c in range(nch):
                kb.dma("sp", Buf(xpad.ap[:, 2 + c * tcl:2 + (c + 1) * tcl], xpad.res[c]),
                       Buf(P.ap[rows, c * tcl:(c + 1) * tcl], chunk_res(P, c * tcl, (c + 1) * tcl, RC)))
            for c in range(nch):
                cs = slice(c * tcl, (c + 1) * tcl)
                xcc = Buf(xc.ap[:, cs], xc.res[c])
                src = lambda k: Buf(xpad.ap[:, c * tcl + k:c * tcl + k + tcl], cres(xpad, c - 1, c + 1))
                kb.act(xcc, src(0), AF.Identity, bias=pc[:, C_CONVB + j3:C_CONVB + j3 + 1],
                       scale=pc[:, C_CONVW + j3 * 4:C_CONVW + j3 * 4 + 1])
                for k in range(1, 4):
                    kb.stt(xcc, src(k), pc[:, C_CONVW + j3 * 4 + k:C_CONVW + j3 * 4 + k + 1], xcc, ALU.mult, ALU.add)
                kb.copy("act", Buf(xcb.ap[:, cs], xcb.res[c]), xcc)
            for d in range(2):
                order = list(range(nch)) if d == 0 else list(range(nch - 1, -1, -1))
                ci = d * 3 + j3
                for p0 in range(0, nch, 2):
                    pair = order[p0:p0 + 2]
                    its = list(range(p0, p0 + len(pair)))
                    bufs = {it: tuple(r[it % 4] for r in ring) for it in its}
                    csl = {c: slice(c * tcl, (c + 1) * tcl) for c in pair}
                    xcc = {c: Buf(xc.ap[:, csl[c]], xc.res[c]) for c in pair}
                    pr, pi = {}, {}
                    for c in pair:
                        xbb = Buf(xcb.ap[:, csl[c]], xcb.res[c])
                        pr[c], pi[c] = kb.bank(2), kb.bank(2)
                        for hh in range(0, tcl, 512):
                            w_ = min(512, tcl - hh)
                            kb.mm(pr[c][:, hh:hh + w_], [(lw[:, (d * 2 + 0) * 3 + j3, :], xbb[:, hh:hh + w_])])
                            kb.mm(pi[c][:, hh:hh + w_], [(lw[:, (d * 2 + 1) * 3 + j3, :], xbb[:, hh:hh + w_])])
                    for it, c in zip(its, pair):
                        A, M, B, H, Gt = bufs[it]
                        kb.act(A, pr[c][:, 0:tcl], AF.Tanh, bias=hba[:, ci:ci + 1], scale=0.5)
                        kb.act(B, pi[c][:, 0:tcl], AF.Tanh, bias=hbi[:, ci:ci + 1], scale=0.5)
                    for it, c in zip(its, pair):
                        A, M, B, H, Gt = bufs[it]
                        kb.act(A, A, AF.Exp, bias=hcl[:, ci:ci + 1], scale=hcl[:, ci:ci + 1])
                        kb.tt("pool", M, A, A, ALU.mult)
                    for it, c in zip(its, pair):
                        A, M, B, H, Gt = bufs[it]
                        kb.act(M, M, AF.Sqrt, bias=q25, scale=-0.25)
                    if d == 1 and want_out:
                        for it, c in zip(its, pair):
                            A, M, B, H, Gt = bufs[it]
                            kb.dma("sp", Gt, Buf(P.ap[768 + j3 * 128:768 + (j3 + 1) * 128, csl[c]],
                                                 chunk_res(P, c * tcl, (c + 1) * tcl, RC)))
                            kb.act(Gt, Gt, AF.Gelu_apprx_tanh)
                    for it, c in zip(its, pair):
                        A, M, B, H, Gt = bufs[it]
                        cs = csl[c]
                        kb.stt(B, B, 1.0, xcc[c], ALU.add, ALU.mult)
                        kb.tt("dve" if it % 2 else "pool", B, B, M, ALU.mult)
                        if d == 0:
                            init = 0.0 if isc else g.h0[:, j3 * 2:j3 * 2 + 1]
                            if c > 0:
                                init = Buf(hf.ap[:, c * tcl - 1:c * tcl], hf.res[max(c - 2, 0):c])
                            kb.scan(Buf(hf.ap[:, cs], hf.res[max(c - 1, 0):c + 1]), A, B, init)
                            if isc and c == nch - 1:
                                kb.copy("dve", g.h0[:, j3 * 2:j3 * 2 + 1], Buf(hf.ap[:, tl - 1:tl], hf.res[max(c - 1, 0):c + 1]))
                        else:
                            init = (0.0 if isc else g.h0[:, j3 * 2 + 1:j3 * 2 + 2]) if it == 0 else carry[:, 0:1]
                            kb.scan(H[:, ::-1], A[:, ::-1], B[:, ::-1], init)
                            kb.copy("dve", carry[:, 0:1], H[:, 0:1])
                            if isc and c == 0:
                                kb.copy("dve", g.h0[:, j3 * 2 + 1:j3 * 2 + 2], H[:, 0:1])
                            if want_out:
                                kb.tt("dve", H, H, Buf(hf.ap[:, cs], hf.res[max(c - 1, 0):c + 1]), ALU.add)
                                kb.tt("pool", yab[it % 2], Gt, H, ALU.mult)
                                YM = g.YM[seq]
                                kb.dma("sp", Buf(YM.ap[rows, cs], chunk_res(YM, c * tcl, (c + 1) * tcl, RC)), yab[it % 2])
    kb.release(mk)


def stage3(kb, g, l, ctx_out):
    pc = g.pcol[l]
    mk = kb.mark()
    E = "dve"
    g.misc = kb.alloc("misc", [128, NMISC])
    kb.dma("sp", g.misc, g.d_misc)
    g.ident = g.misc[:, M_ID:M_ID + 128]
    g.bdmask = g.misc[:, M_BD:M_BD + 128]
    g.colmask = g.misc[:, M_CM:M_CM + 512].re("p (e c) -> p e c", e=4)
    g.ii = g.misc[:, M_II:M_II + 64]
    g.evmask = g.misc[:, M_EV:M_EV + 4]
    g.gmask = g.misc[:, M_GM:M_GM + 8]
    keep = []
    for d in range(2):
        keep.append([kb.alloc(f"s5pw{d}", [128, 17, 2, 24]), kb.alloc(f"s5sv{d}", [128, 10, 2, 24]),
                     kb.alloc(f"s5bst{d}", [128, 384]), kb.alloc(f"s5bsw{d}", [128, 384]),
                     kb.alloc(f"s5cst{d}", [128, 384]), kb.alloc(f"s5csw{d}", [128, 384])])
    diagD = kb.alloc("diagD", [128, 3, 128])
    mkt = kb.mark()
    s5b = kb.alloc("s5b", [128, 2, 384])
    kb.dma("sp", s5b, g.d_s5b[l].re("p (r c) -> p r c", r=2))
    halfpi = kb.alloc("halfpi", [128, 1])
    kb.memset("pool", halfpi, math.pi / 2)
    for j3 in range(3):
        kb.ts("dve", diagD[:, j3, :], g.ident, pc[:, C_S5D + j3:C_S5D + j3 + 1], ALU.mult)
    Pw, Sv, Bst, Bsw, Cst, Csw = [], [], [], [], [], []
    for d in range(2):
        prm = kb.alloc(f"s5prm{d}", [128, 72])
        kb.dma("sp", prm, g.d_s5p[l, d])
        s5c = kb.alloc(f"s5c{d}", [128, 2, 384])
        kb.dma("sp", s5c, g.d_s5c[l, d].re("p (r c) -> p r c", r=2))
        lr, li, ldt = prm[:, 0:24], prm[:, 24:48], prm[:, 48:72]
        tm = kb.alloc(f"s5tm{d}", [128, 12, 24])
        t = lambda i: tm[:, i, :]
        pw, sv, bst, bsw, cst, csw = keep[d]
        qw = kb.alloc(f"s5qw{d}", [128, 10, 2, 24])
        mul = lambda o, a, b: kb.tt(E, o, a, b, ALU.mult)
        add = lambda o, a, b: kb.tt(E, o, a, b, ALU.add)
        sub = lambda o, a, b: kb.tt(E, o, a, b, ALU.subtract)

        def cmul(outr, outi, ar, ai, br, bi):
            mul(t(10), ar, br); mul(t(11), ai, bi); sub(outr, t(10), t(11))
            mul(t(10), ar, bi); mul(t(11), ai, br); add(outi, t(10), t(11))
        kb.act(t(0), ldt, AF.Exp)
        mul(t(1), lr, t(0))
        mul(t(2), li, t(0))
        kb.act(t(3), t(1), AF.Exp, scale=1.0 / 16)
        kb.act(t(4), t(2), AF.Sin, scale=1.0 / 16)
        kb.act(t(5), t(2), AF.Sin, bias=halfpi, scale=1.0 / 16)
        mul(t(6), t(3), t(5)); mul(t(7), t(3), t(4))
        for it in range(4):
            cmul(t(8), t(9), t(6), t(7), t(6), t(7))
            kb.copy(E, t(6), t(8)); kb.copy(E, t(7), t(9))
        kb.memset(E, pw[:, 0, 0, :], 1.0); kb.memset(E, pw[:, 0, 1, :], 0.0)
        kb.copy(E, pw[:, 1, 0, :], t(6)); kb.copy(E, pw[:, 1, 1, :], t(7))
        for m in range(2, 17):
            cmul(pw[:, m, 0, :], pw[:, m, 1, :], pw[:, m - 1, 0, :], pw[:, m - 1, 1, :], pw[:, 1, 0, :], pw[:, 1, 1, :])
        kb.copy(E, qw[:, 0, 0, :], pw[:, 16, 0, :]); kb.copy(E, qw[:, 0, 1, :], pw[:, 16, 1, :])
        for i in range(1, 10):
            cmul(qw[:, i, 0, :], qw[:, i, 1, :], qw[:, i - 1, 0, :], qw[:, i - 1, 1, :], qw[:, i - 1, 0, :], qw[:, i - 1, 1, :])
        kb.copy(E, sv[:, :, 0, :], qw[:, :, 0, :]); kb.copy(E, sv[:, :, 1, :], qw[:, :, 1, :])
        kb.ts("dve", sv[64:128, :, 0, :], qw[64:128, :, 1, :], -1.0, ALU.mult)
        kb.copy("dve", sv[64:128, :, 1, :], qw[64:128, :, 0, :])
        mul(t(0), lr, lr); mul(t(1), li, li); add(t(0), t(0), t(1)); kb.recip(t(0), t(0))
        kb.ts("dve", t(1), pw[:, 1, 0, :], -1.0, ALU.add)
        mul(t(2), t(1), lr); mul(t(3), pw[:, 1, 1, :], li); add(t(2), t(2), t(3)); mul(t(2), t(2), t(0))
        mul(t(3), pw[:, 1, 1, :], lr); mul(t(4), t(1), li); sub(t(3), t(3), t(4)); mul(t(3), t(3), t(0))
        frb = t(2).re("p (g o) -> p g o", o=1).bc([128, 24, 16])
        fib = t(3).re("p (g o) -> p g o", o=1).bc([128, 24, 16])
        bb = kb.alloc(f"s5bb{d}", [128, 4, 384])
        v3 = lambda b: b.re("p (g k) -> p g k", k=16)
        Bre, Bim = v3(s5b[:, 0, :]), v3(s5b[:, 1, :])
        mul(v3(bb[:, 0, :]), Bre, frb); mul(v3(bb[:, 1, :]), Bim, fib); sub(bb[:, 0, :], bb[:, 0, :], bb[:, 1, :])
        mul(v3(bb[:, 1, :]), Bim, frb); mul(v3(bb[:, 2, :]), Bre, fib); add(bb[:, 1, :], bb[:, 1, :], bb[:, 2, :])
        kb.copy(E, bst, bb[:, 0, :]); kb.copy("dve", bst[64:128, :], bb[64:128, 1, :])
        kb.copy(E, bsw, bb[:, 0, :]); kb.ts("dve", bsw[0:64, :], bb[0:64, 1, :], -1.0, ALU.mult)
        kb.copy(E, cst, s5c[:, 0, :]); kb.ts("dve", cst[64:128, :], s5c[64:128, 1, :], -1.0, ALU.mult)
        kb.ts("dve", csw, s5c[:, 0, :], -1.0, ALU.mult); kb.ts("dve", csw[0:64, :], s5c[0:64, 1, :], -1.0, ALU.mult)
        Pw.append(pw); Sv.append(sv); Bst.append(bst); Bsw.append(bsw); Cst.append(cst); Csw.append(csw)

    kb.release(mkt)
    A = kb.alloc("s5A", [128, 16, 128])
    CRt = kb.alloc("s5CRt", [128, 16, 128])
    AT1 = kb.alloc("s5AT", [128, 16, 4, 128], BF16)
    AT = [AT1, AT1]
    CR = [kb.alloc(f"s5CR{d}", [128, 16, 4, 128], BF16) for d in range(2)]
    KF = [kb.alloc(f"s5KF{d}", [128, 16, 128], BF16) for d in range(2)]
    RD = kb.alloc("s5RD", [128, 8, 10, 128], BF16, nres=8)
    NZ = {"lat": T // 16, "ctx": TX // 16}
    Z1 = {s_: [kb.alloc(f"Z{s_}{q}", [128, NZ[s_] + 8]) for q in range(8)] for s_ in ("lat", "ctx")}
    Z = {s_: [Z1[s_], Z1[s_]] for s_ in ("lat", "ctx")}
    hfin = kb.alloc("s5hfin", [128, 8])
    Zb = {s_: [[kb.alloc(f"Zb{s_}{d}{q}", [128, NZ[s_] + 8], BF16) for q in range(8)] for d in range(2)] for s_ in ("lat", "ctx")}
    uB = {"lat": kb.alloc("uBl", [128, 16, T // 16], BF16), "ctx": kb.alloc("uBc", [128, 16, TX // 16], BF16)}
    zst = {"lat": kb.alloc("zstl", [128, T], BF16), "ctx": kb.alloc("zstc", [128, TX], BF16)}
    uld = [kb.alloc(f"uld{i}", [128, 1024]) for i in range(2)]
    v4 = lambda b: b.re("p m (g k) -> p m g k", k=16)
    bcm = lambda b, n: b.re("p (o g k) -> p o g k", o=1, k=16).bc([128, n, 8, 16])

    def prb(d, j3_, ri, m0, m1):
        return Pw[d][:, m0:m1, ri, j3_ * 8:j3_ * 8 + 8].re("p m (g o) -> p m g o", o=1).bc([128, m1 - m0, 8, 16])

    def gen_A(d, j3_):
        cols_ = slice(j3_ * 128, (j3_ + 1) * 128)
        kb.tt("dve", v4(A), bcm(Bst[d][:, cols_], 16), prb(d, j3_, 0, 0, 16), ALU.mult)
        kb.tt("pool", v4(CRt), bcm(Bsw[d][:, cols_], 16), prb(d, j3_, 1, 0, 16), ALU.mult)
        kb.tt("dve", A, A, CRt, ALU.add)

    def gen_AT(d):
        for m4 in range(0, 16, 4):
            bank = kb.bank()
            for m in range(m4, m4 + 4):
                kb.transpose(bank[:, (m - m4) * 128:(m - m4 + 1) * 128], A[:, m, :], g.ident)
            b3 = bank.re("p (m c) -> p m c", m=4)
            for v in range(4):
                if v % 2:
                    kb.act(AT[d][:, m4:m4 + 4, v, :], b3, AF.Copy, scale=g.evmask[:, v:v + 1])
                else:
                    kb.ts("dve", AT[d][:, m4:m4 + 4, v, :], b3, g.evmask[:, v:v + 1], ALU.mult)

    def gen_KF(d, j3_):
        cols_ = slice(j3_ * 128, (j3_ + 1) * 128)
        for m4 in range(0, 16, 4):
            bank2 = kb.bank()
            for m in range(m4, m4 + 4):
                kb.mm(bank2[:, (m - m4) * 128:(m - m4 + 1) * 128], [(A[:, m, :], Cst[d][:, cols_])])
            kb.tt("dve", KF[d][:, m4:m4 + 4, :], bank2.re("p (m c) -> p m c", m=4),
                  g.bdmask.re("p (o c) -> p o c", o=1).bc([128, 4, 128]), ALU.mult)
        if d == 0:
            kb.tt("pool", KF[0][:, 0, :], KF[0][:, 0, :], diagD[:, j3_, :], ALU.add)

    def gen_CR(d, j3_):
        cols_ = slice(j3_ * 128, (j3_ + 1) * 128)
        kb.tt("dve", v4(CRt), bcm(Cst[d][:, cols_], 16), prb(d, j3_, 0, 1, 17), ALU.mult)
        kb.tt("pool", v4(A), bcm(Csw[d][:, cols_], 16), prb(d, j3_, 1, 1, 17), ALU.mult)
        kb.tt("dve", CRt, CRt, A, ALU.add)
        for e in range(4):
            kb.tt("pool" if e % 2 else "dve", CR[d][:, :, e, :], CRt,
                  g.colmask[:, e, :].re("p (o c) -> p o c", o=1).bc([128, 16, 128]), ALU.mult)

    def gen_RD(d, j3_):
        for gl in range(8):
            rd = Buf(RD.ap[:, gl, :, :], RD.res[gl])
            for hh in range(2):
                svb = Sv[d][:, :, hh, j3_ * 8 + gl].re("p (i o) -> p i o", o=1).bc([128, 10, 64])
                kb.tt("pool" if hh else "dve", rd[:, :, hh * 64:(hh + 1) * 64],
                      g.ii.re("p (o c) -> p o c", o=1).bc([128, 10, 64]), svb, ALU.mult)

    def do_S(d):
        for s_ in ("ctx", "lat"):
            n = NZ[s_]
            for gl in range(8):
                q, e = gl // 4, gl % 4
                z = Z[s_][d][gl]
                bank = kb.bank()
                kb.mm(bank[:, 0:n], [(AT[d][64 * q:64 * q + 64, (15 - tau) if d == 0 else tau, e, :],
                                      uB[s_][64 * q:64 * q + 64, tau, :]) for tau in range(16)])
                dst = z[:, 1:n + 1] if d == 0 else z[:, n:0:-1]
                kb.copy("act", dst, bank[:, 0:n])

    def do_dbl(d):
        for s_ in ("ctx", "lat"):
            n = NZ[s_]
            W = n + 1 if s_ == "ctx" else n
            banks = []
            for gl in range(8):
                z, zb = Z[s_][d][gl], Zb[s_][d][gl]
                if s_ == "ctx":
                    kb.memset("pool", z[:, 0:1], 0.0)
                else:
                    kb.copy("pool", z[:, 0:1], hfin[:, gl:gl + 1])
                bank = kb.bank()
                banks.append(bank)
                kb.mm(bank[:, 0:W], [(g.ident, z[:, 0:W])])
                kb.copy("act" if gl % 2 else "dve", zb[:, 0:W], bank[:, 0:W])
            i, s_t = 0, 1
            while s_t < W:
                N = W - s_t
                for gl in range(8):
                    zb = Zb[s_][d][gl]
                    kb.mm(banks[gl][:, s_t:W], [(Buf(RD.ap[:, gl, i, :], RD.res[gl]), zb[:, 0:N])], first_start=False)
                    kb.copy("act" if (gl + i) % 2 else "dve", zb[:, s_t:W], banks[gl][:, s_t:W])
                i += 1
                s_t *= 2
            if s_ == "ctx":
                for gl in range(8):
                    kb.copy("act" if gl % 2 else "dve", hfin[:, gl:gl + 1], banks[gl][:, n:n + 1])
    seqs = ("ctx", "lat")
    for j3 in range(3):
        cols = slice(j3 * 128, (j3 + 1) * 128)
        rows = slice(384 + j3 * 128, 384 + (j3 + 1) * 128)
        it = 0
        for s_ in seqs:
            tl = TX if s_ == "ctx" else T
            for c0 in range(0, tl, 1024):
                w_ = min(1024, tl - c0)
                kb.dma("sp", uld[it % 2][:, 0:w_], Buf(g.P[s_].ap[rows, c0:c0 + w_], chunk_res(g.P[s_], c0, c0 + w_, RC)))
                kb.copy("act" if it % 2 else "dve", uB[s_][:, :, c0 // 16:(c0 + w_) // 16],
                        uld[it % 2][:, 0:w_].re("p (c t) -> p t c", t=16))
                it += 1
        if j3 == 0:
            gen_A(0, 0); gen_AT(0); gen_KF(0, 0); gen_CR(0, 0); gen_RD(0, 0)
        nxt = j3 + 1 < 3
        gen_CR(1, j3); gen_A(1, j3)
        do_S(0); gen_KF(1, j3); gen_AT(1); do_dbl(0); gen_RD(1, j3)
        if nxt:
            gen_A(0, j3 + 1)
        do_S(1)
        if nxt:
            gen_AT(0)
        do_dbl(1)
        if nxt:
            gen_RD(0, j3 + 1)
        for s_ in seqs:
            if s_ == "ctx" and not ctx_out:
                continue
            n = NZ[s_]
            tl = n * 16
            for tau in range(16):
                bank = kb.bank()
                prs = [(KF[0][:, j, :], uB[s_][:, tau - j, :]) for j in range(tau + 1)]
                prs += [(KF[1][:, j, :], uB[s_][:, tau + j, :]) for j in range(16 - tau)]
                for gl in range(8):
                    q, e = gl // 4, gl % 4
                    o = bank[64 * q:64 * q + 64, 0:n]
                    prs.append((CR[0][:, tau, e, 64 * q:64 * q + 64], Zb[s_][0][gl][:, 0:n], o))
                    prs.append((CR[1][:, 15 - tau, e, 64 * q:64 * q + 64], Zb[s_][1][gl][:, n - 1::-1], o))
                kb.mm(bank[:, 0:n], prs)
                kb.act(zst[s_][:, tau::16], bank[:, 0:n], AF.Gelu_apprx_tanh)
            ZS = g.ZS[s_]
            kb.dma("sp", Buf(ZS.ap[j3 * 128:(j3 + 1) * 128, :], ZS.res), zst[s_])
        if nxt:
            gen_KF(0, j3 + 1); gen_CR(0, j3 + 1)
    kb.release(mk)
    alloc_ffn_weights(kb, g)
    g.midmark = kb.mark()
    gw = kb.alloc("gluw", [128, 3, 384], BF16)
    for k in range(3):
        kb.dma("pool", gw[:, k, :], g.d_gluw[l][k * 128:(k + 1) * 128, :])
    g.plat = kb.alloc("plat", [128, PLAT.shape[0], 128], BF16)
    kb.dma("pool", g.plat, g.d_plat.re("n p c -> p n c"))
    g.pctx = kb.alloc("pctx", [128, PCTX.shape[0], 128], BF16)
    kb.dma("pool", g.pctx, g.d_pctx.re("n p c -> p n c"))
    g.poolw = kb.alloc("poolw", [128, 2, 128], BF16)
    kb.memset("pool", g.poolw, 0.0)
    for pt in range(2):
        for hh in range(2):
            kb.dma("pool", g.poolw[hh * 64:(hh + 1) * 64, pt, hh * 64:(hh + 1) * 64], g.d_poolw[l, 2 * pt + hh])
    load_ffn_weights(kb, g, l)
    seqs = ("ctx", "lat")
    mk = kb.mark()
    zc = [kb.alloc(f"gluz{i}", [128, 3, 512], BF16) for i in range(3)]
    sg = [kb.alloc(f"glus{i}", [128, 512]) for i in range(3)]
    yb = [kb.alloc(f"gluy{i}", [128, 3, 512], BF16) for i in range(2)]
    work = []
    for s_ in seqs:
        if s_ == "ctx" and not ctx_out:
            continue
        tl = TX if s_ == "ctx" else T
        tc = min(512, tl)
        work += [(s_, c, tc) for c in range(tl // tc)]

    def gload(i):
        s_, c, tc = work[i]
        ZS = g.ZS[s_]
        kb.dma("sp", zc[i % 3][:, :, 0:tc], Buf(ZS.ap[:, c * tc:(c + 1) * tc].rearrange("(k p) t -> p k t", p=128),
                                               chunk_res(ZS, c * tc, (c + 1) * tc, RC)))
    for i in range(min(2, len(work))):
        gload(i)
    for it, (s_, c, tc) in enumerate(work):
        if it + 2 < len(work):
            gload(it + 2)
        zz, yy = zc[it % 3], yb[it % 2]
        YM = g.YM[s_]
        for mo in range(3):
            bank = kb.bank()
            kb.mm(bank[:, 0:tc], [(gw[:, k, mo * 128:(mo + 1) * 128], zz[:, k, 0:tc]) for k in range(3)])
            kb.act(sg[mo][:, 0:tc], bank[:, 0:tc], AF.Sigmoid, bias=pc[:, C_GLUB + mo:C_GLUB + mo + 1])
            kb.tt("dve", yy[:, mo, 0:tc], zz[:, mo, 0:tc], sg[mo][:, 0:tc], ALU.mult)
        kb.dma("sp", Buf(YM.ap[384:768, c * tc:(c + 1) * tc].rearrange("(k p) t -> p k t", p=128),
                          chunk_res(YM, c * tc, (c + 1) * tc, RC)), yy[:, :, 0:tc])
    kb.release(mk)


def stage4(kb, g, l, ctx_out):
    pc = g.pcol[l]
    mk = kb.mark()
    plat, pctx, pw = g.plat, g.pctx, g.poolw
    rinv = kb.alloc("rinv", [128, 2, 128])
    kb.dma("sp", rinv, g.d_rinv.re("t p r -> p t r"))
    sb = kb.alloc("poolsb", [128, 2])
    kb.tt("dve", sb, pc[:, C_POOLB:C_POOLB + 2], pc[:, C_POOLS:C_POOLS + 2], ALU.mult)
    vbuf = [kb.alloc(f"poolv{i}", [128, 512]) for i in range(3)]
    tbuf = [kb.alloc(f"poolt{i}", [128, 512]) for i in range(2)]
    mbuf = [kb.alloc(f"poolm{i}", [128, 512], BF16) for i in range(2)]
    ybuf = [kb.alloc(f"pooly{i}", [128, 512], BF16) for i in range(2)]
    it = 0
    for seq in ("ctx", "lat"):
        isc = seq == "ctx"
        if isc and not ctx_out:
            continue
        tl = TX if isc else T
        nb = tl // 128
        tc = min(512, tl)
        P, YM = g.P[seq], g.YM[seq]
        nhalf = 1 if isc else 2
        bper = nb // nhalf
        for half in range(nhalf):
          mk2 = kb.mark()
          blo, bhi = max(0, half * bper - 4), min(nb, (half + 1) * bper + 4)
          vt = kb.alloc("vt", [128, bhi - blo, 256], BF16)
          kb.dma("sp", vt, Buf(g.VT[seq].ap[:, blo:bhi, :], g.VT[seq].res))
          cper = (tl // tc) // nhalf
          wl = [(c, pt) for c in range(half * cper, (half + 1) * cper) for pt in range(2)]

          def vload(i, base):
              c_, pt_ = wl[i]
              kb.dma("sp", vbuf[(base + i) % 3][:, 0:tc],
                     Buf(P.ap[1152 + pt_ * 128:1152 + (pt_ + 1) * 128, c_ * tc:(c_ + 1) * tc],
                         chunk_res(P, c_ * tc, (c_ + 1) * tc, RC)))
          base = it
          for i in range(min(2, len(wl))):
              vload(i, base)
          for wi, (c, pt) in enumerate(wl):
              if True:
                  if wi + 2 < len(wl):
                      vload(wi + 2, base)
                  v, t_, m_, y_ = vbuf[it % 3], tbuf[it % 2], mbuf[it % 2], ybuf[it % 2]
                  it += 1
                  bank = kb.bank()
                  for b4 in range(tc // 128):
                      B = c * (tc // 128) + b4
                      for gl in range(2):
                          gg = 2 * pt + gl
                          prs = []
                          for dlt in range(-4, 5):
                              if not 0 <= B + dlt < nb:
                                  continue
                              if isc:
                                  idx = CTXIDX.get((gg, B, dlt))
                                  mat = None if idx is None else pctx[:, idx, :]
                              else:
                                  idx = LATIDX.get((gg, dlt))
                                  mat = None if idx is None else plat[:, idx, :]
                              if mat is not None:
                                  prs.append((vt[:, B + dlt - blo, gg * 64:(gg + 1) * 64], mat))
                          kb.mm(bank[gl * 64:(gl + 1) * 64, b4 * 128:(b4 + 1) * 128], prs)
                  if isc:
                      kb.tt("dve", m_[:, 0:tc], bank[:, 0:tc], v[:, 0:tc], ALU.subtract)
                  else:
                      r0 = c * (tc // 64)
                      kb.tt("dve", t_.re("p (r c) -> p r c", c=64), bank.re("p (r c) -> p r c", c=64),
                            rinv[:, pt, r0:r0 + tc // 64].re("p (r o) -> p r o", o=1).bc([128, tc // 64, 64]), ALU.mult)
                      kb.tt("dve", m_[:, 0:tc], t_[:, 0:tc], v[:, 0:tc], ALU.subtract)
                  bank2 = kb.bank()
                  kb.mm(bank2[:, 0:tc], [(pw[:, pt, :], m_[:, 0:tc])])
                  kb.act(y_[:, 0:tc], bank2[:, 0:tc], AF.Identity, bias=sb[:, pt:pt + 1], scale=pc[:, C_POOLS + pt:C_POOLS + pt + 1])
                  kb.dma("sp", Buf(YM.ap[768 + pt * 128:768 + (pt + 1) * 128, c * tc:(c + 1) * tc],
                                   chunk_res(YM, c * tc, (c + 1) * tc, RC)), y_[:, 0:tc])
          kb.release(mk2)
    kb.release(mk)


def alloc_ffn_weights(kb, g):
    g.ffnw = (kb.alloc("wo", [128, 8, D], BF16), kb.alloc("wg", [128, 8, DFF], BF16),
              kb.alloc("wu", [128, 8, DFF], BF16), kb.alloc("wd", [128, 22, D], BF16))


def load_ffn_weights(kb, g, l):
    wo, wg, wu, wd = g.ffnw
    for k in range(8):
        kb.dma("pool", wo[:, k, :], g.d_wout[l][k * 128:(k + 1) * 128, :])
    for k in range(8):
        kb.dma("pool", wg[:, k, :], g.d_wg[l][k * 128:(k + 1) * 128, :], max_dma_last_dim=4096)
        kb.dma("pool", wu[:, k, :], g.d_wu[l][k * 128:(k + 1) * 128, :], max_dma_last_dim=4096)
    for f in range(22):
        kb.dma("pool", wd[:, f, :], g.d_wd[l][f * 128:(f + 1) * 128, :])


def stage56(kb, g, l, seqs, last):
    pc = g.pcol[l]
    mk = kb.mark()
    tc = 256
    wo, wg, wu, wd = g.ffnw
    NX = 3
    xs = [kb.alloc(f"fxs{i}", [128, 8, tc]) for i in range(NX)]
    ym = kb.alloc("fym", [128, 8, tc], BF16)
    hx = [kb.alloc(f"fhx{i}", [128, 8, tc], BF16) for i in range(2)]
    sq1 = kb.alloc("fsq", [128, 8, tc], BF16)
    sq = [sq1, sq1]
    rs1 = kb.alloc("frs", [128, tc])
    rs = [rs1, rs1]
    xn = [kb.alloc(f"fxn{i}", [128, tc]) for i in range(2)]
    hmid = kb.alloc("fhmid", [128, 22, tc], BF16)
    sg = [kb.alloc(f"fsg{i}", [128, tc]) for i in range(2)]
    items = []
    for seq in seqs:
        tl = TX if seq == "ctx" else T
        items += [(seq, c) for c in range(tl // tc)]
    n_it = len(items)

    def jof(i):
        return 1 if items[i][0] == "ctx" else 0

    def load_x(i):
        seq, c = items[i]
        X = g.xcur[seq]
        kb.dma("sp", xs[i % NX], Buf(X.ap[:, c * tc:(c + 1) * tc].rearrange("(k p) t -> p k t", p=128),
                                    chunk_res(X, c * tc, (c + 1) * tc, RC)))

    def load_y(i):
        seq, c = items[i]
        YM = g.YM[seq]
        kb.dma("sp", ym, Buf(YM.ap[:, c * tc:(c + 1) * tc].rearrange("(k p) t -> p k t", p=128),
                             chunk_res(YM, c * tc, (c + 1) * tc, RC)))

    def wout(i):
        x, j = xs[i % NX], jof(i)
        for m in range(8):
            bank = kb.bank()
            kb.mm(bank[:, 0:tc], [(wo[:, k, m * 128:(m + 1) * 128], ym[:, k, :]) for k in range(8)])
            kb.stt(x[:, m, :], bank[:, 0:tc], modcol(g, l, 2, m, j), x[:, m, :], ALU.mult, ALU.add)

    def norm_a(i):
        kb.act(sq[i % 2], xs[i % NX], AF.Square)

    def norm_b(i, final=False):
        x, j, r_ = xs[i % NX], jof(i), rs[i % 2]
        bank = kb.bank()
        kb.mm(bank[:, 0:tc], [(g.ones, sq[i % 2][:, k, :]) for k in range(8)])
        kb.act(r_, bank[:, 0:tc], AF.Sqrt, bias=g.epsc, scale=1.0 / D)
        kb.recip(r_, r_)
        if final:
            for m in range(8):
                kb.tt("dve" if m % 2 else "pool", x[:, m, :], x[:, m, :], r_, ALU.mult)
                kb.act(x[:, m, :], x[:, m, :], AF.Copy, scale=pc[:, C_FING + m:C_FING + m + 1])
            return
        for k in range(8):
            kb.tt("dve" if k % 2 else "pool", xn[k % 2], x[:, k, :], r_, ALU.mult)
            kb.act(hx[i % 2][:, k, :], xn[k % 2], AF.Identity, bias=modcol(g, l, 3, k, j), scale=g.gs2[l][:, k, j:j + 1])

    def gateup(i):
        h = hx[i % 2]
        for f in range(22):
            bg, bu = kb.bank(), kb.bank()
            kb.mm(bg[:, 0:tc], [(wg[:, k, f * 128:(f + 1) * 128], h[:, k, :]) for k in range(8)])
            kb.mm(bu[:, 0:tc], [(wu[:, k, f * 128:(f + 1) * 128], h[:, k, :]) for k in range(8)])
            kb.act(sg[f % 2], bg[:, 0:tc], AF.Silu)
            kb.tt("dve", hmid[:, f, :], sg[f % 2], bu[:, 0:tc], ALU.mult)

    def down(i):
        x, j = xs[i % NX], jof(i)
        seq, c = items[i]
        for m in range(8):
            bank = kb.bank()
            kb.mm(bank[:, 0:tc], [(wd[:, f, m * 128:(m + 1) * 128], hmid[:, f, :]) for f in range(22)])
            kb.stt(x[:, m, :], bank[:, 0:tc], modcol(g, l, 5, m, j), x[:, m, :], ALU.mult, ALU.add)
        if last:
            norm_a(i)
            norm_b(i, final=True)
            dst = g.d_out
        else:
            dst = g.X1[seq]
        kb.dma("sp", Buf(dst.ap[:, c * tc:(c + 1) * tc].rearrange("(k p) t -> p k t", p=128),
                         chunk_res(dst, c * tc, (c + 1) * tc, RC)), x)
    load_x(0)
    load_y(0)
    if n_it > 1:
        load_x(1)
    wout(0)
    if n_it > 1:
        load_y(1)
    norm_a(0)
    norm_b(0)
    for i in range(n_it):
        if i + 2 < n_it:
            load_x(i + 2)
        if i + 1 < n_it:
            wout(i + 1)
            if i + 2 < n_it:
                load_y(i + 2)
            norm_a(i + 1)
        gateup(i)
        if i + 1 < n_it:
            norm_b(i + 1)
        down(i)
    kb.release(mk)
```

```python
import math
from contextlib import ExitStack

import numpy as np
import concourse.bass as bass
import concourse.mybir as mybir
from concourse.bass_utils import run_bass_kernel_spmd

F32 = mybir.dt.float32
BF16 = mybir.dt.bfloat16
AF = mybir.ActivationFunctionType
ALU = mybir.AluOpType

D = 1024
T = 8192
TX = 256
DIN = 1408
DFF = 2816
NL = 2
NCOL = 116
ENG = ("pe", "act", "dve", "pool", "sp")
SBW = 53184

C_N1G, C_N2G, C_BMOD, C_CONVW, C_CONVB, C_BA, C_BI, C_LAM, C_S5D, C_GLUB, C_POOLB, C_POOLS, C_FING = (
    0, 8, 16, 64, 76, 79, 85, 91, 97, 100, 103, 105, 107)

POOL_HALF = (1, 2, 4, 8)


class Res:
    __slots__ = ("w", "r", "name")

    def __init__(self, name=""):
        self.w = None
        self.r = {}
        self.name = name


class Buf:
    def __init__(self, ap, res):
        self.ap = ap
        self.res = list(res) if isinstance(res, (list, tuple)) else [res]

    def __getitem__(self, idx):
        return Buf(self.ap[idx], self.res)

    def re(self, pat, **kw):
        return Buf(self.ap.rearrange(pat, **kw), self.res)

    def bc(self, shape):
        return Buf(self.ap.broadcast_to(list(shape)), self.res)

    def wr(self, res):
        return Buf(self.ap, res)


def _aps(x):
    return x.ap if isinstance(x, Buf) else x


class KB:
    def __init__(self, nc, big, psums):
        self.nc = nc
        self.big = big
        self.streams = {e: [] for e in ENG}
        self.cnt = {e: 0 for e in ENG}
        self.seen = {e: {} for e in ENG}
        self.ndma = 40
        self.dma_val = [0] * self.ndma
        self.dma_rr = 0
        self.sb_off = 0
        self.sb_peak = 0
        self.psums = psums
        self.bank_res = [Res(f"bank{i}") for i in range(8)]
        self.bank_rr = 0

    def alloc(self, name, shape, dtype=F32, nres=1):
        p = shape[0]
        n = int(np.prod(shape[1:]))
        isz = 2 if dtype == BF16 else 4
        words = (n * isz + 3) // 4
        words = (words + 7) // 8 * 8
        off = self.sb_off
        self.sb_off += words
        self.sb_peak = max(self.sb_peak, self.sb_off)
        assert self.sb_off <= SBW, f"SBUF arena overflow at {name}: {self.sb_off}"
        ap = self.big[0:p, off:off + words]
        if dtype == BF16:
            ap = ap.bitcast(BF16)
        ap = ap[:, 0:n]
        if len(shape) == 3:
            ap = ap.rearrange("p (a b) -> p a b", a=shape[1])
        elif len(shape) == 4:
            ap = ap.rearrange("p (a b c) -> p a b c", a=shape[1], b=shape[2])
        if nres == 1:
            return Buf(ap, Res(name))
        return Buf(ap, [Res(f"{name}{i}") for i in range(nres)])

    def mark(self):
        return self.sb_off

    def release(self, mark):
        self.barrier()
        self.sb_off = mark

    def bank(self, n=1):
        if n == 2 and self.bank_rr % 2:
            self.bank_rr += 1
        k = self.bank_rr % 8
        self.bank_rr += n
        ap = self.psums[k // 2]
        if n == 1:
            return Buf(ap[:, (k % 2) * 512:(k % 2) * 512 + 512], self.bank_res[k])
        return Buf(ap[:, :], [self.bank_res[k], self.bank_res[k + 1]])

    def _need(self, eng, ev):
        if ev is None:
            return
        key, val = ev
        if key == eng and eng == "pe":
            return
        if self.seen[eng].get(key, 0) >= val:
            return
        self.seen[eng][key] = val
        self.streams[eng].append(("w", key, val))

    def _deps(self, eng, reads, writes):
        for r in reads:
            self._need(eng, r.w)
        for w in writes:
            self._need(eng, w.w)
            for ev in w.r.values():
                self._need(eng, ev)

    def _done(self, key, ev, reads, writes):
        for r in reads:
            r.r[key] = ev
        for w in writes:
            w.w = ev
            w.r = {}

    def op(self, eng, fn, ins=(), outs=()):
        reads = [r for b in ins if isinstance(b, Buf) for r in b.res]
        writes = [r for b in outs if isinstance(b, Buf) for r in b.res]
        self._deps(eng, reads, writes)
        self.cnt[eng] += 1
        ev = (eng, self.cnt[eng])
        self.streams[eng].append(("i", fn, True))
        self._done(eng, ev, reads, writes)

    def mm(self, out, pairs, first_start=True):
        reads = [r for pr in pairs for b in pr[:2] for r in b.res]
        writes = list(out.res)
        self._deps("pe", reads, writes)
        n = len(pairs)
        for i, pr in enumerate(pairs):
            l, rr = pr[0], pr[1]
            o = pr[2].ap if len(pr) > 2 else out.ap
            self.streams["pe"].append(
                ("i", (lambda e, o=o, l=l.ap, r=rr.ap, st=(i == 0 and first_start), sp=(i == n - 1):
                       e.matmul(o, l, r, start=st, stop=sp)), i == n - 1))
        self.cnt["pe"] += 1
        self._done("pe", ("pe", self.cnt["pe"]), reads, writes)

    def transpose(self, out, in_, ident):
        self.op("pe", lambda e, o=out.ap, i=in_.ap, d=ident.ap: e.transpose(o, i, d), [in_, ident], [out])

    def dma(self, q, out, in_, **kw):
        i = self.dma_rr
        self.dma_rr = (i + 1) % self.ndma
        key = ("d", i)
        if self.dma_val[i] > 0:
            self._need(q, (key, self.dma_val[i]))
        reads = list(in_.res)
        writes = list(out.res)
        self._deps(q, reads, writes)
        self.dma_val[i] += 16
        ev = (key, self.dma_val[i])
        self.streams[q].append(("d", out.ap, in_.ap, i, kw))
        self._done(key, ev, reads, writes)

    def barrier(self):
        evs = [(e, self.cnt[e]) for e in ENG if self.cnt[e] > 0]
        evs += [(("d", i), v) for i, v in enumerate(self.dma_val) if v > 0]
        for e in ENG:
            for ev in evs:
                self._need(e, ev)

    def act(self, out, in_, func, bias=None, scale=None, eng="act"):
        kw = {}
        ins = [in_]
        if bias is not None:
            kw["bias"] = _aps(bias)
            ins.append(bias)
        if scale is not None:
            kw["scale"] = _aps(scale)
            ins.append(scale)
        self.op(eng, lambda e, o=out.ap, i=in_.ap, f=func, kw=kw: e.activation(out=o, in_=i, func=f, **kw), ins, [out])

    def tt(self, eng, out, a, b, op):
        self.op(eng, lambda e, o=out.ap, a_=a.ap, b_=b.ap, op=op: e.tensor_tensor(out=o, in0=a_, in1=b_, op=op), [a, b], [out])

    def ts(self, eng, out, a, s1, op0, s2=None, op1=None):
        def fn(e, o=out.ap, a_=a.ap, s1=_aps(s1), s2=_aps(s2), op0=op0, op1=op1):
            if op1 is None:
                return e.tensor_scalar(out=o, in0=a_, scalar1=s1, scalar2=None, op0=op0)
            return e.tensor_scalar(out=o, in0=a_, scalar1=s1, scalar2=s2, op0=op0, op1=op1)
        self.op(eng, fn, [a, s1, s2], [out])

    def stt(self, out, a, s, b, op0, op1):
        self.op("dve", lambda e, o=out.ap, a_=a.ap, s_=_aps(s), b_=b.ap, op0=op0, op1=op1:
                e.scalar_tensor_tensor(out=o, in0=a_, scalar=s_, in1=b_, op0=op0, op1=op1), [a, s, b], [out])

    def copy(self, eng, out, in_):
        if eng == "act":
            self.act(out, in_, AF.Copy)
        else:
            self.op(eng, lambda e, o=out.ap, i=in_.ap: e.tensor_copy(out=o, in_=i), [in_], [out])

    def memset(self, eng, out, val):
        self.op(eng, lambda e, o=out.ap, v=val: e.memset(o, v), [], [out])

    def scan(self, out, a, b, init):
        self.op("dve", lambda e, o=out.ap, a_=a.ap, b_=b.ap, i_=_aps(init):
                e.tensor_tensor_scan(out=o, data0=a_, data1=b_, initial=i_, op0=ALU.mult, op1=ALU.add),
                [a, b, init], [out])

    def recip(self, out, in_):
        self.op("dve", lambda e, o=out.ap, i=in_.ap: e.reciprocal(out=o, in_=i), [in_], [out])

    def replay(self, sems, dsems):
        nc = self.nc
        engs = {"pe": "tensor", "act": "scalar", "dve": "vector", "pool": "gpsimd", "sp": "sync"}

        def semh(key):
            return dsems[key[1]] if isinstance(key, tuple) else sems[key]

        with nc.Block() as block:
            for name in ENG:
                stream = self.streams[name]

                def body(e, stream=stream, name=name):
                    for it in stream:
                        if it[0] == "w":
                            e.wait_ge(semh(it[1]), it[2])
                        elif it[0] == "i":
                            ins = it[1](e)
                            if it[2]:
                                ins.then_inc(sems[name], 1)
                        else:
                            _, o, i, k, kw = it
                            e.dma_start(out=o, in_=i, **kw).then_inc(dsems[k], 16)
                getattr(block, engs[name])(body)


def _colT(v):
    return np.ascontiguousarray(np.asarray(v, np.float32).reshape(-1, 128).T)


def _pool_consts():
    lat, latidx = [], {}
    for g, half in enumerate(POOL_HALF):
        for dlt in range(-4, 5):
            m = np.zeros((128, 128), np.float32)
            for ri in range(2):
                for ro in range(2):
                    rel = 2 * dlt + ri
                    if not (ro - half <= rel < ro + half):
                        continue
                    for co in range(64):
                        lo, hi = max(co - half, 0), min(co + half, 64)
                        m[ri * 64 + lo:ri * 64 + hi, ro * 64 + co] = 1.0 / (hi - lo)
            if m.any():
                latidx[(g, dlt)] = len(lat)
                lat.append(m)
    ctx, ctxidx = [], {}
    for g, half in enumerate(POOL_HALF):
        for B in range(2):
            for dlt in (-1, 0, 1):
                if not 0 <= B + dlt < 2:
                    continue
                m = np.zeros((128, 128), np.float32)
                for o in range(128):
                    to = 128 * B + o
                    lo, hi = max(to - half, 0), min(to + half, TX)
                    for ti in range(lo, hi):
                        if 128 * (B + dlt) <= ti < 128 * (B + dlt + 1):
                            m[ti - 128 * (B + dlt), o] = 1.0 / (hi - lo)
                if m.any():
                    ctxidx[(g, B, dlt)] = len(ctx)
                    ctx.append(m)
    rinv = np.zeros((2, 128, 128), np.float32)
    for pt in range(2):
        for p in range(128):
            half = POOL_HALF[2 * pt + p // 64]
            for r in range(128):
                rinv[pt, p, r] = 1.0 / (min(r + half, 128) - max(r - half, 0))
    return np.stack(lat), latidx, np.stack(ctx), ctxidx, rinv


PLAT, LATIDX, PCTX, CTXIDX, RINV = _pool_consts()


def _consts():
    p = np.arange(128)
    ident = np.eye(128, dtype=np.float32)
    bdmask = (p[:, None] // 16 == p[None, :] // 16).astype(np.float32)
    qmask = np.stack([((p // 16) % 4 == v) for v in range(4)], axis=1).astype(np.float32)
    colmask = np.broadcast_to(qmask.T[None, :, :], (128, 4, 128)).astype(np.float32).copy()
    ii = np.concatenate([np.eye(64, dtype=np.float32)] * 2, axis=0)
    gmask = np.stack([((p // 16) == v) for v in range(8)], axis=1).astype(np.float32)
    misc = np.concatenate([ident, bdmask, colmask.reshape(128, 512), ii, qmask, gmask], axis=1)
    return np.ascontiguousarray(misc)


MISC = _consts()
M_ID, M_BD, M_CM, M_II, M_EV = 0, 128, 256, 768, 832
NMISC = 844
M_GM = 836


def pack_shared(inp):
    f = lambda k: np.asarray(inp[k], np.float32)
    sh = {}
    pcol = np.zeros((NL, 128, NCOL), np.float32)
    for l in range(NL):
        pc = pcol[l]
        pc[:, C_N1G:C_N1G + 8] = _colT(f("norm1_g")[l])
        pc[:, C_N2G:C_N2G + 8] = _colT(f("norm2_g")[l])
        pc[:, C_BMOD:C_BMOD + 48] = _colT(f("b_mod")[l])
        for j in range(3):
            sl = slice(j * 128, (j + 1) * 128)
            for k in range(4):
                pc[:, C_CONVW + j * 4 + k] = f("lru_conv_w")[l, k, sl]
            pc[:, C_CONVB + j] = f("lru_conv_b")[l, sl]
            for d in range(2):
                pc[:, C_BA + d * 3 + j] = f("lru_ba")[l, d, sl]
                pc[:, C_BI + d * 3 + j] = f("lru_bi")[l, d, sl]
                pc[:, C_LAM + d * 3 + j] = f("lru_lambda")[l, d, sl]
            pc[:, C_S5D + j] = f("s5_d")[l, sl]
            pc[:, C_GLUB + j] = f("s5_glu_b")[l, sl]
        for j in range(2):
            sl = slice(j * 128, (j + 1) * 128)
            pc[:, C_POOLB + j] = f("pool_b")[l, sl]
            pc[:, C_POOLS + j] = f("pool_scale")[l, sl]
        pc[:, C_FING:C_FING + 8] = _colT(f("final_g"))
    sh["pcol"] = pcol
    for k in ("w_mod", "w_in", "w_out", "ffn_w_gate", "ffn_w_up", "ffn_w_down", "pool_w"):
        sh[k] = np.ascontiguousarray(f(k))
    sh["glu_w"] = np.ascontiguousarray(f("s5_glu_w"))
    sh["lruw"] = np.ascontiguousarray(np.stack([f("lru_wa"), f("lru_wi")], axis=2))
    s5p = np.zeros((NL, 2, 128, 72), np.float32)
    s5b = np.zeros((NL, 128, 768), np.float32)
    s5c = np.zeros((NL, 2, 128, 768), np.float32)
    for l in range(NL):
        s5b[l, :, 0:384] = np.tile(f("s5_b_re")[l].transpose(1, 0, 2).reshape(64, 384), (2, 1))
        s5b[l, :, 384:768] = np.tile(f("s5_b_im")[l].transpose(1, 0, 2).reshape(64, 384), (2, 1))
        for d in range(2):
            s5p[l, d, :, 0:24] = np.tile(f("s5_lambda_re")[l, d].T, (2, 1))
            s5p[l, d, :, 24:48] = np.tile(f("s5_lambda_im")[l, d].T, (2, 1))
            s5p[l, d, :, 48:72] = np.broadcast_to(f("s5_log_dt")[l, d][None, :], (128, 24))
            s5c[l, d, :, 0:384] = np.tile(f("s5_c_re")[l, d].transpose(2, 0, 1).reshape(64, 384), (2, 1))
            s5c[l, d, :, 384:768] = np.tile(f("s5_c_im")[l, d].transpose(2, 0, 1).reshape(64, 384), (2, 1))
    sh["s5p"], sh["s5b"], sh["s5c"] = s5p, s5b, s5c
    sh["misc"] = MISC
    sh["plat"] = PLAT
    sh["pctx"] = PCTX
    sh["rinv"] = RINV
    return sh


def pack_core(inp, b):
    x = np.asarray(inp["x"], np.float32)[b]
    ctx = np.asarray(inp["ctx"], np.float32)[b]
    cc = np.stack([_colT(np.asarray(inp["c"], np.float32)[b]), _colT(np.asarray(inp["c_ctx"], np.float32))], axis=2)
    return {"xT": np.ascontiguousarray(x.T), "ctxT": np.ascontiguousarray(ctx.T),
            "cc": np.ascontiguousarray(cc.reshape(128, 16))}


class G:
    pass


def dram_in(nc, name, shape, dtype=F32):
    return Buf(nc.dram_tensor(name, list(shape), dtype, kind="ExternalInput").ap(), Res(name))


def dram_tmp(nc, name, shape, dtype, nres, dbg):
    kind = "ExternalOutput" if name in dbg else "Internal"
    return Buf(nc.dram_tensor(name, list(shape), dtype, kind=kind).ap(), [Res(f"{name}{i}") for i in range(nres)])


def chunk_res(buf, c0, c1, csz):
    return buf.res[c0 // csz:(c1 - 1) // csz + 1]


RC = 512


def colsl(buf, rows, c0, c1):
    return Buf(buf.ap[rows, c0:c1], chunk_res(buf, c0, c1, RC))


def stage0(kb, g):
    nc = kb.nc
    g.ones = kb.alloc("ones", [128, 128], BF16)
    kb.memset("pool", g.ones, 1.0)
    g.epsc = kb.alloc("epsc", [128, 1])
    kb.memset("pool", g.epsc, 1e-6)
    g.pcol = []
    for l in range(NL):
        pc = kb.alloc(f"pcol{l}", [128, NCOL])
        kb.dma("sp", pc, g.d_pcol[l])
        g.pcol.append(pc)
    cc = kb.alloc("cc", [128, 16])
    kb.dma("sp", cc, g.d_cc)
    scc = kb.alloc("scc", [128, 16])
    kb.act(scc, cc, AF.Silu)
    scc3 = scc.re("p (k j) -> p k j", j=2)
    g.mod = [kb.alloc(f"mod{l}", [128, 48, 2]) for l in range(NL)]
    mk = kb.mark()
    idn2 = kb.alloc("idn2", [128, 2])
    kb.dma("sp", idn2[0:2, :], g.d_misc[0:2, 0:2])
    modrow = kb.alloc("modrow", [128, 6 * D])
    wm = [kb.alloc(f"wm{i}", [128, 8, 512]) for i in range(2)]
    it = 0
    for l in range(NL):
        for cg in range(12):
            w = wm[it % 2]
            it += 1
            kb.dma("sp", w, g.d_wmod[l][:, cg * 512:(cg + 1) * 512].re("(k p) c -> p k c", p=128))
            bank = kb.bank()
            kb.mm(bank[0:2, :], [(scc3[:, k, :], w[:, k, :]) for k in range(8)])
            kb.copy("act" if cg % 2 else "dve", modrow[0:2, cg * 512:(cg + 1) * 512], bank[0:2, :])
        bank = kb.bank()
        for m in range(48):
            kb.transpose(bank[:, 2 * m:2 * m + 2], modrow[0:2, m * 128:(m + 1) * 128], idn2[0:2, 0:2])
        kb.tt("dve", g.mod[l], bank[:, 0:96].re("p (m j) -> p m j", j=2),
              g.pcol[l][:, C_BMOD:C_BMOD + 48].re("p (m o) -> p m o", o=1).bc([128, 48, 2]), ALU.add)
    kb.release(mk)
    g.gs1, g.gs2 = [], []
    for l in range(NL):
        for which, cg0, coln, lst in ((1, 8, C_N1G, g.gs1), (2, 32, C_N2G, g.gs2)):
            t = kb.alloc(f"gs{which}_{l}", [128, 8, 2])
            kb.ts("dve", t, g.mod[l][:, cg0:cg0 + 8, :], 1.0, ALU.add)
            kb.tt("dve", t, t, g.pcol[l][:, coln:coln + 8].re("p (m o) -> p m o", o=1).bc([128, 8, 2]), ALU.mult)
            lst.append(t)


def modcol(g, l, which, m, j):
    return g.mod[l][:, which * 8 + m, j:j + 1]


def rmsnorm_mod(kb, g, xs, tc, gs, l, which_sh, j, hx, tmp):
    sq, rs, xn = tmp["sq"], tmp["rs"], tmp["xn"]
    if not tmp.get("presquared"):
        kb.act(sq[:, :, 0:tc], xs[:, :, 0:tc], AF.Square)
    bank = kb.bank()
    kb.mm(bank[:, 0:tc], [(g.ones, sq[:, k, 0:tc]) for k in range(8)])
    kb.act(rs[:, 0:tc], bank[:, 0:tc], AF.Sqrt, bias=g.epsc, scale=1.0 / D)
    kb.recip(rs[:, 0:tc], rs[:, 0:tc])
    kb.tt("dve", xn[:, :, 0:tc], xs[:, :, 0:tc], rs[:, 0:tc].re("p (o t) -> p o t", o=1).bc([128, 8, tc]), ALU.mult)
    for k in range(8):
        kb.act(hx[:, k, 0:tc], xn[:, k, 0:tc], AF.Identity, bias=modcol(g, l, which_sh, k, j), scale=gs[:, k, j:j + 1])


def stage1(kb, g, l, seq):
    j = 1 if seq == "ctx" else 0
    tlen = TX if j else T
    tc = min(512, tlen)
    src = g.xcur[seq]
    dst = g.P[seq]
    VT = g.VT[seq]
    mk = kb.mark()
    vts = [kb.alloc(f"s1vt{i}", [128, 4, 256], BF16) for i in range(2)]
    xs = [kb.alloc(f"s1xs{i}", [128, 8, tc]) for i in range(2)]
    hx = [kb.alloc(f"s1hx{i}", [128, 8, tc], BF16) for i in range(2)]
    tmp = {"sq": kb.alloc("s1sq", [128, 8, tc], BF16), "rs": kb.alloc("s1rs", [128, tc]),
           "xn": kb.alloc("s1xn", [128, 8, tc])}
    pst = [kb.alloc(f"s1pst{i}", [128, 11, tc]) for i in range(2)]
    nch = tlen // tc

    def load(c):
        kb.dma("sp", xs[c % 2], Buf(src.ap[:, c * tc:(c + 1) * tc].rearrange("(k p) t -> p k t", p=128),
                                    chunk_res(src, c * tc, (c + 1) * tc, RC)))
    load(0)
    if nch > 1:
        load(1)
    rmsnorm_mod(kb, g, xs[0], tc, g.gs1[l], l, 0, j, hx[0], tmp)
    tmp["presquared"] = True
    if nch > 1:
        kb.act(tmp["sq"], xs[1], AF.Square)
    for c in range(nch):
        if c + 1 < nch:
            rmsnorm_mod(kb, g, xs[(c + 1) % 2], tc, g.gs1[l], l, 0, j, hx[(c + 1) % 2], tmp)
        if c + 2 < nch:
            load(c + 2)
            kb.act(tmp["sq"], xs[c % 2], AF.Square)
        h = hx[c % 2]
        ps = pst[c % 2]
        for m in range(11):
            bank = kb.bank()
            kb.mm(bank[:, 0:tc], [(g.win[:, k, m * 128:(m + 1) * 128], h[:, k, :]) for k in range(8)])
            kb.copy("act" if m % 2 else "dve", ps[:, m, :], bank[:, 0:tc])
        kb.dma("sp", Buf(dst.ap[:, c * tc:(c + 1) * tc].rearrange("(m p) t -> p m t", p=128),
                         chunk_res(dst, c * tc, (c + 1) * tc, RC)), ps)
        for b4 in range(tc // 128):
            blk = c * (tc // 128) + b4
            bank = kb.bank()
            kb.mm(bank[:, 0:256], [(h[:, k, b4 * 128:(b4 + 1) * 128], g.win[:, k, 1152:1408]) for k in range(8)])
            kb.copy("act", vts[c % 2][:, b4, :], bank[:, 0:256])
        nb4 = tc // 128
        kb.dma("sp", Buf(VT.ap[:, c * nb4:(c + 1) * nb4, :], VT.res), vts[c % 2][:, 0:nb4, :])
    kb.release(mk)


def build_program(stop="all", dbg=()):
    nc = bass.Bass("TRN2", target_bir_lowering=False)
    g = G()
    g.d_xT = dram_in(nc, "xT", [D, T]); g.d_xT.res = [Res(f"xT{i}") for i in range(T // RC)]
    g.d_ctxT = dram_in(nc, "ctxT", [D, TX])
    g.d_cc = dram_in(nc, "cc", [128, 16])
    g.d_misc = dram_in(nc, "misc", [128, NMISC])
    d_pcol = dram_in(nc, "pcol", [NL, 128, NCOL]); g.d_pcol = [d_pcol[l] for l in range(NL)]
    d_wmod = dram_in(nc, "w_mod", [NL, D, 6 * D]); g.d_wmod = [d_wmod[l] for l in range(NL)]
    g.d_win = dram_in(nc, "w_in", [NL, D, DIN])
    g.d_wout = dram_in(nc, "w_out", [NL, D, D])
    g.d_lruw = dram_in(nc, "lruw", [NL, 2, 2, 6, 64, 64])
    g.d_gluw = dram_in(nc, "glu_w", [NL, 384, 384])
    g.d_poolw = dram_in(nc, "pool_w", [NL, 4, 64, 64])
    g.d_wg = dram_in(nc, "ffn_w_gate", [NL, D, DFF])
    g.d_wu = dram_in(nc, "ffn_w_up", [NL, D, DFF])
    g.d_wd = dram_in(nc, "ffn_w_down", [NL, DFF, D])
    g.d_s5p = dram_in(nc, "s5p", [NL, 2, 128, 72])
    g.d_s5b = dram_in(nc, "s5b", [NL, 128, 768])
    g.d_s5c = dram_in(nc, "s5c", [NL, 2, 128, 768])
    g.d_plat = dram_in(nc, "plat", list(PLAT.shape))
    g.d_pctx = dram_in(nc, "pctx", list(PCTX.shape))
    g.d_rinv = dram_in(nc, "rinv", [2, 128, 128])
    g.d_out = Buf(nc.dram_tensor("outT", [D, T], F32, kind="ExternalOutput").ap(), [Res(f"out{i}") for i in range(T // RC)])
    g.P = {"lat": dram_tmp(nc, "P_lat", [DIN, T], F32, T // RC, dbg), "ctx": dram_tmp(nc, "P_ctx", [DIN, TX], F32, 1, dbg)}
    g.YM = {"lat": dram_tmp(nc, "YM_lat", [D, T], BF16, T // RC, dbg), "ctx": dram_tmp(nc, "YM_ctx", [D, TX], BF16, 1, dbg)}
    g.ZS = {"lat": dram_tmp(nc, "ZS_lat", [384, T], BF16, T // RC, dbg), "ctx": dram_tmp(nc, "ZS_ctx", [384, TX], BF16, 1, dbg)}
    g.X1 = {"lat": dram_tmp(nc, "X1_lat", [D, T], F32, T // RC, dbg), "ctx": dram_tmp(nc, "X1_ctx", [D, TX], F32, 1, dbg)}
    g.VT = {"lat": dram_tmp(nc, "VT_lat", [128, T // 128, 256], BF16, 1, dbg), "ctx": dram_tmp(nc, "VT_ctx", [128, TX // 128, 256], BF16, 1, dbg)}
    with ExitStack() as es:
        big = es.enter_context(nc.sbuf_tensor("big", [128, SBW], F32))
        psums = [es.enter_context(nc.psum_tensor(f"ps{i}", [128, 1024], F32)) for i in range(4)]
        sems = {e: es.enter_context(nc.semaphore(f"s_{e}")) for e in ENG}
        kb = KB(nc, big, psums)
        dsems = [es.enter_context(nc.semaphore(f"d{i}")) for i in range(kb.ndma)]
        emit_all(kb, g, stop, dbg)
        kb.barrier()
        kb.replay(sems, dsems)
    g.kb = kb
    return nc, g


def emit_all(kb, g, stop, dbg):
    stage0(kb, g)
    g.xcur = {"lat": g.d_xT, "ctx": g.d_ctxT}
    for l in range(NL):
        last = l == NL - 1
        lmark = kb.mark()
        g.h0 = kb.alloc("h0", [128, 6])
        s1mark = kb.mark()
        g.win = kb.alloc("win", [128, 8, DIN], BF16)
        for k in range(8):
            kb.dma("pool", g.win[:, k, :], g.d_win[l][k * 128:(k + 1) * 128, :])
        stage1(kb, g, l, "ctx")
        stage1(kb, g, l, "lat")
        kb.release(s1mark)
        if stop == "s1":
            return
        stage2(kb, g, l, ctx_out=not last)
        if stop == "s2":
            return
        stage3(kb, g, l, ctx_out=not last)
        if stop == "s3":
            return
        stage4(kb, g, l, ctx_out=not last)
        kb.release(g.midmark)
        if stop == "s4":
            return
        stage56(kb, g, l, ("lat",) if last else ("ctx", "lat"), last)
        if stop == "l0":
            return
        g.xcur = dict(g.X1)
        kb.release(lmark)


def run(inputs, stop="all", dbg=(), cores=8, trace=False):
    nc, g = build_program(stop, dbg)
    sh = pack_shared(inputs)
    in_maps = []
    for cid in range(cores):
        m = dict(sh)
        m.update(pack_core(inputs, cid % 4))
        in_maps.append(m)
    res = run_bass_kernel_spmd(nc, in_maps, core_ids=list(range(cores)), trace=trace)
    return res, g


def kernel(**inputs):
    res, g = run(inputs)
    out = np.stack([np.ascontiguousarray(res.results[b]["outT"].T) for b in range(4)], axis=0)
    return out.astype(np.float32)


def stage2(kb, g, l, ctx_out):
    pc = g.pcol[l]
    mk = kb.mark()
    lw = kb.alloc("lruw", [128, 12, 128], BF16)
    kb.memset("pool", lw, 0.0)
    for d in range(2):
        for gate in range(2):
            for j3 in range(3):
                for hh in range(2):
                    kb.dma("pool", lw[hh * 64:(hh + 1) * 64, (d * 2 + gate) * 3 + j3, hh * 64:(hh + 1) * 64],
                           g.d_lruw[l, d, gate, 2 * j3 + hh])
    cst = kb.alloc("lrucst", [128, 18])
    hcl, hba, hbi = cst[:, 0:6], cst[:, 6:12], cst[:, 12:18]
    kb.act(hcl, pc[:, C_LAM:C_LAM + 6], AF.Exp, scale=-1.0)
    kb.ts("dve", hcl, hcl, 1.0, ALU.add)
    kb.act(hcl, hcl, AF.Ln)
    kb.ts("dve", hcl, hcl, -4.0, ALU.mult)
    kb.ts("dve", hba, pc[:, C_BA:C_BA + 6], 0.5, ALU.mult)
    kb.ts("dve", hbi, pc[:, C_BI:C_BI + 6], 0.5, ALU.mult)
    q25 = kb.alloc("q25", [128, 1])
    kb.memset("pool", q25, 0.25)
    xpad_f = kb.alloc("xpad", [128, T + 8], F32, nres=T // 1024)
    xc_f = kb.alloc("xc", [128, T], F32, nres=T // 1024)
    xcb_f = kb.alloc("xcb", [128, T], BF16, nres=T // 1024)
    ring_f = [[kb.alloc(f"lr{n}{i}", [128, 1024]) for i in range(4)] for n in ("A", "M", "B", "H", "G")]
    yab_f = [kb.alloc(f"yab{i}", [128, 1024], BF16) for i in range(2)]
    carry = kb.alloc("carry", [128, 2])
    for j3 in range(3):
        rows = slice(j3 * 128, (j3 + 1) * 128)
        for seq in ("ctx", "lat"):
            isc = seq == "ctx"
            tl = TX if isc else T
            tcl = min(1024, tl)
            nch = tl // tcl
            want_out = (not isc) or ctx_out
            xpad = Buf(xpad_f.ap[:, 0:tl + 8], xpad_f.res[0:nch])
            xc = Buf(xc_f.ap[:, 0:tl], xc_f.res[0:nch])
            xcb = Buf(xcb_f.ap[:, 0:tl], xcb_f.res[0:nch])
            hf = Buf(xpad.ap[:, 0:tl], xpad.res)
            ring = [[b[:, 0:tcl] for b in r] for r in ring_f]
            yab = [b[:, 0:tcl] for b in yab_f]
            P = g.P[seq]

            def cres(buf, c0, c1):
                return buf.res[max(c0, 0):min(c1, nch - 1) + 1]
            kb.memset("pool", Buf(xpad.ap[:, 0:2], xpad.res[0]), 0.0)
            kb.memset("pool", Buf(xpad.ap[:, tl + 2:tl + 4], xpad.res[nch - 1]), 0.0)
            for c in range(nch):
                kb.dma("sp", Buf(xpad.ap[:, 2 + c * tcl:2 + (c + 1) * tcl], xpad.res[c]),
                       Buf(P.ap[rows, c * tcl:(c + 1) * tcl], chunk_res(P, c * tcl, (c + 1) * tcl, RC)))
            for c in range(nch):
                cs = slice(c * tcl, (c + 1) * tcl)
                xcc = Buf(xc.ap[:, cs], xc.res[c])
                src = lambda k: Buf(xpad.ap[:, c * tcl + k:c * tcl + k + tcl], cres(xpad, c - 1, c + 1))
                kb.act(xcc, src(0), AF.Identity, bias=pc[:, C_CONVB + j3:C_CONVB + j3 + 1],
                       scale=pc[:, C_CONVW + j3 * 4:C_CONVW + j3 * 4 + 1])
                for k in range(1, 4):
                    kb.stt(xcc, src(k), pc[:, C_CONVW + j3 * 4 + k:C_CONVW + j3 * 4 + k + 1], xcc, ALU.mult, ALU.add)
                kb.copy("act", Buf(xcb.ap[:, cs], xcb.res[c]), xcc)
            for d in range(2):
                order = list(range(nch)) if d == 0 else list(range(nch - 1, -1, -1))
                ci = d * 3 + j3
                for p0 in range(0, nch, 2):
                    pair = order[p0:p0 + 2]
                    its = list(range(p0, p0 + len(pair)))
                    bufs = {it: tuple(r[it % 4] for r in ring) for it in its}
                    csl = {c: slice(c * tcl, (c + 1) * tcl) for c in pair}
                    xcc = {c: Buf(xc.ap[:, csl[c]], xc.res[c]) for c in pair}
                    pr, pi = {}, {}
                    for c in pair:
                        xbb = Buf(xcb.ap[:, csl[c]], xcb.res[c])
                        pr[c], pi[c] = kb.bank(2), kb.bank(2)
                        for hh in range(0, tcl, 512):
                            w_ = min(512, tcl - hh)
                            kb.mm(pr[c][:, hh:hh + w_], [(lw[:, (d * 2 + 0) * 3 + j3, :], xbb[:, hh:hh + w_])])
                            kb.mm(pi[c][:, hh:hh + w_], [(lw[:, (d * 2 + 1) * 3 + j3, :], xbb[:, hh:hh + w_])])
                    for it, c in zip(its, pair):
                        A, M, B, H, Gt = bufs[it]
                        kb.act(A, pr[c][:, 0:tcl], AF.Tanh, bias=hba[:, ci:ci + 1], scale=0.5)
                        kb.act(B, pi[c][:, 0:tcl], AF.Tanh, bias=hbi[:, ci:ci + 1], scale=0.5)
                    for it, c in zip(its, pair):
                        A, M, B, H, Gt = bufs[it]
                        kb.act(A, A, AF.Exp, bias=hcl[:, ci:ci + 1], scale=hcl[:, ci:ci + 1])
                        kb.tt("pool", M, A, A, ALU.mult)
                    for it, c in zip(its, pair):
                        A, M, B, H, Gt = bufs[it]
                        kb.act(M, M, AF.Sqrt, bias=q25, scale=-0.25)
                    if d == 1 and want_out:
                        for it, c in zip(its, pair):
                            A, M, B, H, Gt = bufs[it]
                            kb.dma("sp", Gt, Buf(P.ap[768 + j3 * 128:768 + (j3 + 1) * 128, csl[c]],
                                                 chunk_res(P, c * tcl, (c + 1) * tcl, RC)))
                            kb.act(Gt, Gt, AF.Gelu_apprx_tanh)
                    for it, c in zip(its, pair):
                        A, M, B, H, Gt = bufs[it]
                        cs = csl[c]
                        kb.stt(B, B, 1.0, xcc[c], ALU.add, ALU.mult)
                        kb.tt("dve" if it % 2 else "pool", B, B, M, ALU.mult)
                        if d == 0:
                            init = 0.0 if isc else g.h0[:, j3 * 2:j3 * 2 + 1]
                            if c > 0:
                                init = Buf(hf.ap[:, c * tcl - 1:c * tcl], hf.res[max(c - 2, 0):c])
                            kb.scan(Buf(hf.ap[:, cs], hf.res[max(c - 1, 0):c + 1]), A, B, init)
                            if isc and c == nch - 1:
                                kb.copy("dve", g.h0[:, j3 * 2:j3 * 2 + 1], Buf(hf.ap[:, tl - 1:tl], hf.res[max(c - 1, 0):c + 1]))
                        else:
                            init = (0.0 if isc else g.h0[:, j3 * 2 + 1:j3 * 2 + 2]) if it == 0 else carry[:, 0:1]
                            kb.scan(H[:, ::-1], A[:, ::-1], B[:, ::-1], init)
                            kb.copy("dve", carry[:, 0:1], H[:, 0:1])
                            if isc and c == 0:
                                kb.copy("dve", g.h0[:, j3 * 2 + 1:j3 * 2 + 2], H[:, 0:1])
                            if want_out:
                                kb.tt("dve", H, H, Buf(hf.ap[:, cs], hf.res[max(c - 1, 0):c + 1]), ALU.add)
                                kb.tt("pool", yab[it % 2], Gt, H, ALU.mult)
                                YM = g.YM[seq]
                                kb.dma("sp", Buf(YM.ap[rows, cs], chunk_res(YM, c * tcl, (c + 1) * tcl, RC)), yab[it % 2])
    kb.release(mk)


def stage3(kb, g, l, ctx_out):
    pc = g.pcol[l]
    mk = kb.mark()
    E = "dve"
    g.misc = kb.alloc("misc", [128, NMISC])
    kb.dma("sp", g.misc, g.d_misc)
    g.ident = g.misc[:, M_ID:M_ID + 128]
    g.bdmask = g.misc[:, M_BD:M_BD + 128]
    g.colmask = g.misc[:, M_CM:M_CM + 512].re("p (e c) -> p e c", e=4)
    g.ii = g.misc[:, M_II:M_II + 64]
    g.evmask = g.misc[:, M_EV:M_EV + 4]
    g.gmask = g.misc[:, M_GM:M_GM + 8]
    keep = []
    for d in range(2):
        keep.append([kb.alloc(f"s5pw{d}", [128, 17, 2, 24]), kb.alloc(f"s5sv{d}", [128, 10, 2, 24]),
                     kb.alloc(f"s5bst{d}", [128, 384]), kb.alloc(f"s5bsw{d}", [128, 384]),
                     kb.alloc(f"s5cst{d}", [128, 384]), kb.alloc(f"s5csw{d}", [128, 384])])
    diagD = kb.alloc("diagD", [128, 3, 128])
    mkt = kb.mark()
    s5b = kb.alloc("s5b", [128, 2, 384])
    kb.dma("sp", s5b, g.d_s5b[l].re("p (r c) -> p r c", r=2))
    halfpi = kb.alloc("halfpi", [128, 1])
    kb.memset("pool", halfpi, math.pi / 2)
    for j3 in range(3):
        kb.ts("dve", diagD[:, j3, :], g.ident, pc[:, C_S5D + j3:C_S5D + j3 + 1], ALU.mult)
    Pw, Sv, Bst, Bsw, Cst, Csw = [], [], [], [], [], []
    for d in range(2):
        prm = kb.alloc(f"s5prm{d}", [128, 72])
        kb.dma("sp", prm, g.d_s5p[l, d])
        s5c = kb.alloc(f"s5c{d}", [128, 2, 384])
        kb.dma("sp", s5c, g.d_s5c[l, d].re("p (r c) -> p r c", r=2))
        lr, li, ldt = prm[:, 0:24], prm[:, 24:48], prm[:, 48:72]
        tm = kb.alloc(f"s5tm{d}", [128, 12, 24])
        t = lambda i: tm[:, i, :]
        pw, sv, bst, bsw, cst, csw = keep[d]
        qw = kb.alloc(f"s5qw{d}", [128, 10, 2, 24])
        mul = lambda o, a, b: kb.tt(E, o, a, b, ALU.mult)
        add = lambda o, a, b: kb.tt(E, o, a, b, ALU.add)
        sub = lambda o, a, b: kb.tt(E, o, a, b, ALU.subtract)

        def cmul(outr, outi, ar, ai, br, bi):
            mul(t(10), ar, br); mul(t(11), ai, bi); sub(outr, t(10), t(11))
            mul(t(10), ar, bi); mul(t(11), ai, br); add(outi, t(10), t(11))
        kb.act(t(0), ldt, AF.Exp)
        mul(t(1), lr, t(0))
        mul(t(2), li, t(0))
        kb.act(t(3), t(1), AF.Exp, scale=1.0 / 16)
        kb.act(t(4), t(2), AF.Sin, scale=1.0 / 16)
        kb.act(t(5), t(2), AF.Sin, bias=halfpi, scale=1.0 / 16)
        mul(t(6), t(3), t(5)); mul(t(7), t(3), t(4))
        for it in range(4):
            cmul(t(8), t(9), t(6), t(7), t(6), t(7))
            kb.copy(E, t(6), t(8)); kb.copy(E, t(7), t(9))
        kb.memset(E, pw[:, 0, 0, :], 1.0); kb.memset(E, pw[:, 0, 1, :], 0.0)
        kb.copy(E, pw[:, 1, 0, :], t(6)); kb.copy(E, pw[:, 1, 1, :], t(7))
        for m in range(2, 17):
            cmul(pw[:, m, 0, :], pw[:, m, 1, :], pw[:, m - 1, 0, :], pw[:, m - 1, 1, :], pw[:, 1, 0, :], pw[:, 1, 1, :])
        kb.copy(E, qw[:, 0, 0, :], pw[:, 16, 0, :]); kb.copy(E, qw[:, 0, 1, :], pw[:, 16, 1, :])
        for i in range(1, 10):
            cmul(qw[:, i, 0, :], qw[:, i, 1, :], qw[:, i - 1, 0, :], qw[:, i - 1, 1, :], qw[:, i - 1, 0, :], qw[:, i - 1, 1, :])
        kb.copy(E, sv[:, :, 0, :], qw[:, :, 0, :]); kb.copy(E, sv[:, :, 1, :], qw[:, :, 1, :])
        kb.ts("dve", sv[64:128, :, 0, :], qw[64:128, :, 1, :], -1.0, ALU.mult)
        kb.copy("dve", sv[64:128, :, 1, :], qw[64:128, :, 0, :])
        mul(t(0), lr, lr); mul(t(1), li, li); add(t(0), t(0), t(1)); kb.recip(t(0), t(0))
        kb.ts("dve", t(1), pw[:, 1, 0, :], -1.0, ALU.add)
        mul(t(2), t(1), lr); mul(t(3), pw[:, 1, 1, :], li); add(t(2), t(2), t(3)); mul(t(2), t(2), t(0))
        mul(t(3), pw[:, 1, 1, :], lr); mul(t(4), t(1), li); sub(t(3), t(3), t(4)); mul(t(3), t(3), t(0))
        frb = t(2).re("p (g o) -> p g o", o=1).bc([128, 24, 16])
        fib = t(3).re("p (g o) -> p g o", o=1).bc([128, 24, 16])
        bb = kb.alloc(f"s5bb{d}", [128, 4, 384])
        v3 = lambda b: b.re("p (g k) -> p g k", k=16)
        Bre, Bim = v3(s5b[:, 0, :]), v3(s5b[:, 1, :])
        mul(v3(bb[:, 0, :]), Bre, frb); mul(v3(bb[:, 1, :]), Bim, fib); sub(bb[:, 0, :], bb[:, 0, :], bb[:, 1, :])
        mul(v3(bb[:, 1, :]), Bim, frb); mul(v3(bb[:, 2, :]), Bre, fib); add(bb[:, 1, :], bb[:, 1, :], bb[:, 2, :])
        kb.copy(E, bst, bb[:, 0, :]); kb.copy("dve", bst[64:128, :], bb[64:128, 1, :])
        kb.copy(E, bsw, bb[:, 0, :]); kb.ts("dve", bsw[0:64, :], bb[0:64, 1, :], -1.0, ALU.mult)
        kb.copy(E, cst, s5c[:, 0, :]); kb.ts("dve", cst[64:128, :], s5c[64:128, 1, :], -1.0, ALU.mult)
        kb.ts("dve", csw, s5c[:, 0, :], -1.0, ALU.mult); kb.ts("dve", csw[0:64, :], s5c[0:64, 1, :], -1.0, ALU.mult)
        Pw.append(pw); Sv.append(sv); Bst.append(bst); Bsw.append(bsw); Cst.append(cst); Csw.append(csw)

    kb.release(mkt)
    A = kb.alloc("s5A", [128, 16, 128])
    CRt = kb.alloc("s5CRt", [128, 16, 128])
    AT1 = kb.alloc("s5AT", [128, 16, 4, 128], BF16)
    AT = [AT1, AT1]
    CR = [kb.alloc(f"s5CR{d}", [128, 16, 4, 128], BF16) for d in range(2)]
    KF = [kb.alloc(f"s5KF{d}", [128, 16, 128], BF16) for d in range(2)]
    RD = kb.alloc("s5RD", [128, 8, 10, 128], BF16, nres=8)
    NZ = {"lat": T // 16, "ctx": TX // 16}
    Z1 = {s_: [kb.alloc(f"Z{s_}{q}", [128, NZ[s_] + 8]) for q in range(8)] for s_ in ("lat", "ctx")}
    Z = {s_: [Z1[s_], Z1[s_]] for s_ in ("lat", "ctx")}
    hfin = kb.alloc("s5hfin", [128, 8])
    Zb = {s_: [[kb.alloc(f"Zb{s_}{d}{q}", [128, NZ[s_] + 8], BF16) for q in range(8)] for d in range(2)] for s_ in ("lat", "ctx")}
    uB = {"lat": kb.alloc("uBl", [128, 16, T // 16], BF16), "ctx": kb.alloc("uBc", [128, 16, TX // 16], BF16)}
    zst = {"lat": kb.alloc("zstl", [128, T], BF16), "ctx": kb.alloc("zstc", [128, TX], BF16)}
    uld = [kb.alloc(f"uld{i}", [128, 1024]) for i in range(2)]
    v4 = lambda b: b.re("p m (g k) -> p m g k", k=16)
    bcm = lambda b, n: b.re("p (o g k) -> p o g k", o=1, k=16).bc([128, n, 8, 16])

    def prb(d, j3_, ri, m0, m1):
        return Pw[d][:, m0:m1, ri, j3_ * 8:j3_ * 8 + 8].re("p m (g o) -> p m g o", o=1).bc([128, m1 - m0, 8, 16])

    def gen_A(d, j3_):
        cols_ = slice(j3_ * 128, (j3_ + 1) * 128)
        kb.tt("dve", v4(A), bcm(Bst[d][:, cols_], 16), prb(d, j3_, 0, 0, 16), ALU.mult)
        kb.tt("pool", v4(CRt), bcm(Bsw[d][:, cols_], 16), prb(d, j3_, 1, 0, 16), ALU.mult)
        kb.tt("dve", A, A, CRt, ALU.add)

    def gen_AT(d):
        for m4 in range(0, 16, 4):
            bank = kb.bank()
            for m in range(m4, m4 + 4):
                kb.transpose(bank[:, (m - m4) * 128:(m - m4 + 1) * 128], A[:, m, :], g.ident)
            b3 = bank.re("p (m c) -> p m c", m=4)
            for v in range(4):
                if v % 2:
                    kb.act(AT[d][:, m4:m4 + 4, v, :], b3, AF.Copy, scale=g.evmask[:, v:v + 1])
                else:
                    kb.ts("dve", AT[d][:, m4:m4 + 4, v, :], b3, g.evmask[:, v:v + 1], ALU.mult)

    def gen_KF(d, j3_):
        cols_ = slice(j3_ * 128, (j3_ + 1) * 128)
        for m4 in range(0, 16, 4):
            bank2 = kb.bank()
            for m in range(m4, m4 + 4):
                kb.mm(bank2[:, (m - m4) * 128:(m - m4 + 1) * 128], [(A[:, m, :], Cst[d][:, cols_])])
            kb.tt("dve", KF[d][:, m4:m4 + 4, :], bank2.re("p (m c) -> p m c", m=4),
                  g.bdmask.re("p (o c) -> p o c", o=1).bc([128, 4, 128]), ALU.mult)
        if d == 0:
            kb.tt("pool", KF[0][:, 0, :], KF[0][:, 0, :], diagD[:, j3_, :], ALU.add)

    def gen_CR(d, j3_):
        cols_ = slice(j3_ * 128, (j3_ + 1) * 128)
        kb.tt("dve", v4(CRt), bcm(Cst[d][:, cols_], 16), prb(d, j3_, 0, 1, 17), ALU.mult)
        kb.tt("pool", v4(A), bcm(Csw[d][:, cols_], 16), prb(d, j3_, 1, 1, 17), ALU.mult)
        kb.tt("dve", CRt, CRt, A, ALU.add)
        for e in range(4):
            kb.tt("pool" if e % 2 else "dve", CR[d][:, :, e, :], CRt,
                  g.colmask[:, e, :].re("p (o c) -> p o c", o=1).bc([128, 16, 128]), ALU.mult)

    def gen_RD(d, j3_):
        for gl in range(8):
            rd = Buf(RD.ap[:, gl, :, :], RD.res[gl])
            for hh in range(2):
                svb = Sv[d][:, :, hh, j3_ * 8 + gl].re("p (i o) -> p i o", o=1).bc([128, 10, 64])
                kb.tt("pool" if hh else "dve", rd[:, :, hh * 64:(hh + 1) * 64],
                      g.ii.re("p (o c) -> p o c", o=1).bc([128, 10, 64]), svb, ALU.mult)

    def do_S(d):
        for s_ in ("ctx", "lat"):
            n = NZ[s_]
            for gl in range(8):
                q, e = gl // 4, gl % 4
                z = Z[s_][d][gl]
                bank = kb.bank()
                kb.mm(bank[:, 0:n], [(AT[d][64 * q:64 * q + 64, (15 - tau) if d == 0 else tau, e, :],
                                      uB[s_][64 * q:64 * q + 64, tau, :]) for tau in range(16)])
                dst = z[:, 1:n + 1] if d == 0 else z[:, n:0:-1]
                kb.copy("act", dst, bank[:, 0:n])

    def do_dbl(d):
        for s_ in ("ctx", "lat"):
            n = NZ[s_]
            W = n + 1 if s_ == "ctx" else n
            banks = []
            for gl in range(8):
                z, zb = Z[s_][d][gl], Zb[s_][d][gl]
                if s_ == "ctx":
                    kb.memset("pool", z[:, 0:1], 0.0)
                else:
                    kb.copy("pool", z[:, 0:1], hfin[:, gl:gl + 1])
                bank = kb.bank()
                banks.append(bank)
                kb.mm(bank[:, 0:W], [(g.ident, z[:, 0:W])])
                kb.copy("act" if gl % 2 else "dve", zb[:, 0:W], bank[:, 0:W])
            i, s_t = 0, 1
            while s_t < W:
                N = W - s_t
                for gl in range(8):
                    zb = Zb[s_][d][gl]
                    kb.mm(banks[gl][:, s_t:W], [(Buf(RD.ap[:, gl, i, :], RD.res[gl]), zb[:, 0:N])], first_start=False)
                    kb.copy("act" if (gl + i) % 2 else "dve", zb[:, s_t:W], banks[gl][:, s_t:W])
                i += 1
                s_t *= 2
            if s_ == "ctx":
                for gl in range(8):
                    kb.copy("act" if gl % 2 else "dve", hfin[:, gl:gl + 1], banks[gl][:, n:n + 1])
    seqs = ("ctx", "lat")
    for j3 in range(3):
        cols = slice(j3 * 128, (j3 + 1) * 128)
        rows = slice(384 + j3 * 128, 384 + (j3 + 1) * 128)
        it = 0
        for s_ in seqs:
            tl = TX if s_ == "ctx" else T
            for c0 in range(0, tl, 1024):
                w_ = min(1024, tl - c0)
                kb.dma("sp", uld[it % 2][:, 0:w_], Buf(g.P[s_].ap[rows, c0:c0 + w_], chunk_res(g.P[s_], c0, c0 + w_, RC)))
                kb.copy("act" if it % 2 else "dve", uB[s_][:, :, c0 // 16:(c0 + w_) // 16],
                        uld[it % 2][:, 0:w_].re("p (c t) -> p t c", t=16))
                it += 1
        if j3 == 0:
            gen_A(0, 0); gen_AT(0); gen_KF(0, 0); gen_CR(0, 0); gen_RD(0, 0)
        nxt = j3 + 1 < 3
        gen_CR(1, j3); gen_A(1, j3)
        do_S(0); gen_KF(1, j3); gen_AT(1); do_dbl(0); gen_RD(1, j3)
        if nxt:
            gen_A(0, j3 + 1)
        do_S(1)
        if nxt:
            gen_AT(0)
        do_dbl(1)
        if nxt:
            gen_RD(0, j3 + 1)
        for s_ in seqs:
            if s_ == "ctx" and not ctx_out:
                continue
            n = NZ[s_]
            tl = n * 16
            for tau in range(16):
                bank = kb.bank()
                prs = [(KF[0][:, j, :], uB[s_][:, tau - j, :]) for j in range(tau + 1)]
                prs += [(KF[1][:, j, :], uB[s_][:, tau + j, :]) for j in range(16 - tau)]
                for gl in range(8):
                    q, e = gl // 4, gl % 4
                    o = bank[64 * q:64 * q + 64, 0:n]
                    prs.append((CR[0][:, tau, e, 64 * q:64 * q + 64], Zb[s_][0][gl][:, 0:n], o))
                    prs.append((CR[1][:, 15 - tau, e, 64 * q:64 * q + 64], Zb[s_][1][gl][:, n - 1::-1], o))
                kb.mm(bank[:, 0:n], prs)
                kb.act(zst[s_][:, tau::16], bank[:, 0:n], AF.Gelu_apprx_tanh)
            ZS = g.ZS[s_]
            kb.dma("sp", Buf(ZS.ap[j3 * 128:(j3 + 1) * 128, :], ZS.res), zst[s_])
        if nxt:
            gen_KF(0, j3 + 1); gen_CR(0, j3 + 1)
    kb.release(mk)
    alloc_ffn_weights(kb, g)
    g.midmark = kb.mark()
    gw = kb.alloc("gluw", [128, 3, 384], BF16)
    for k in range(3):
        kb.dma("pool", gw[:, k, :], g.d_gluw[l][k * 128:(k + 1) * 128, :])
    g.plat = kb.alloc("plat", [128, PLAT.shape[0], 128], BF16)
    kb.dma("pool", g.plat, g.d_plat.re("n p c -> p n c"))
    g.pctx = kb.alloc("pctx", [128, PCTX.shape[0], 128], BF16)
    kb.dma("pool", g.pctx, g.d_pctx.re("n p c -> p n c"))
    g.poolw = kb.alloc("poolw", [128, 2, 128], BF16)
    kb.memset("pool", g.poolw, 0.0)
    for pt in range(2):
        for hh in range(2):
            kb.dma("pool", g.poolw[hh * 64:(hh + 1) * 64, pt, hh * 64:(hh + 1) * 64], g.d_poolw[l, 2 * pt + hh])
    load_ffn_weights(kb, g, l)
    seqs = ("ctx", "lat")
    mk = kb.mark()
    zc = [kb.alloc(f"gluz{i}", [128, 3, 512], BF16) for i in range(3)]
    sg = [kb.alloc(f"glus{i}", [128, 512]) for i in range(3)]
    yb = [kb.alloc(f"gluy{i}", [128, 3, 512], BF16) for i in range(2)]
    work = []
    for s_ in seqs:
        if s_ == "ctx" and not ctx_out:
            continue
        tl = TX if s_ == "ctx" else T
        tc = min(512, tl)
        work += [(s_, c, tc) for c in range(tl // tc)]

    def gload(i):
        s_, c, tc = work[i]
        ZS = g.ZS[s_]
        kb.dma("sp", zc[i % 3][:, :, 0:tc], Buf(ZS.ap[:, c * tc:(c + 1) * tc].rearrange("(k p) t -> p k t", p=128),
                                               chunk_res(ZS, c * tc, (c + 1) * tc, RC)))
    for i in range(min(2, len(work))):
        gload(i)
    for it, (s_, c, tc) in enumerate(work):
        if it + 2 < len(work):
            gload(it + 2)
        zz, yy = zc[it % 3], yb[it % 2]
        YM = g.YM[s_]
        for mo in range(3):
            bank = kb.bank()
            kb.mm(bank[:, 0:tc], [(gw[:, k, mo * 128:(mo + 1) * 128], zz[:, k, 0:tc]) for k in range(3)])
            kb.act(sg[mo][:, 0:tc], bank[:, 0:tc], AF.Sigmoid, bias=pc[:, C_GLUB + mo:C_GLUB + mo + 1])
            kb.tt("dve", yy[:, mo, 0:tc], zz[:, mo, 0:tc], sg[mo][:, 0:tc], ALU.mult)
        kb.dma("sp", Buf(YM.ap[384:768, c * tc:(c + 1) * tc].rearrange("(k p) t -> p k t", p=128),
                          chunk_res(YM, c * tc, (c + 1) * tc, RC)), yy[:, :, 0:tc])
    kb.release(mk)


def stage4(kb, g, l, ctx_out):
    pc = g.pcol[l]
    mk = kb.mark()
    plat, pctx, pw = g.plat, g.pctx, g.poolw
    rinv = kb.alloc("rinv", [128, 2, 128])
    kb.dma("sp", rinv, g.d_rinv.re("t p r -> p t r"))
    sb = kb.alloc("poolsb", [128, 2])
    kb.tt("dve", sb, pc[:, C_POOLB:C_POOLB + 2], pc[:, C_POOLS:C_POOLS + 2], ALU.mult)
    vbuf = [kb.alloc(f"poolv{i}", [128, 512]) for i in range(3)]
    tbuf = [kb.alloc(f"poolt{i}", [128, 512]) for i in range(2)]
    mbuf = [kb.alloc(f"poolm{i}", [128, 512], BF16) for i in range(2)]
    ybuf = [kb.alloc(f"pooly{i}", [128, 512], BF16) for i in range(2)]
    vt_f = kb.alloc("vt", [128, T // 256 + 4, 256], BF16)
    it = 0
    for seq in ("ctx", "lat"):
        isc = seq == "ctx"
        if isc and not ctx_out:
            continue
        tl = TX if isc else T
        nb = tl // 128
        tc = min(512, tl)
        P, YM = g.P[seq], g.YM[seq]
        nhalf = 1 if isc else 2
        bper = nb // nhalf
        for half in range(nhalf):
          blo, bhi = max(0, half * bper - 4), min(nb, (half + 1) * bper + 4)
          vt = vt_f[:, 0:bhi - blo, :]
          kb.dma("sp", vt, Buf(g.VT[seq].ap[:, blo:bhi, :], g.VT[seq].res))
          cper = (tl // tc) // nhalf
          wl = [(c, pt) for c in range(half * cper, (half + 1) * cper) for pt in range(2)]

          def vload(i, base):
              c_, pt_ = wl[i]
              kb.dma("sp", vbuf[(base + i) % 3][:, 0:tc],
                     Buf(P.ap[1152 + pt_ * 128:1152 + (pt_ + 1) * 128, c_ * tc:(c_ + 1) * tc],
                         chunk_res(P, c_ * tc, (c_ + 1) * tc, RC)))
          base = it
          for i in range(min(2, len(wl))):
              vload(i, base)
          for wi, (c, pt) in enumerate(wl):
              if True:
                  if wi + 2 < len(wl):
                      vload(wi + 2, base)
                  v, t_, m_, y_ = vbuf[it % 3], tbuf[it % 2], mbuf[it % 2], ybuf[it % 2]
                  it += 1
                  bank = kb.bank()
                  for b4 in range(tc // 128):
                      B = c * (tc // 128) + b4
                      for gl in range(2):
                          gg = 2 * pt + gl
                          prs = []
                          for dlt in range(-4, 5):
                              if not 0 <= B + dlt < nb:
                                  continue
                              if isc:
                                  idx = CTXIDX.get((gg, B, dlt))
                                  mat = None if idx is None else pctx[:, idx, :]
                              else:
                                  idx = LATIDX.get((gg, dlt))
                                  mat = None if idx is None else plat[:, idx, :]
                              if mat is not None:
                                  prs.append((vt[:, B + dlt - blo, gg * 64:(gg + 1) * 64], mat))
                          kb.mm(bank[gl * 64:(gl + 1) * 64, b4 * 128:(b4 + 1) * 128], prs)
                  if isc:
                      kb.tt("dve", m_[:, 0:tc], bank[:, 0:tc], v[:, 0:tc], ALU.subtract)
                  else:
                      r0 = c * (tc // 64)
                      kb.tt("dve", t_.re("p (r c) -> p r c", c=64), bank.re("p (r c) -> p r c", c=64),
                            rinv[:, pt, r0:r0 + tc // 64].re("p (r o) -> p r o", o=1).bc([128, tc // 64, 64]), ALU.mult)
                      kb.tt("dve", m_[:, 0:tc], t_[:, 0:tc], v[:, 0:tc], ALU.subtract)
                  bank2 = kb.bank()
                  kb.mm(bank2[:, 0:tc], [(pw[:, pt, :], m_[:, 0:tc])])
                  kb.act(y_[:, 0:tc], bank2[:, 0:tc], AF.Identity, bias=sb[:, pt:pt + 1], scale=pc[:, C_POOLS + pt:C_POOLS + pt + 1])
                  kb.dma("sp", Buf(YM.ap[768 + pt * 128:768 + (pt + 1) * 128, c * tc:(c + 1) * tc],
                                   chunk_res(YM, c * tc, (c + 1) * tc, RC)), y_[:, 0:tc])
    kb.release(mk)


def alloc_ffn_weights(kb, g):
    g.ffnw = (kb.alloc("wo", [128, 8, D], BF16), kb.alloc("wg", [128, 8, DFF], BF16),
              kb.alloc("wu", [128, 8, DFF], BF16), kb.alloc("wd", [128, 22, D], BF16))


def load_ffn_weights(kb, g, l):
    wo, wg, wu, wd = g.ffnw
    for k in range(8):
        kb.dma("pool", wo[:, k, :], g.d_wout[l][k * 128:(k + 1) * 128, :])
    for k in range(8):
        kb.dma("pool", wg[:, k, :], g.d_wg[l][k * 128:(k + 1) * 128, :], max_dma_last_dim=4096)
        kb.dma("pool", wu[:, k, :], g.d_wu[l][k * 128:(k + 1) * 128, :], max_dma_last_dim=4096)
    for f in range(22):
        kb.dma("pool", wd[:, f, :], g.d_wd[l][f * 128:(f + 1) * 128, :])


def stage56(kb, g, l, seqs, last):
    pc = g.pcol[l]
    mk = kb.mark()
    tc = 256
    wo, wg, wu, wd = g.ffnw
    NX = 3
    xs = [kb.alloc(f"fxs{i}", [128, 8, tc]) for i in range(NX)]
    ym = kb.alloc("fym", [128, 8, tc], BF16)
    hx = [kb.alloc(f"fhx{i}", [128, 8, tc], BF16) for i in range(2)]
    sq1 = kb.alloc("fsq", [128, 8, tc], BF16)
    sq = [sq1, sq1]
    rs1 = kb.alloc("frs", [128, tc])
    rs = [rs1, rs1]
    xn = [kb.alloc(f"fxn{i}", [128, tc]) for i in range(2)]
    hmid = kb.alloc("fhmid", [128, 22, tc], BF16)
    sg = [kb.alloc(f"fsg{i}", [128, tc]) for i in range(2)]
    items = []
    for seq in seqs:
        tl = TX if seq == "ctx" else T
        items += [(seq, c) for c in range(tl // tc)]
    n_it = len(items)

    def jof(i):
        return 1 if items[i][0] == "ctx" else 0

    def load_x(i):
        seq, c = items[i]
        X = g.xcur[seq]
        kb.dma("sp", xs[i % NX], Buf(X.ap[:, c * tc:(c + 1) * tc].rearrange("(k p) t -> p k t", p=128),
                                    chunk_res(X, c * tc, (c + 1) * tc, RC)))

    def load_y(i):
        seq, c = items[i]
        YM = g.YM[seq]
        kb.dma("sp", ym, Buf(YM.ap[:, c * tc:(c + 1) * tc].rearrange("(k p) t -> p k t", p=128),
                             chunk_res(YM, c * tc, (c + 1) * tc, RC)))

    def wout(i):
        x, j = xs[i % NX], jof(i)
        for m in range(8):
            bank = kb.bank()
            kb.mm(bank[:, 0:tc], [(wo[:, k, m * 128:(m + 1) * 128], ym[:, k, :]) for k in range(8)])
            kb.stt(x[:, m, :], bank[:, 0:tc], modcol(g, l, 2, m, j), x[:, m, :], ALU.mult, ALU.add)

    def norm_a(i):
        kb.act(sq[i % 2], xs[i % NX], AF.Square)

    def norm_b(i, final=False):
        x, j, r_ = xs[i % NX], jof(i), rs[i % 2]
        bank = kb.bank()
        kb.mm(bank[:, 0:tc], [(g.ones, sq[i % 2][:, k, :]) for k in range(8)])
        kb.act(r_, bank[:, 0:tc], AF.Sqrt, bias=g.epsc, scale=1.0 / D)
        kb.recip(r_, r_)
        if final:
            for m in range(8):
                kb.tt("dve" if m % 2 else "pool", x[:, m, :], x[:, m, :], r_, ALU.mult)
                kb.act(x[:, m, :], x[:, m, :], AF.Copy, scale=pc[:, C_FING + m:C_FING + m + 1])
            return
        for k in range(8):
            kb.tt("dve" if k % 2 else "pool", xn[k % 2], x[:, k, :], r_, ALU.mult)
            kb.act(hx[i % 2][:, k, :], xn[k % 2], AF.Identity, bias=modcol(g, l, 3, k, j), scale=g.gs2[l][:, k, j:j + 1])

    def gateup(i):
        h = hx[i % 2]
        for f in range(22):
            bg, bu = kb.bank(), kb.bank()
            kb.mm(bg[:, 0:tc], [(wg[:, k, f * 128:(f + 1) * 128], h[:, k, :]) for k in range(8)])
            kb.mm(bu[:, 0:tc], [(wu[:, k, f * 128:(f + 1) * 128], h[:, k, :]) for k in range(8)])
            kb.act(sg[f % 2], bg[:, 0:tc], AF.Silu)
            kb.tt("dve", hmid[:, f, :], sg[f % 2], bu[:, 0:tc], ALU.mult)

    def down(i):
        x, j = xs[i % NX], jof(i)
        seq, c = items[i]
        for m in range(8):
            bank = kb.bank()
            kb.mm(bank[:, 0:tc], [(wd[:, f, m * 128:(m + 1) * 128], hmid[:, f, :]) for f in range(22)])
            kb.stt(x[:, m, :], bank[:, 0:tc], modcol(g, l, 5, m, j), x[:, m, :], ALU.mult, ALU.add)
        if last:
            norm_a(i)
            norm_b(i, final=True)
            dst = g.d_out
        else:
            dst = g.X1[seq]
        kb.dma("sp", Buf(dst.ap[:, c * tc:(c + 1) * tc].rearrange("(k p) t -> p k t", p=128),
                         chunk_res(dst, c * tc, (c + 1) * tc, RC)), x)
    load_x(0)
    load_y(0)
    if n_it > 1:
        load_x(1)
    wout(0)
    if n_it > 1:
        load_y(1)
    norm_a(0)
    norm_b(0)
    for i in range(n_it):
        if i + 2 < n_it:
            load_x(i + 2)
        if i + 1 < n_it:
            wout(i + 1)
            if i + 2 < n_it:
                load_y(i + 2)
            norm_a(i + 1)
        gateup(i)
        if i + 1 < n_it:
            norm_b(i + 1)
        down(i)
    kb.release(mk)
```

```python
import math
from contextlib import ExitStack

import numpy as np
import concourse.bass as bass
import concourse.mybir as mybir
from concourse.bass_utils import run_bass_kernel_spmd

F32 = mybir.dt.float32
BF16 = mybir.dt.bfloat16
AF = mybir.ActivationFunctionType
ALU = mybir.AluOpType

D = 1024
T = 8192
TX = 256
DIN = 1408
DFF = 2816
NL = 2
NCOL = 116
ENG = ("pe", "act", "dve", "pool", "sp")
SBW = 53184

C_N1G, C_N2G, C_BMOD, C_CONVW, C_CONVB, C_BA, C_BI, C_LAM, C_S5D, C_GLUB, C_POOLB, C_POOLS, C_FING = (
    0, 8, 16, 64, 76, 79, 85, 91, 97, 100, 103, 105, 107)

POOL_HALF = (1, 2, 4, 8)


class Res:
    __slots__ = ("w", "r", "name")

    def __init__(self, name=""):
        self.w = None
        self.r = {}
        self.name = name


class Buf:
    def __init__(self, ap, res):
        self.ap = ap
        self.res = list(res) if isinstance(res, (list, tuple)) else [res]

    def __getitem__(self, idx):
        return Buf(self.ap[idx], self.res)

    def re(self, pat, **kw):
        return Buf(self.ap.rearrange(pat, **kw), self.res)

    def bc(self, shape):
        return Buf(self.ap.broadcast_to(list(shape)), self.res)

    def wr(self, res):
        return Buf(self.ap, res)


def _aps(x):
    return x.ap if isinstance(x, Buf) else x


class KB:
    def __init__(self, nc, big, psums):
        self.nc = nc
        self.big = big
        self.streams = {e: [] for e in ENG}
        self.cnt = {e: 0 for e in ENG}
        self.seen = {e: {} for e in ENG}
        self.ndma = 40
        self.dma_val = [0] * self.ndma
        self.dma_rr = 0
        self.sb_off = 0
        self.sb_peak = 0
        self.psums = psums
        self.bank_res = [Res(f"bank{i}") for i in range(8)]
        self.bank_rr = 0

    def alloc(self, name, shape, dtype=F32, nres=1):
        p = shape[0]
        n = int(np.prod(shape[1:]))
        isz = 2 if dtype == BF16 else 4
        words = (n * isz + 3) // 4
        words = (words + 7) // 8 * 8
        off = self.sb_off
        self.sb_off += words
        self.sb_peak = max(self.sb_peak, self.sb_off)
        assert self.sb_off <= SBW, f"SBUF arena overflow at {name}: {self.sb_off}"
        ap = self.big[0:p, off:off + words]
        if dtype == BF16:
            ap = ap.bitcast(BF16)
        ap = ap[:, 0:n]
        if len(shape) == 3:
            ap = ap.rearrange("p (a b) -> p a b", a=shape[1])
        elif len(shape) == 4:
            ap = ap.rearrange("p (a b c) -> p a b c", a=shape[1], b=shape[2])
        if nres == 1:
            return Buf(ap, Res(name))
        return Buf(ap, [Res(f"{name}{i}") for i in range(nres)])

    def mark(self):
        return self.sb_off

    def release(self, mark):
        self.barrier()
        self.sb_off = mark

    def bank(self, n=1):
        if n == 2 and self.bank_rr % 2:
            self.bank_rr += 1
        k = self.bank_rr % 8
        self.bank_rr += n
        ap = self.psums[k // 2]
        if n == 1:
            return Buf(ap[:, (k % 2) * 512:(k % 2) * 512 + 512], self.bank_res[k])
        return Buf(ap[:, :], [self.bank_res[k], self.bank_res[k + 1]])

    def _need(self, eng, ev):
        if ev is None:
            return
        key, val = ev
        if key == eng and eng == "pe":
            return
        if self.seen[eng].get(key, 0) >= val:
            return
        self.seen[eng][key] = val
        self.streams[eng].append(("w", key, val))

    def _deps(self, eng, reads, writes):
        for r in reads:
            self._need(eng, r.w)
        for w in writes:
            self._need(eng, w.w)
            for ev in w.r.values():
                self._need(eng, ev)

    def _done(self, key, ev, reads, writes):
        for r in reads:
            r.r[key] = ev
        for w in writes:
            w.w = ev
            w.r = {}

    def op(self, eng, fn, ins=(), outs=()):
        reads = [r for b in ins if isinstance(b, Buf) for r in b.res]
        writes = [r for b in outs if isinstance(b, Buf) for r in b.res]
        self._deps(eng, reads, writes)
        self.cnt[eng] += 1
        ev = (eng, self.cnt[eng])
        self.streams[eng].append(("i", fn, True))
        self._done(eng, ev, reads, writes)

    def mm(self, out, pairs, first_start=True):
        reads = [r for pr in pairs for b in pr[:2] for r in b.res]
        writes = list(out.res)
        self._deps("pe", reads, writes)
        n = len(pairs)
        for i, pr in enumerate(pairs):
            l, rr = pr[0], pr[1]
            o = pr[2].ap if len(pr) > 2 else out.ap
            self.streams["pe"].append(
                ("i", (lambda e, o=o, l=l.ap, r=rr.ap, st=(i == 0 and first_start), sp=(i == n - 1):
                       e.matmul(o, l, r, start=st, stop=sp)), i == n - 1))
        self.cnt["pe"] += 1
        self._done("pe", ("pe", self.cnt["pe"]), reads, writes)

    def transpose(self, out, in_, ident):
        self.op("pe", lambda e, o=out.ap, i=in_.ap, d=ident.ap: e.transpose(o, i, d), [in_, ident], [out])

    def dma(self, q, out, in_, **kw):
        i = self.dma_rr
        self.dma_rr = (i + 1) % self.ndma
        key = ("d", i)
        if self.dma_val[i] > 0:
            self._need(q, (key, self.dma_val[i]))
        reads = list(in_.res)
        writes = list(out.res)
        self._deps(q, reads, writes)
        self.dma_val[i] += 16
        ev = (key, self.dma_val[i])
        self.streams[q].append(("d", out.ap, in_.ap, i, kw))
        self._done(key, ev, reads, writes)

    def barrier(self):
        evs = [(e, self.cnt[e]) for e in ENG if self.cnt[e] > 0]
        evs += [(("d", i), v) for i, v in enumerate(self.dma_val) if v > 0]
        for e in ENG:
            for ev in evs:
                self._need(e, ev)

    def act(self, out, in_, func, bias=None, scale=None, eng="act"):
        kw = {}
        ins = [in_]
        if bias is not None:
            kw["bias"] = _aps(bias)
            ins.append(bias)
        if scale is not None:
            kw["scale"] = _aps(scale)
            ins.append(scale)
        self.op(eng, lambda e, o=out.ap, i=in_.ap, f=func, kw=kw: e.activation(out=o, in_=i, func=f, **kw), ins, [out])

    def tt(self, eng, out, a, b, op):
        self.op(eng, lambda e, o=out.ap, a_=a.ap, b_=b.ap, op=op: e.tensor_tensor(out=o, in0=a_, in1=b_, op=op), [a, b], [out])

    def ts(self, eng, out, a, s1, op0, s2=None, op1=None):
        def fn(e, o=out.ap, a_=a.ap, s1=_aps(s1), s2=_aps(s2), op0=op0, op1=op1):
            if op1 is None:
                return e.tensor_scalar(out=o, in0=a_, scalar1=s1, scalar2=None, op0=op0)
            return e.tensor_scalar(out=o, in0=a_, scalar1=s1, scalar2=s2, op0=op0, op1=op1)
        self.op(eng, fn, [a, s1, s2], [out])

    def stt(self, out, a, s, b, op0, op1):
        self.op("dve", lambda e, o=out.ap, a_=a.ap, s_=_aps(s), b_=b.ap, op0=op0, op1=op1:
                e.scalar_tensor_tensor(out=o, in0=a_, scalar=s_, in1=b_, op0=op0, op1=op1), [a, s, b], [out])

    def copy(self, eng, out, in_):
        if eng == "act":
            self.act(out, in_, AF.Copy)
        else:
            self.op(eng, lambda e, o=out.ap, i=in_.ap: e.tensor_copy(out=o, in_=i), [in_], [out])

    def memset(self, eng, out, val):
        self.op(eng, lambda e, o=out.ap, v=val: e.memset(o, v), [], [out])

    def scan(self, out, a, b, init):
        self.op("dve", lambda e, o=out.ap, a_=a.ap, b_=b.ap, i_=_aps(init):
                e.tensor_tensor_scan(out=o, data0=a_, data1=b_, initial=i_, op0=ALU.mult, op1=ALU.add),
                [a, b, init], [out])

    def recip(self, out, in_):
        self.op("dve", lambda e, o=out.ap, i=in_.ap: e.reciprocal(out=o, in_=i), [in_], [out])

    def replay(self, sems, dsems):
        nc = self.nc
        engs = {"pe": "tensor", "act": "scalar", "dve": "vector", "pool": "gpsimd", "sp": "sync"}

        def semh(key):
            return dsems[key[1]] if isinstance(key, tuple) else sems[key]

        with nc.Block() as block:
            for name in ENG:
                stream = self.streams[name]

                def body(e, stream=stream, name=name):
                    for it in stream:
                        if it[0] == "w":
                            e.wait_ge(semh(it[1]), it[2])
                        elif it[0] == "i":
                            ins = it[1](e)
                            if it[2]:
                                ins.then_inc(sems[name], 1)
                        else:
                            _, o, i, k, kw = it
                            e.dma_start(out=o, in_=i, **kw).then_inc(dsems[k], 16)
                getattr(block, engs[name])(body)


def _colT(v):
    return np.ascontiguousarray(np.asarray(v, np.float32).reshape(-1, 128).T)


def _pool_consts():
    lat, latidx = [], {}
    for g, half in enumerate(POOL_HALF):
        for dlt in range(-4, 5):
            m = np.zeros((128, 128), np.float32)
            for ri in range(2):
                for ro in range(2):
                    rel = 2 * dlt + ri
                    if not (ro - half <= rel < ro + half):
                        continue
                    for co in range(64):
                        lo, hi = max(co - half, 0), min(co + half, 64)
                        m[ri * 64 + lo:ri * 64 + hi, ro * 64 + co] = 1.0 / (hi - lo)
            if m.any():
                latidx[(g, dlt)] = len(lat)
                lat.append(m)
    ctx, ctxidx = [], {}
    for g, half in enumerate(POOL_HALF):
        for B in range(2):
            for dlt in (-1, 0, 1):
                if not 0 <= B + dlt < 2:
                    continue
                m = np.zeros((128, 128), np.float32)
                for o in range(128):
                    to = 128 * B + o
                    lo, hi = max(to - half, 0), min(to + half, TX)
                    for ti in range(lo, hi):
                        if 128 * (B + dlt) <= ti < 128 * (B + dlt + 1):
                            m[ti - 128 * (B + dlt), o] = 1.0 / (hi - lo)
                if m.any():
                    ctxidx[(g, B, dlt)] = len(ctx)
                    ctx.append(m)
    rinv = np.zeros((2, 128, 128), np.float32)
    for pt in range(2):
        for p in range(128):
            half = POOL_HALF[2 * pt + p // 64]
            for r in range(128):
                rinv[pt, p, r] = 1.0 / (min(r + half, 128) - max(r - half, 0))
    return np.stack(lat), latidx, np.stack(ctx), ctxidx, rinv


PLAT, LATIDX, PCTX, CTXIDX, RINV = _pool_consts()


def _consts():
    p = np.arange(128)
    ident = np.eye(128, dtype=np.float32)
    bdmask = (p[:, None] // 16 == p[None, :] // 16).astype(np.float32)
    qmask = np.stack([((p // 16) % 4 == v) for v in range(4)], axis=1).astype(np.float32)
    colmask = np.broadcast_to(qmask.T[None, :, :], (128, 4, 128)).astype(np.float32).copy()
    ii = np.concatenate([np.eye(64, dtype=np.float32)] * 2, axis=0)
    gmask = np.stack([((p // 16) == v) for v in range(8)], axis=1).astype(np.float32)
    misc = np.concatenate([ident, bdmask, colmask.reshape(128, 512), ii, qmask, gmask], axis=1)
    return np.ascontiguousarray(misc)


MISC = _consts()
M_ID, M_BD, M_CM, M_II, M_EV = 0, 128, 256, 768, 832
NMISC = 844
M_GM = 836


def pack_shared(inp):
    f = lambda k: np.asarray(inp[k], np.float32)
    sh = {}
    pcol = np.zeros((NL, 128, NCOL), np.float32)
    for l in range(NL):
        pc = pcol[l]
        pc[:, C_N1G:C_N1G + 8] = _colT(f("norm1_g")[l])
        pc[:, C_N2G:C_N2G + 8] = _colT(f("norm2_g")[l])
        pc[:, C_BMOD:C_BMOD + 48] = _colT(f("b_mod")[l])
        for j in range(3):
            sl = slice(j * 128, (j + 1) * 128)
            for k in range(4):
                pc[:, C_CONVW + j * 4 + k] = f("lru_conv_w")[l, k, sl]
            pc[:, C_CONVB + j] = f("lru_conv_b")[l, sl]
            for d in range(2):
                pc[:, C_BA + d * 3 + j] = f("lru_ba")[l, d, sl]
                pc[:, C_BI + d * 3 + j] = f("lru_bi")[l, d, sl]
                pc[:, C_LAM + d * 3 + j] = f("lru_lambda")[l, d, sl]
            pc[:, C_S5D + j] = f("s5_d")[l, sl]
            pc[:, C_GLUB + j] = f("s5_glu_b")[l, sl]
        for j in range(2):
            sl = slice(j * 128, (j + 1) * 128)
            pc[:, C_POOLB + j] = f("pool_b")[l, sl]
            pc[:, C_POOLS + j] = f("pool_scale")[l, sl]
        pc[:, C_FING:C_FING + 8] = _colT(f("final_g"))
    sh["pcol"] = pcol
    for k in ("w_mod", "w_in", "w_out", "ffn_w_gate", "ffn_w_up", "ffn_w_down", "pool_w"):
        sh[k] = np.ascontiguousarray(f(k))
    sh["glu_w"] = np.ascontiguousarray(f("s5_glu_w"))
    sh["lruw"] = np.ascontiguousarray(np.stack([f("lru_wa"), f("lru_wi")], axis=2))
    s5p = np.zeros((NL, 2, 128, 72), np.float32)
    s5b = np.zeros((NL, 128, 768), np.float32)
    s5c = np.zeros((NL, 2, 128, 768), np.float32)
    for l in range(NL):
        s5b[l, :, 0:384] = np.tile(f("s5_b_re")[l].transpose(1, 0, 2).reshape(64, 384), (2, 1))
        s5b[l, :, 384:768] = np.tile(f("s5_b_im")[l].transpose(1, 0, 2).reshape(64, 384), (2, 1))
        for d in range(2):
            s5p[l, d, :, 0:24] = np.tile(f("s5_lambda_re")[l, d].T, (2, 1))
            s5p[l, d, :, 24:48] = np.tile(f("s5_lambda_im")[l, d].T, (2, 1))
            s5p[l, d, :, 48:72] = np.broadcast_to(f("s5_log_dt")[l, d][None, :], (128, 24))
            s5c[l, d, :, 0:384] = np.tile(f("s5_c_re")[l, d].transpose(2, 0, 1).reshape(64, 384), (2, 1))
            s5c[l, d, :, 384:768] = np.tile(f("s5_c_im")[l, d].transpose(2, 0, 1).reshape(64, 384), (2, 1))
    sh["s5p"], sh["s5b"], sh["s5c"] = s5p, s5b, s5c
    sh["misc"] = MISC
    sh["plat"] = PLAT
    sh["pctx"] = PCTX
    sh["rinv"] = RINV
    return sh


def pack_core(inp, b):
    x = np.asarray(inp["x"], np.float32)[b]
    ctx = np.asarray(inp["ctx"], np.float32)[b]
    cc = np.stack([_colT(np.asarray(inp["c"], np.float32)[b]), _colT(np.asarray(inp["c_ctx"], np.float32))], axis=2)
    return {"xT": np.ascontiguousarray(x.T), "ctxT": np.ascontiguousarray(ctx.T),
            "cc": np.ascontiguousarray(cc.reshape(128, 16))}


class G:
    pass


def dram_in(nc, name, shape, dtype=F32):
    return Buf(nc.dram_tensor(name, list(shape), dtype, kind="ExternalInput").ap(), Res(name))


def dram_tmp(nc, name, shape, dtype, nres, dbg):
    kind = "ExternalOutput" if name in dbg else "Internal"
    return Buf(nc.dram_tensor(name, list(shape), dtype, kind=kind).ap(), [Res(f"{name}{i}") for i in range(nres)])


def chunk_res(buf, c0, c1, csz):
    return buf.res[c0 // csz:(c1 - 1) // csz + 1]


RC = 512


def colsl(buf, rows, c0, c1):
    return Buf(buf.ap[rows, c0:c1], chunk_res(buf, c0, c1, RC))


def stage0(kb, g):
    nc = kb.nc
    g.ones = kb.alloc("ones", [128, 128], BF16)
    kb.memset("pool", g.ones, 1.0)
    g.epsc = kb.alloc("epsc", [128, 1])
    kb.memset("pool", g.epsc, 1e-6)
    g.pcol = []
    for l in range(NL):
        pc = kb.alloc(f"pcol{l}", [128, NCOL])
        kb.dma("sp", pc, g.d_pcol[l])
        g.pcol.append(pc)
    cc = kb.alloc("cc", [128, 16])
    kb.dma("sp", cc, g.d_cc)
    scc = kb.alloc("scc", [128, 16])
    kb.act(scc, cc, AF.Silu)
    scc3 = scc.re("p (k j) -> p k j", j=2)
    g.mod = [kb.alloc(f"mod{l}", [128, 48, 2]) for l in range(NL)]
    mk = kb.mark()
    idn2 = kb.alloc("idn2", [128, 2])
    kb.dma("sp", idn2[0:2, :], g.d_misc[0:2, 0:2])
    modrow = kb.alloc("modrow", [128, 6 * D])
    wm = [kb.alloc(f"wm{i}", [128, 8, 512]) for i in range(2)]
    it = 0
    for l in range(NL):
        for cg in range(12):
            w = wm[it % 2]
            it += 1
            kb.dma("sp", w, g.d_wmod[l][:, cg * 512:(cg + 1) * 512].re("(k p) c -> p k c", p=128))
            bank = kb.bank()
            kb.mm(bank[0:2, :], [(scc3[:, k, :], w[:, k, :]) for k in range(8)])
            kb.copy("act" if cg % 2 else "dve", modrow[0:2, cg * 512:(cg + 1) * 512], bank[0:2, :])
        bank = kb.bank()
        for m in range(48):
            kb.transpose(bank[:, 2 * m:2 * m + 2], modrow[0:2, m * 128:(m + 1) * 128], idn2[0:2, 0:2])
        kb.tt("dve", g.mod[l], bank[:, 0:96].re("p (m j) -> p m j", j=2),
              g.pcol[l][:, C_BMOD:C_BMOD + 48].re("p (m o) -> p m o", o=1).bc([128, 48, 2]), ALU.add)
    kb.release(mk)
    g.gs1, g.gs2 = [], []
    for l in range(NL):
        for which, cg0, coln, lst in ((1, 8, C_N1G, g.gs1), (2, 32, C_N2G, g.gs2)):
            t = kb.alloc(f"gs{which}_{l}", [128, 8, 2])
            kb.ts("dve", t, g.mod[l][:, cg0:cg0 + 8, :], 1.0, ALU.add)
            kb.tt("dve", t, t, g.pcol[l][:, coln:coln + 8].re("p (m o) -> p m o", o=1).bc([128, 8, 2]), ALU.mult)
            lst.append(t)


def modcol(g, l, which, m, j):
    return g.mod[l][:, which * 8 + m, j:j + 1]


def rmsnorm_mod(kb, g, xs, tc, gs, l, which_sh, j, hx, tmp):
    sq, rs, xn = tmp["sq"], tmp["rs"], tmp["xn"]
    if not tmp.get("presquared"):
        kb.act(sq[:, :, 0:tc], xs[:, :, 0:tc], AF.Square)
    bank = kb.bank()
    kb.mm(bank[:, 0:tc], [(g.ones, sq[:, k, 0:tc]) for k in range(8)])
    kb.act(rs[:, 0:tc], bank[:, 0:tc], AF.Sqrt, bias=g.epsc, scale=1.0 / D)
    kb.recip(rs[:, 0:tc], rs[:, 0:tc])
    kb.tt("dve", xn[:, :, 0:tc], xs[:, :, 0:tc], rs[:, 0:tc].re("p (o t) -> p o t", o=1).bc([128, 8, tc]), ALU.mult)
    for k in range(8):
        kb.act(hx[:, k, 0:tc], xn[:, k, 0:tc], AF.Identity, bias=modcol(g, l, which_sh, k, j), scale=gs[:, k, j:j + 1])


def stage1(kb, g, l, seq):
    j = 1 if seq == "ctx" else 0
    tlen = TX if j else T
    tc = min(512, tlen)
    src = g.xcur[seq]
    dst = g.P[seq]
    VT = g.VT[seq]
    mk = kb.mark()
    vts = [kb.alloc(f"s1vt{i}", [128, 4, 256], BF16) for i in range(2)]
    xs = [kb.alloc(f"s1xs{i}", [128, 8, tc]) for i in range(2)]
    hx = [kb.alloc(f"s1hx{i}", [128, 8, tc], BF16) for i in range(2)]
    tmp = {"sq": kb.alloc("s1sq", [128, 8, tc], BF16), "rs": kb.alloc("s1rs", [128, tc]),
           "xn": kb.alloc("s1xn", [128, 8, tc])}
    pst = [kb.alloc(f"s1pst{i}", [128, 11, tc]) for i in range(2)]
    nch = tlen // tc

    def load(c):
        kb.dma("sp", xs[c % 2], Buf(src.ap[:, c * tc:(c + 1) * tc].rearrange("(k p) t -> p k t", p=128),
                                    chunk_res(src, c * tc, (c + 1) * tc, RC)))
    load(0)
    if nch > 1:
        load(1)
    rmsnorm_mod(kb, g, xs[0], tc, g.gs1[l], l, 0, j, hx[0], tmp)
    tmp["presquared"] = True
    if nch > 1:
        kb.act(tmp["sq"], xs[1], AF.Square)
    for c in range(nch):
        if c + 1 < nch:
            rmsnorm_mod(kb, g, xs[(c + 1) % 2], tc, g.gs1[l], l, 0, j, hx[(c + 1) % 2], tmp)
        if c + 2 < nch:
            load(c + 2)
            kb.act(tmp["sq"], xs[c % 2], AF.Square)
        h = hx[c % 2]
        ps = pst[c % 2]
        for m in range(11):
            bank = kb.bank()
            kb.mm(bank[:, 0:tc], [(g.win[:, k, m * 128:(m + 1) * 128], h[:, k, :]) for k in range(8)])
            kb.copy("act" if m % 2 else "dve", ps[:, m, :], bank[:, 0:tc])
        kb.dma("sp", Buf(dst.ap[:, c * tc:(c + 1) * tc].rearrange("(m p) t -> p m t", p=128),
                         chunk_res(dst, c * tc, (c + 1) * tc, RC)), ps)
        for b4 in range(tc // 128):
            blk = c * (tc // 128) + b4
            bank = kb.bank()
            kb.mm(bank[:, 0:256], [(h[:, k, b4 * 128:(b4 + 1) * 128], g.win[:, k, 1152:1408]) for k in range(8)])
            kb.copy("act", vts[c % 2][:, b4, :], bank[:, 0:256])
        nb4 = tc // 128
        kb.dma("sp", Buf(VT.ap[:, c * nb4:(c + 1) * nb4, :], VT.res), vts[c % 2][:, 0:nb4, :])
    kb.release(mk)


def build_program(stop="all", dbg=()):
    nc = bass.Bass("TRN2", target_bir_lowering=False)
    g = G()
    g.d_xT = dram_in(nc, "xT", [D, T]); g.d_xT.res = [Res(f"xT{i}") for i in range(T // RC)]
    g.d_ctxT = dram_in(nc, "ctxT", [D, TX])
    g.d_cc = dram_in(nc, "cc", [128, 16])
    g.d_misc = dram_in(nc, "misc", [128, NMISC])
    d_pcol = dram_in(nc, "pcol", [NL, 128, NCOL]); g.d_pcol = [d_pcol[l] for l in range(NL)]
    d_wmod = dram_in(nc, "w_mod", [NL, D, 6 * D]); g.d_wmod = [d_wmod[l] for l in range(NL)]
    g.d_win = dram_in(nc, "w_in", [NL, D, DIN])
    g.d_wout = dram_in(nc, "w_out", [NL, D, D])
    g.d_lruw = dram_in(nc, "lruw", [NL, 2, 2, 6, 64, 64])
    g.d_gluw = dram_in(nc, "glu_w", [NL, 384, 384])
    g.d_poolw = dram_in(nc, "pool_w", [NL, 4, 64, 64])
    g.d_wg = dram_in(nc, "ffn_w_gate", [NL, D, DFF])
    g.d_wu = dram_in(nc, "ffn_w_up", [NL, D, DFF])
    g.d_wd = dram_in(nc, "ffn_w_down", [NL, DFF, D])
    g.d_s5p = dram_in(nc, "s5p", [NL, 2, 128, 72])
    g.d_s5b = dram_in(nc, "s5b", [NL, 128, 768])
    g.d_s5c = dram_in(nc, "s5c", [NL, 2, 128, 768])
    g.d_plat = dram_in(nc, "plat", list(PLAT.shape))
    g.d_pctx = dram_in(nc, "pctx", list(PCTX.shape))
    g.d_rinv = dram_in(nc, "rinv", [2, 128, 128])
    g.d_out = Buf(nc.dram_tensor("outT", [D, T], F32, kind="ExternalOutput").ap(), [Res(f"out{i}") for i in range(T // RC)])
    g.P = {"lat": dram_tmp(nc, "P_lat", [DIN, T], F32, T // RC, dbg), "ctx": dram_tmp(nc, "P_ctx", [DIN, TX], F32, 1, dbg)}
    g.YM = {"lat": dram_tmp(nc, "YM_lat", [D, T], BF16, T // RC, dbg), "ctx": dram_tmp(nc, "YM_ctx", [D, TX], BF16, 1, dbg)}
    g.ZS = {"lat": dram_tmp(nc, "ZS_lat", [384, T], BF16, T // RC, dbg), "ctx": dram_tmp(nc, "ZS_ctx", [384, TX], BF16, 1, dbg)}
    g.X1 = {"lat": dram_tmp(nc, "X1_lat", [D, T], F32, T // RC, dbg), "ctx": dram_tmp(nc, "X1_ctx", [D, TX], F32, 1, dbg)}
    g.VT = {"lat": dram_tmp(nc, "VT_lat", [128, T // 128, 256], BF16, 1, dbg), "ctx": dram_tmp(nc, "VT_ctx", [128, TX // 128, 256], BF16, 1, dbg)}
    with ExitStack() as es:
        big = es.enter_context(nc.sbuf_tensor("big", [128, SBW], F32))
        psums = [es.enter_context(nc.psum_tensor(f"ps{i}", [128, 1024], F32)) for i in range(4)]
        sems = {e: es.enter_context(nc.semaphore(f"s_{e}")) for e in ENG}
        kb = KB(nc, big, psums)
        dsems = [es.enter_context(nc.semaphore(f"d{i}")) for i in range(kb.ndma)]
        emit_all(kb, g, stop, dbg)
        kb.barrier()
        kb.replay(sems, dsems)
    g.kb = kb
    return nc, g


def emit_all(kb, g, stop, dbg):
    stage0(kb, g)
    g.xcur = {"lat": g.d_xT, "ctx": g.d_ctxT}
    for l in range(NL):
        last = l == NL - 1
        lmark = kb.mark()
        g.h0 = kb.alloc("h0", [128, 6])
        s1mark = kb.mark()
        g.win = kb.alloc("win", [128, 8, DIN], BF16)
        for k in range(8):
            kb.dma("pool", g.win[:, k, :], g.d_win[l][k * 128:(k + 1) * 128, :])
        stage1(kb, g, l, "ctx")
        stage1(kb, g, l, "lat")
        kb.release(s1mark)
        if stop == "s1":
            return
        stage2(kb, g, l, ctx_out=not last)
        if stop == "s2":
            return
        stage3(kb, g, l, ctx_out=not last)
        if stop == "s3":
            return
        stage4(kb, g, l, ctx_out=not last)
        kb.release(g.midmark)
        if stop == "s4":
            return
        stage56(kb, g, l, ("lat",) if last else ("ctx", "lat"), last)
        if stop == "l0":
            return
        g.xcur = dict(g.X1)
        kb.release(lmark)


def run(inputs, stop="all", dbg=(), cores=8, trace=False):
    nc, g = build_program(stop, dbg)
    sh = pack_shared(inputs)
    in_maps = []
    for cid in range(cores):
        m = dict(sh)
        m.update(pack_core(inputs, cid % 4))
        in_maps.append(m)
    res = run_bass_kernel_spmd(nc, in_maps, core_ids=list(range(cores)), trace=trace)
    return res, g


def kernel(**inputs):
    res, g = run(inputs)
    out = np.stack([np.ascontiguousarray(res.results[b]["outT"].T) for b in range(4)], axis=0)
    return out.astype(np.float32)


def stage2(kb, g, l, ctx_out):
    pc = g.pcol[l]
    mk = kb.mark()
    lw = kb.alloc("lruw", [128, 12, 128], BF16)
    kb.memset("pool", lw, 0.0)
    for d in range(2):
        for gate in range(2):
            for j3 in range(3):
                for hh in range(2):
                    kb.dma("pool", lw[hh * 64:(hh + 1) * 64, (d * 2 + gate) * 3 + j3, hh * 64:(hh + 1) * 64],
                           g.d_lruw[l, d, gate, 2 * j3 + hh])
    cst = kb.alloc("lrucst", [128, 18])
    hcl, hba, hbi = cst[:, 0:6], cst[:, 6:12], cst[:, 12:18]
    kb.act(hcl, pc[:, C_LAM:C_LAM + 6], AF.Exp, scale=-1.0)
    kb.ts("dve", hcl, hcl, 1.0, ALU.add)
    kb.act(hcl, hcl, AF.Ln)
    kb.ts("dve", hcl, hcl, -4.0, ALU.mult)
    kb.ts("dve", hba, pc[:, C_BA:C_BA + 6], 0.5, ALU.mult)
    kb.ts("dve", hbi, pc[:, C_BI:C_BI + 6], 0.5, ALU.mult)
    q25 = kb.alloc("q25", [128, 1])
    kb.memset("pool", q25, 0.25)
    xpad_f = kb.alloc("xpad", [128, T + 8], F32, nres=T // 1024)
    xc_f = kb.alloc("xc", [128, T], F32, nres=T // 1024)
    xcb_f = kb.alloc("xcb", [128, T], BF16, nres=T // 1024)
    ring_f = [[kb.alloc(f"lr{n}{i}", [128, 1024]) for i in range(4)] for n in ("A", "M", "B", "H", "G")]
    yab_f = [kb.alloc(f"yab{i}", [128, 1024], BF16) for i in range(2)]
    carry = kb.alloc("carry", [128, 2])
    for j3 in range(3):
        rows = slice(j3 * 128, (j3 + 1) * 128)
        for seq in ("ctx", "lat"):
            isc = seq == "ctx"
            tl = TX if isc else T
            tcl = min(1024, tl)
            nch = tl // tcl
            want_out = (not isc) or ctx_out
            xpad = Buf(xpad_f.ap[:, 0:tl + 8], xpad_f.res[0:nch])
            xc = Buf(xc_f.ap[:, 0:tl], xc_f.res[0:nch])
            xcb = Buf(xcb_f.ap[:, 0:tl], xcb_f.res[0:nch])
            hf = Buf(xpad.ap[:, 0:tl], xpad.res)
            ring = [[b[:, 0:tcl] for b in r] for r in ring_f]
            yab = [b[:, 0:tcl] for b in yab_f]
            P = g.P[seq]

            def cres(buf, c0, c1):
                return buf.res[max(c0, 0):min(c1, nch - 1) + 1]
            kb.memset("pool", Buf(xpad.ap[:, 0:2], xpad.res[0]), 0.0)
            kb.memset("pool", Buf(xpad.ap[:, tl + 2:tl + 4], xpad.res[nch - 1]), 0.0)
            for c in range(nch):
                kb.dma("sp", Buf(xpad.ap[:, 2 + c * tcl:2 + (c + 1) * tcl], xpad.res[c]),
                       Buf(P.ap[rows, c * tcl:(c + 1) * tcl], chunk_res(P, c * tcl, (c + 1) * tcl, RC)))
            for c in range(nch):
                cs = slice(c * tcl, (c + 1) * tcl)
                xcc = Buf(xc.ap[:, cs], xc.res[c])
                src = lambda k: Buf(xpad.ap[:, c * tcl + k:c * tcl + k + tcl], cres(xpad, c - 1, c + 1))
                kb.act(xcc, src(0), AF.Identity, bias=pc[:, C_CONVB + j3:C_CONVB + j3 + 1],
                       scale=pc[:, C_CONVW + j3 * 4:C_CONVW + j3 * 4 + 1])
                for k in range(1, 4):
                    kb.stt(xcc, src(k), pc[:, C_CONVW + j3 * 4 + k:C_CONVW + j3 * 4 + k + 1], xcc, ALU.mult, ALU.add)
                kb.copy("act", Buf(xcb.ap[:, cs], xcb.res[c]), xcc)
            for d in range(2):
                order = list(range(nch)) if d == 0 else list(range(nch - 1, -1, -1))
                ci = d * 3 + j3
                for p0 in range(0, nch, 2):
                    pair = order[p0:p0 + 2]
                    its = list(range(p0, p0 + len(pair)))
                    bufs = {it: tuple(r[it % 4] for r in ring) for it in its}
                    csl = {c: slice(c * tcl, (c + 1) * tcl) for c in pair}
                    xcc = {c: Buf(xc.ap[:, csl[c]], xc.res[c]) for c in pair}
                    pr, pi = {}, {}
                    for c in pair:
                        xbb = Buf(xcb.ap[:, csl[c]], xcb.res[c])
                        pr[c], pi[c] = kb.bank(2), kb.bank(2)
                        for hh in range(0, tcl, 512):
                            w_ = min(512, tcl - hh)
                            kb.mm(pr[c][:, hh:hh + w_], [(lw[:, (d * 2 + 0) * 3 + j3, :], xbb[:, hh:hh + w_])])
                            kb.mm(pi[c][:, hh:hh + w_], [(lw[:, (d * 2 + 1) * 3 + j3, :], xbb[:, hh:hh + w_])])
                    for it, c in zip(its, pair):
                        A, M, B, H, Gt = bufs[it]
                        kb.act(A, pr[c][:, 0:tcl], AF.Tanh, bias=hba[:, ci:ci + 1], scale=0.5)
                        kb.act(B, pi[c][:, 0:tcl], AF.Tanh, bias=hbi[:, ci:ci + 1], scale=0.5)
                    for it, c in zip(its, pair):
                        A, M, B, H, Gt = bufs[it]
                        kb.act(A, A, AF.Exp, bias=hcl[:, ci:ci + 1], scale=hcl[:, ci:ci + 1])
                        kb.tt("dve", M, A, A, ALU.mult)
                    for it, c in zip(its, pair):
                        A, M, B, H, Gt = bufs[it]
                        kb.act(M, M, AF.Sqrt, bias=q25, scale=-0.25)
                    if d == 1 and want_out:
                        for it, c in zip(its, pair):
                            A, M, B, H, Gt = bufs[it]
                            kb.dma("sp", Gt, Buf(P.ap[768 + j3 * 128:768 + (j3 + 1) * 128, csl[c]],
                                                 chunk_res(P, c * tcl, (c + 1) * tcl, RC)))
                            kb.act(Gt, Gt, AF.Gelu_apprx_tanh)
                    for it, c in zip(its, pair):
                        A, M, B, H, Gt = bufs[it]
                        cs = csl[c]
                        kb.stt(B, B, 1.0, xcc[c], ALU.add, ALU.mult)
                        kb.tt("dve", B, B, M, ALU.mult)
                        if d == 0:
                            init = 0.0 if isc else g.h0[:, j3 * 2:j3 * 2 + 1]
                            if c > 0:
                                init = Buf(hf.ap[:, c * tcl - 1:c * tcl], hf.res[max(c - 2, 0):c])
                            kb.scan(Buf(hf.ap[:, cs], hf.res[max(c - 1, 0):c + 1]), A, B, init)
                            if isc and c == nch - 1:
                                kb.copy("dve", g.h0[:, j3 * 2:j3 * 2 + 1], Buf(hf.ap[:, tl - 1:tl], hf.res[max(c - 1, 0):c + 1]))
                        else:
                            init = (0.0 if isc else g.h0[:, j3 * 2 + 1:j3 * 2 + 2]) if it == 0 else carry[:, 0:1]
                            kb.scan(H[:, ::-1], A[:, ::-1], B[:, ::-1], init)
                            kb.copy("dve", carry[:, 0:1], H[:, 0:1])
                            if isc and c == 0:
                                kb.copy("dve", g.h0[:, j3 * 2 + 1:j3 * 2 + 2], H[:, 0:1])
                            if want_out:
                                kb.tt("dve", H, H, Buf(hf.ap[:, cs], hf.res[max(c - 1, 0):c + 1]), ALU.add)
                                kb.tt("pool", yab[it % 2], Gt, H, ALU.mult)
                                YM = g.YM[seq]
                                kb.dma("sp", Buf(YM.ap[rows, cs], chunk_res(YM, c * tcl, (c + 1) * tcl, RC)), yab[it % 2])
    kb.release(mk)


def stage3(kb, g, l, ctx_out):
    pc = g.pcol[l]
    mk = kb.mark()
    E = "dve"
    g.misc = kb.alloc("misc", [128, NMISC])
    kb.dma("sp", g.misc, g.d_misc)
    g.ident = g.misc[:, M_ID:M_ID + 128]
    g.bdmask = g.misc[:, M_BD:M_BD + 128]
    g.colmask = g.misc[:, M_CM:M_CM + 512].re("p (e c) -> p e c", e=4)
    g.ii = g.misc[:, M_II:M_II + 64]
    g.evmask = g.misc[:, M_EV:M_EV + 4]
    g.gmask = g.misc[:, M_GM:M_GM + 8]
    keep = []
    for d in range(2):
        keep.append([kb.alloc(f"s5pw{d}", [128, 17, 2, 24]), kb.alloc(f"s5sv{d}", [128, 10, 2, 24]),
                     kb.alloc(f"s5bst{d}", [128, 384]), kb.alloc(f"s5bsw{d}", [128, 384]),
                     kb.alloc(f"s5cst{d}", [128, 384]), kb.alloc(f"s5csw{d}", [128, 384])])
    diagD = kb.alloc("diagD", [128, 3, 128])
    mkt = kb.mark()
    s5b = kb.alloc("s5b", [128, 2, 384])
    kb.dma("sp", s5b, g.d_s5b[l].re("p (r c) -> p r c", r=2))
    halfpi = kb.alloc("halfpi", [128, 1])
    kb.memset("pool", halfpi, math.pi / 2)
    for j3 in range(3):
        kb.ts("dve", diagD[:, j3, :], g.ident, pc[:, C_S5D + j3:C_S5D + j3 + 1], ALU.mult)
    Pw, Sv, Bst, Bsw, Cst, Csw = [], [], [], [], [], []
    for d in range(2):
        prm = kb.alloc(f"s5prm{d}", [128, 72])
        kb.dma("sp", prm, g.d_s5p[l, d])
        s5c = kb.alloc(f"s5c{d}", [128, 2, 384])
        kb.dma("sp", s5c, g.d_s5c[l, d].re("p (r c) -> p r c", r=2))
        lr, li, ldt = prm[:, 0:24], prm[:, 24:48], prm[:, 48:72]
        tm = kb.alloc(f"s5tm{d}", [128, 12, 24])
        t = lambda i: tm[:, i, :]
        pw, sv, bst, bsw, cst, csw = keep[d]
        qw = kb.alloc(f"s5qw{d}", [128, 10, 2, 24])
        mul = lambda o, a, b: kb.tt(E, o, a, b, ALU.mult)
        add = lambda o, a, b: kb.tt(E, o, a, b, ALU.add)
        sub = lambda o, a, b: kb.tt(E, o, a, b, ALU.subtract)

        def cmul(outr, outi, ar, ai, br, bi):
            mul(t(10), ar, br); mul(t(11), ai, bi); sub(outr, t(10), t(11))
            mul(t(10), ar, bi); mul(t(11), ai, br); add(outi, t(10), t(11))
        kb.act(t(0), ldt, AF.Exp)
        mul(t(1), lr, t(0))
        mul(t(2), li, t(0))
        kb.act(t(3), t(1), AF.Exp, scale=1.0 / 16)
        kb.act(t(4), t(2), AF.Sin, scale=1.0 / 16)
        kb.act(t(5), t(2), AF.Sin, bias=halfpi, scale=1.0 / 16)
        mul(t(6), t(3), t(5)); mul(t(7), t(3), t(4))
        for it in range(4):
            cmul(t(8), t(9), t(6), t(7), t(6), t(7))
            kb.copy(E, t(6), t(8)); kb.copy(E, t(7), t(9))
        kb.memset(E, pw[:, 0, 0, :], 1.0); kb.memset(E, pw[:, 0, 1, :], 0.0)
        kb.copy(E, pw[:, 1, 0, :], t(6)); kb.copy(E, pw[:, 1, 1, :], t(7))
        for m in range(2, 17):
            cmul(pw[:, m, 0, :], pw[:, m, 1, :], pw[:, m - 1, 0, :], pw[:, m - 1, 1, :], pw[:, 1, 0, :], pw[:, 1, 1, :])
        kb.copy(E, qw[:, 0, 0, :], pw[:, 16, 0, :]); kb.copy(E, qw[:, 0, 1, :], pw[:, 16, 1, :])
        for i in range(1, 10):
            cmul(qw[:, i, 0, :], qw[:, i, 1, :], qw[:, i - 1, 0, :], qw[:, i - 1, 1, :], qw[:, i - 1, 0, :], qw[:, i - 1, 1, :])
        kb.copy(E, sv[:, :, 0, :], qw[:, :, 0, :]); kb.copy(E, sv[:, :, 1, :], qw[:, :, 1, :])
        kb.ts("dve", sv[64:128, :, 0, :], qw[64:128, :, 1, :], -1.0, ALU.mult)
        kb.copy("dve", sv[64:128, :, 1, :], qw[64:128, :, 0, :])
        mul(t(0), lr, lr); mul(t(1), li, li); add(t(0), t(0), t(1)); kb.recip(t(0), t(0))
        kb.ts("dve", t(1), pw[:, 1, 0, :], -1.0, ALU.add)
        mul(t(2), t(1), lr); mul(t(3), pw[:, 1, 1, :], li); add(t(2), t(2), t(3)); mul(t(2), t(2), t(0))
        mul(t(3), pw[:, 1, 1, :], lr); mul(t(4), t(1), li); sub(t(3), t(3), t(4)); mul(t(3), t(3), t(0))
        frb = t(2).re("p (g o) -> p g o", o=1).bc([128, 24, 16])
        fib = t(3).re("p (g o) -> p g o", o=1).bc([128, 24, 16])
        bb = kb.alloc(f"s5bb{d}", [128, 4, 384])
        v3 = lambda b: b.re("p (g k) -> p g k", k=16)
        Bre, Bim = v3(s5b[:, 0, :]), v3(s5b[:, 1, :])
        mul(v3(bb[:, 0, :]), Bre, frb); mul(v3(bb[:, 1, :]), Bim, fib); sub(bb[:, 0, :], bb[:, 0, :], bb[:, 1, :])
        mul(v3(bb[:, 1, :]), Bim, frb); mul(v3(bb[:, 2, :]), Bre, fib); add(bb[:, 1, :], bb[:, 1, :], bb[:, 2, :])
        kb.copy(E, bst, bb[:, 0, :]); kb.copy("dve", bst[64:128, :], bb[64:128, 1, :])
        kb.copy(E, bsw, bb[:, 0, :]); kb.ts("dve", bsw[0:64, :], bb[0:64, 1, :], -1.0, ALU.mult)
        kb.copy(E, cst, s5c[:, 0, :]); kb.ts("dve", cst[64:128, :], s5c[64:128, 1, :], -1.0, ALU.mult)
        kb.ts("dve", csw, s5c[:, 0, :], -1.0, ALU.mult); kb.ts("dve", csw[0:64, :], s5c[0:64, 1, :], -1.0, ALU.mult)
        Pw.append(pw); Sv.append(sv); Bst.append(bst); Bsw.append(bsw); Cst.append(cst); Csw.append(csw)

    kb.release(mkt)
    A = kb.alloc("s5A", [128, 16, 128])
    CRt = kb.alloc("s5CRt", [128, 16, 128])
    AT1 = kb.alloc("s5AT", [128, 16, 4, 128], BF16)
    AT = [AT1, AT1]
    CR = [kb.alloc(f"s5CR{d}", [128, 16, 4, 128], BF16) for d in range(2)]
    KF = [kb.alloc(f"s5KF{d}", [128, 16, 128], BF16) for d in range(2)]
    RD = kb.alloc("s5RD", [128, 8, 10, 128], BF16, nres=8)
    NZ = {"lat": T // 16, "ctx": TX // 16}
    Z1 = {s_: [kb.alloc(f"Z{s_}{q}", [128, NZ[s_] + 8]) for q in range(8)] for s_ in ("lat", "ctx")}
    Z = {s_: [Z1[s_], Z1[s_]] for s_ in ("lat", "ctx")}
    hfin = kb.alloc("s5hfin", [128, 8])
    Zb = {s_: [[kb.alloc(f"Zb{s_}{d}{q}", [128, NZ[s_] + 8], BF16) for q in range(8)] for d in range(2)] for s_ in ("lat", "ctx")}
    uB = {"lat": kb.alloc("uBl", [128, 16, T // 16], BF16), "ctx": kb.alloc("uBc", [128, 16, TX // 16], BF16)}
    zst = {"lat": kb.alloc("zstl", [128, T], BF16), "ctx": kb.alloc("zstc", [128, TX], BF16)}
    uld = [kb.alloc(f"uld{i}", [128, 1024]) for i in range(2)]
    v4 = lambda b: b.re("p m (g k) -> p m g k", k=16)
    bcm = lambda b, n: b.re("p (o g k) -> p o g k", o=1, k=16).bc([128, n, 8, 16])

    def prb(d, j3_, ri, m0, m1):
        return Pw[d][:, m0:m1, ri, j3_ * 8:j3_ * 8 + 8].re("p m (g o) -> p m g o", o=1).bc([128, m1 - m0, 8, 16])

    def gen_A(d, j3_):
        cols_ = slice(j3_ * 128, (j3_ + 1) * 128)
        kb.tt("dve", v4(A), bcm(Bst[d][:, cols_], 16), prb(d, j3_, 0, 0, 16), ALU.mult)
        kb.tt("pool", v4(CRt), bcm(Bsw[d][:, cols_], 16), prb(d, j3_, 1, 0, 16), ALU.mult)
        kb.tt("dve", A, A, CRt, ALU.add)

    def gen_AT(d):
        for m4 in range(0, 16, 4):
            bank = kb.bank()
            for m in range(m4, m4 + 4):
                kb.transpose(bank[:, (m - m4) * 128:(m - m4 + 1) * 128], A[:, m, :], g.ident)
            b3 = bank.re("p (m c) -> p m c", m=4)
            for v in range(4):
                if v % 2:
                    kb.act(AT[d][:, m4:m4 + 4, v, :], b3, AF.Copy, scale=g.evmask[:, v:v + 1])
                else:
                    kb.ts("dve", AT[d][:, m4:m4 + 4, v, :], b3, g.evmask[:, v:v + 1], ALU.mult)

    def gen_KF(d, j3_):
        cols_ = slice(j3_ * 128, (j3_ + 1) * 128)
        for m4 in range(0, 16, 4):
            bank2 = kb.bank()
            for m in range(m4, m4 + 4):
                kb.mm(bank2[:, (m - m4) * 128:(m - m4 + 1) * 128], [(A[:, m, :], Cst[d][:, cols_])])
            kb.tt("dve", KF[d][:, m4:m4 + 4, :], bank2.re("p (m c) -> p m c", m=4),
                  g.bdmask.re("p (o c) -> p o c", o=1).bc([128, 4, 128]), ALU.mult)
        if d == 0:
            kb.tt("pool", KF[0][:, 0, :], KF[0][:, 0, :], diagD[:, j3_, :], ALU.add)

    def gen_CR(d, j3_):
        cols_ = slice(j3_ * 128, (j3_ + 1) * 128)
        kb.tt("dve", v4(CRt), bcm(Cst[d][:, cols_], 16), prb(d, j3_, 0, 1, 17), ALU.mult)
        kb.tt("pool", v4(A), bcm(Csw[d][:, cols_], 16), prb(d, j3_, 1, 1, 17), ALU.mult)
        kb.tt("dve", CRt, CRt, A, ALU.add)
        for e in range(4):
            kb.tt("pool" if e % 2 else "dve", CR[d][:, :, e, :], CRt,
                  g.colmask[:, e, :].re("p (o c) -> p o c", o=1).bc([128, 16, 128]), ALU.mult)

    def gen_RD(d, j3_):
        for gl in range(8):
            rd = Buf(RD.ap[:, gl, :, :], RD.res[gl])
            for hh in range(2):
                svb = Sv[d][:, :, hh, j3_ * 8 + gl].re("p (i o) -> p i o", o=1).bc([128, 10, 64])
                kb.tt("pool" if hh else "dve", rd[:, :, hh * 64:(hh + 1) * 64],
                      g.ii.re("p (o c) -> p o c", o=1).bc([128, 10, 64]), svb, ALU.mult)

    def do_S(d):
        for s_ in ("ctx", "lat"):
            n = NZ[s_]
            for gl in range(8):
                q, e = gl // 4, gl % 4
                z = Z[s_][d][gl]
                bank = kb.bank()
                kb.mm(bank[:, 0:n], [(AT[d][64 * q:64 * q + 64, (15 - tau) if d == 0 else tau, e, :],
                                      uB[s_][64 * q:64 * q + 64, tau, :]) for tau in range(16)])
                dst = z[:, 1:n + 1] if d == 0 else z[:, n:0:-1]
                kb.copy("act", dst, bank[:, 0:n])

    def do_dbl(d):
        for s_ in ("ctx", "lat"):
            n = NZ[s_]
            W = n + 1 if s_ == "ctx" else n
            banks = []
            for gl in range(8):
                z, zb = Z[s_][d][gl], Zb[s_][d][gl]
                if s_ == "ctx":
                    kb.memset("pool", z[:, 0:1], 0.0)
                else:
                    kb.copy("pool", z[:, 0:1], hfin[:, gl:gl + 1])
                bank = kb.bank()
                banks.append(bank)
                kb.mm(bank[:, 0:W], [(g.ident, z[:, 0:W])])
                kb.copy("act" if gl % 2 else "dve", zb[:, 0:W], bank[:, 0:W])
            i, s_t = 0, 1
            while s_t < W:
                N = W - s_t
                for gl in range(8):
                    zb = Zb[s_][d][gl]
                    kb.mm(banks[gl][:, s_t:W], [(Buf(RD.ap[:, gl, i, :], RD.res[gl]), zb[:, 0:N])], first_start=False)
                    kb.copy("act" if (gl + i) % 2 else "dve", zb[:, s_t:W], banks[gl][:, s_t:W])
                i += 1
                s_t *= 2
            if s_ == "ctx":
                for gl in range(8):
                    kb.copy("act" if gl % 2 else "dve", hfin[:, gl:gl + 1], banks[gl][:, n:n + 1])
    seqs = ("ctx", "lat")
    for j3 in range(3):
        cols = slice(j3 * 128, (j3 + 1) * 128)
        rows = slice(384 + j3 * 128, 384 + (j3 + 1) * 128)
        it = 0
        for s_ in seqs:
            tl = TX if s_ == "ctx" else T
            for c0 in range(0, tl, 1024):
                w_ = min(1024, tl - c0)
                kb.dma("sp", uld[it % 2][:, 0:w_], Buf(g.P[s_].ap[rows, c0:c0 + w_], chunk_res(g.P[s_], c0, c0 + w_, RC)))
                kb.copy("act" if it % 2 else "dve", uB[s_][:, :, c0 // 16:(c0 + w_) // 16],
                        uld[it % 2][:, 0:w_].re("p (c t) -> p t c", t=16))
                it += 1
        if j3 == 0:
            gen_A(0, 0); gen_AT(0); gen_KF(0, 0); gen_CR(0, 0); gen_RD(0, 0)
        nxt = j3 + 1 < 3
        gen_CR(1, j3); gen_A(1, j3)
        do_S(0); gen_KF(1, j3); gen_AT(1); do_dbl(0); gen_RD(1, j3)
        if nxt:
            gen_A(0, j3 + 1)
        do_S(1)
        if nxt:
            gen_AT(0)
        do_dbl(1)
        if nxt:
            gen_RD(0, j3 + 1)
        for s_ in seqs:
            if s_ == "ctx" and not ctx_out:
                continue
            n = NZ[s_]
            tl = n * 16
            for tau in range(16):
                bank = kb.bank()
                prs = [(KF[0][:, j, :], uB[s_][:, tau - j, :]) for j in range(tau + 1)]
                prs += [(KF[1][:, j, :], uB[s_][:, tau + j, :]) for j in range(16 - tau)]
                for gl in range(8):
                    q, e = gl // 4, gl % 4
                    o = bank[64 * q:64 * q + 64, 0:n]
                    prs.append((CR[0][:, tau, e, 64 * q:64 * q + 64], Zb[s_][0][gl][:, 0:n], o))
                    prs.append((CR[1][:, 15 - tau, e, 64 * q:64 * q + 64], Zb[s_][1][gl][:, n - 1::-1], o))
                kb.mm(bank[:, 0:n], prs)
                kb.act(zst[s_][:, tau::16], bank[:, 0:n], AF.Gelu_apprx_tanh)
            ZS = g.ZS[s_]
            kb.dma("sp", Buf(ZS.ap[j3 * 128:(j3 + 1) * 128, :], ZS.res), zst[s_])
        if nxt:
            gen_KF(0, j3 + 1); gen_CR(0, j3 + 1)
    kb.release(mk)
    alloc_ffn_weights(kb, g)
    g.midmark = kb.mark()
    gw = kb.alloc("gluw", [128, 3, 384], BF16)
    for k in range(3):
        kb.dma("pool", gw[:, k, :], g.d_gluw[l][k * 128:(k + 1) * 128, :])
    g.plat = kb.alloc("plat", [128, PLAT.shape[0], 128], BF16)
    kb.dma("pool", g.plat, g.d_plat.re("n p c -> p n c"))
    g.pctx = kb.alloc("pctx", [128, PCTX.shape[0], 128], BF16)
    kb.dma("pool", g.pctx, g.d_pctx.re("n p c -> p n c"))
    g.poolw = kb.alloc("poolw", [128, 2, 128], BF16)
    kb.memset("pool", g.poolw, 0.0)
    for pt in range(2):
        for hh in range(2):
            kb.dma("pool", g.poolw[hh * 64:(hh + 1) * 64, pt, hh * 64:(hh + 1) * 64], g.d_poolw[l, 2 * pt + hh])
    load_ffn_weights(kb, g, l)
    seqs = ("ctx", "lat")
    mk = kb.mark()
    zc = [kb.alloc(f"gluz{i}", [128, 3, 512], BF16) for i in range(3)]
    sg = [kb.alloc(f"glus{i}", [128, 512]) for i in range(3)]
    yb = [kb.alloc(f"gluy{i}", [128, 3, 512], BF16) for i in range(2)]
    work = []
    for s_ in seqs:
        if s_ == "ctx" and not ctx_out:
            continue
        tl = TX if s_ == "ctx" else T
        tc = min(512, tl)
        work += [(s_, c, tc) for c in range(tl // tc)]

    def gload(i):
        s_, c, tc = work[i]
        ZS = g.ZS[s_]
        kb.dma("sp", zc[i % 3][:, :, 0:tc], Buf(ZS.ap[:, c * tc:(c + 1) * tc].rearrange("(k p) t -> p k t", p=128),
                                               chunk_res(ZS, c * tc, (c + 1) * tc, RC)))
    for i in range(min(2, len(work))):
        gload(i)
    for it, (s_, c, tc) in enumerate(work):
        if it + 2 < len(work):
            gload(it + 2)
        zz, yy = zc[it % 3], yb[it % 2]
        YM = g.YM[s_]
        for mo in range(3):
            bank = kb.bank()
            kb.mm(bank[:, 0:tc], [(gw[:, k, mo * 128:(mo + 1) * 128], zz[:, k, 0:tc]) for k in range(3)])
            kb.act(sg[mo][:, 0:tc], bank[:, 0:tc], AF.Sigmoid, bias=pc[:, C_GLUB + mo:C_GLUB + mo + 1])
            kb.tt("dve", yy[:, mo, 0:tc], zz[:, mo, 0:tc], sg[mo][:, 0:tc], ALU.mult)
        kb.dma("sp", Buf(YM.ap[384:768, c * tc:(c + 1) * tc].rearrange("(k p) t -> p k t", p=128),
                          chunk_res(YM, c * tc, (c + 1) * tc, RC)), yy[:, :, 0:tc])
    kb.release(mk)


def stage4(kb, g, l, ctx_out):
    pc = g.pcol[l]
    mk = kb.mark()
    plat, pctx, pw = g.plat, g.pctx, g.poolw
    rinv = kb.alloc("rinv", [128, 2, 128])
    kb.dma("sp", rinv, g.d_rinv.re("t p r -> p t r"))
    sb = kb.alloc("poolsb", [128, 2])
    kb.tt("dve", sb, pc[:, C_POOLB:C_POOLB + 2], pc[:, C_POOLS:C_POOLS + 2], ALU.mult)
    vbuf = [kb.alloc(f"poolv{i}", [128, 512]) for i in range(3)]
    tbuf = [kb.alloc(f"poolt{i}", [128, 512]) for i in range(2)]
    mbuf = [kb.alloc(f"poolm{i}", [128, 512], BF16) for i in range(2)]
    ybuf = [kb.alloc(f"pooly{i}", [128, 512], BF16) for i in range(2)]
    it = 0
    for seq in ("ctx", "lat"):
        isc = seq == "ctx"
        if isc and not ctx_out:
            continue
        tl = TX if isc else T
        nb = tl // 128
        tc = min(512, tl)
        P, YM = g.P[seq], g.YM[seq]
        nhalf = 1 if isc else 2
        bper = nb // nhalf
        for half in range(nhalf):
          mk2 = kb.mark()
          blo, bhi = max(0, half * bper - 4), min(nb, (half + 1) * bper + 4)
          vt = kb.alloc("vt", [128, bhi - blo, 256], BF16)
          kb.dma("sp", vt, Buf(g.VT[seq].ap[:, blo:bhi, :], g.VT[seq].res))
          cper = (tl // tc) // nhalf
          wl = [(c, pt) for c in range(half * cper, (half + 1) * cper) for pt in range(2)]

          def vload(i, base):
              c_, pt_ = wl[i]
              kb.dma("sp", vbuf[(base + i) % 3][:, 0:tc],
                     Buf(P.ap[1152 + pt_ * 128:1152 + (pt_ + 1) * 128, c_ * tc:(c_ + 1) * tc],
                         chunk_res(P, c_ * tc, (c_ + 1) * tc, RC)))
          base = it
          for i in range(min(2, len(wl))):
              vload(i, base)
          for wi, (c, pt) in enumerate(wl):
              if True:
                  if wi + 2 < len(wl):
                      vload(wi + 2, base)
                  v, t_, m_, y_ = vbuf[it % 3], tbuf[it % 2], mbuf[it % 2], ybuf[it % 2]
                  it += 1
                  bank = kb.bank()
                  for b4 in range(tc // 128):
                      B = c * (tc // 128) + b4
                      for gl in range(2):
                          gg = 2 * pt + gl
                          prs = []
                          for dlt in range(-4, 5):
                              if not 0 <= B + dlt < nb:
                                  continue
                              if isc:
                                  idx = CTXIDX.get((gg, B, dlt))
                                  mat = None if idx is None else pctx[:, idx, :]
                              else:
                                  idx = LATIDX.get((gg, dlt))
                                  mat = None if idx is None else plat[:, idx, :]
                              if mat is not None:
                                  prs.append((vt[:, B + dlt - blo, gg * 64:(gg + 1) * 64], mat))
                          kb.mm(bank[gl * 64:(gl + 1) * 64, b4 * 128:(b4 + 1) * 128], prs)
                  if isc:
                      kb.tt("dve", m_[:, 0:tc], bank[:, 0:tc], v[:, 0:tc], ALU.subtract)
                  else:
                      r0 = c * (tc // 64)
                      kb.tt("dve", t_.re("p (r c) -> p r c", c=64), bank.re("p (r c) -> p r c", c=64),
                            rinv[:, pt, r0:r0 + tc // 64].re("p (r o) -> p r o", o=1).bc([128, tc // 64, 64]), ALU.mult)
                      kb.tt("dve", m_[:, 0:tc], t_[:, 0:tc], v[:, 0:tc], ALU.subtract)
                  bank2 = kb.bank()
                  kb.mm(bank2[:, 0:tc], [(pw[:, pt, :], m_[:, 0:tc])])
                  kb.act(y_[:, 0:tc], bank2[:, 0:tc], AF.Identity, bias=sb[:, pt:pt + 1], scale=pc[:, C_POOLS + pt:C_POOLS + pt + 1])
                  kb.dma("sp", Buf(YM.ap[768 + pt * 128:768 + (pt + 1) * 128, c * tc:(c + 1) * tc],
                                   chunk_res(YM, c * tc, (c + 1) * tc, RC)), y_[:, 0:tc])
          kb.release(mk2)
    kb.release(mk)


def alloc_ffn_weights(kb, g):
    g.ffnw = (kb.alloc("wo", [128, 8, D], BF16), kb.alloc("wg", [128, 8, DFF], BF16),
              kb.alloc("wu", [128, 8, DFF], BF16), kb.alloc("wd", [128, 22, D], BF16))


def load_ffn_weights(kb, g, l):
    wo, wg, wu, wd = g.ffnw
    for k in range(8):
        kb.dma("pool", wo[:, k, :], g.d_wout[l][k * 128:(k + 1) * 128, :])
    for k in range(8):
        kb.dma("pool", wg[:, k, :], g.d_wg[l][k * 128:(k + 1) * 128, :], max_dma_last_dim=4096)
        kb.dma("pool", wu[:, k, :], g.d_wu[l][k * 128:(k + 1) * 128, :], max_dma_last_dim=4096)
    for f in range(22):
        kb.dma("pool", wd[:, f, :], g.d_wd[l][f * 128:(f + 1) * 128, :])


def stage56(kb, g, l, seqs, last):
    pc = g.pcol[l]
    mk = kb.mark()
    tc = 256
    wo, wg, wu, wd = g.ffnw
    NX = 3
    xs = [kb.alloc(f"fxs{i}", [128, 8, tc]) for i in range(NX)]
    ym = kb.alloc("fym", [128, 8, tc], BF16)
    hx = [kb.alloc(f"fhx{i}", [128, 8, tc], BF16) for i in range(2)]
    sq1 = kb.alloc("fsq", [128, 8, tc], BF16)
    sq = [sq1, sq1]
    rs1 = kb.alloc("frs", [128, tc])
    rs = [rs1, rs1]
    xn = [kb.alloc(f"fxn{i}", [128, tc]) for i in range(2)]
    hmid = kb.alloc("fhmid", [128, 22, tc], BF16)
    sg = [kb.alloc(f"fsg{i}", [128, tc]) for i in range(2)]
    items = []
    for seq in seqs:
        tl = TX if seq == "ctx" else T
        items += [(seq, c) for c in range(tl // tc)]
    n_it = len(items)

    def jof(i):
        return 1 if items[i][0] == "ctx" else 0

    def load_x(i):
        seq, c = items[i]
        X = g.xcur[seq]
        kb.dma("sp", xs[i % NX], Buf(X.ap[:, c * tc:(c + 1) * tc].rearrange("(k p) t -> p k t", p=128),
                                    chunk_res(X, c * tc, (c + 1) * tc, RC)))

    def load_y(i):
        seq, c = items[i]
        YM = g.YM[seq]
        kb.dma("sp", ym, Buf(YM.ap[:, c * tc:(c + 1) * tc].rearrange("(k p) t -> p k t", p=128),
                             chunk_res(YM, c * tc, (c + 1) * tc, RC)))

    def wout(i):
        x, j = xs[i % NX], jof(i)
        for m in range(8):
            bank = kb.bank()
            kb.mm(bank[:, 0:tc], [(wo[:, k, m * 128:(m + 1) * 128], ym[:, k, :]) for k in range(8)])
            kb.stt(x[:, m, :], bank[:, 0:tc], modcol(g, l, 2, m, j), x[:, m, :], ALU.mult, ALU.add)

    def norm_a(i):
        kb.act(sq[i % 2], xs[i % NX], AF.Square)

    def norm_b(i, final=False):
        x, j, r_ = xs[i % NX], jof(i), rs[i % 2]
        bank = kb.bank()
        kb.mm(bank[:, 0:tc], [(g.ones, sq[i % 2][:, k, :]) for k in range(8)])
        kb.act(r_, bank[:, 0:tc], AF.Sqrt, bias=g.epsc, scale=1.0 / D)
        kb.recip(r_, r_)
        if final:
            for m in range(8):
                kb.tt("dve" if m % 2 else "pool", x[:, m, :], x[:, m, :], r_, ALU.mult)
                kb.act(x[:, m, :], x[:, m, :], AF.Copy, scale=pc[:, C_FING + m:C_FING + m + 1])
            return
        for k in range(8):
            kb.tt("dve" if k % 2 else "pool", xn[k % 2], x[:, k, :], r_, ALU.mult)
            kb.act(hx[i % 2][:, k, :], xn[k % 2], AF.Identity, bias=modcol(g, l, 3, k, j), scale=g.gs2[l][:, k, j:j + 1])

    def gateup(i):
        h = hx[i % 2]
        for f in range(22):
            bg, bu = kb.bank(), kb.bank()
            kb.mm(bg[:, 0:tc], [(wg[:, k, f * 128:(f + 1) * 128], h[:, k, :]) for k in range(8)])
            kb.mm(bu[:, 0:tc], [(wu[:, k, f * 128:(f + 1) * 128], h[:, k, :]) for k in range(8)])
            kb.act(sg[f % 2], bg[:, 0:tc], AF.Silu)
            kb.tt("dve", hmid[:, f, :], sg[f % 2], bu[:, 0:tc], ALU.mult)

    def down(i):
        x, j = xs[i % NX], jof(i)
        seq, c = items[i]
        for m in range(8):
            bank = kb.bank()
            kb.mm(bank[:, 0:tc], [(wd[:, f, m * 128:(m + 1) * 128], hmid[:, f, :]) for f in range(22)])
            kb.stt(x[:, m, :], bank[:, 0:tc], modcol(g, l, 5, m, j), x[:, m, :], ALU.mult, ALU.add)
        if last:
            norm_a(i)
            norm_b(i, final=True)
            dst = g.d_out
        else:
            dst = g.X1[seq]
        kb.dma("sp", Buf(dst.ap[:, c * tc:(c + 1) * tc].rearrange("(k p) t -> p k t", p=128),
                         chunk_res(dst, c * tc, (c + 1) * tc, RC)), x)
    load_x(0)
    load_y(0)
    if n_it > 1:
        load_x(1)
    wout(0)
    if n_it > 1:
        load_y(1)
    norm_a(0)
    norm_b(0)
    for i in range(n_it):
        if i + 2 < n_it:
            load_x(i + 2)
        if i + 1 < n_it:
            wout(i + 1)
            if i + 2 < n_it:
                load_y(i + 2)
            norm_a(i + 1)
        gateup(i)
        if i + 1 < n_it:
            norm_b(i + 1)
        down(i)
    kb.release(mk)
```
